# Optimizing a Trainium2 kernel written in Bass

```python
import math
import jax, jax.numpy as jnp
from jax import lax
import numpy as np

D_MODEL = 1024
BATCH = 8
SEQ = 4096
DEPTH = 4
DEC_BATCH = 8
DEC_SEQ = 8192
PAST_LEN = 128

GDN_HEADS = 4
GDN_DK = 128
GDN_DV = 128
GDN_CONV = 5
GDN_CHUNK = 64
GDN_Q_W = GDN_HEADS * GDN_DK
GDN_V_W = GDN_HEADS * GDN_DV
GDN_QKV_W = 2 * GDN_Q_W + GDN_V_W
HG_HEADS = 4
HG_DK = 128
HG_DV = 128
HG_CHUNK = 16
HG_K_W = HG_HEADS * HG_DK
HG_V_W = HG_HEADS * HG_DV
D_FF = -(-8 * D_MODEL // (3 * 256)) * 256
NORM_EPS = 1e-6

IN_SIZES = (GDN_QKV_W, GDN_V_W, 2 * GDN_HEADS, 2 * GDN_HEADS, HG_K_W, 2 * HG_K_W, HG_V_W, HG_V_W, 2 * D_MODEL)
IN_WIDTH = sum(IN_SIZES)
IN_SPLITS = [sum(IN_SIZES[:i + 1]) for i in range(len(IN_SIZES) - 1)]

kernel_name = 'hybrid_gdn_hgrn2_bidir_encoder'


def rms_norm(x, w):
    xf = x.astype(jnp.float32)
    y = xf * lax.rsqrt(jnp.mean(xf * xf, axis=-1, keepdims=True) + NORM_EPS)
    return (y * w.astype(jnp.float32)).astype(x.dtype)


def l2norm(x):
    return x * lax.rsqrt(jnp.sum(x * x, axis=-1, keepdims=True) + NORM_EPS)


def centred_dwconv(x, w):
    k = w.shape[0]
    return lax.conv_general_dilated(x, w[:, None, :].astype(x.dtype), window_strides=(1,),
                                    padding=[(k // 2, k // 2)],
                                    dimension_numbers=('NWC', 'WIO', 'NWC'),
                                    feature_group_count=x.shape[-1])


def to_heads(t, n_heads):
    b, l, _ = t.shape
    return t.reshape(b, l, n_heads, -1).transpose(0, 2, 1, 3)


def flip_seq(t):
    return jnp.flip(t, axis=2)


def gated_delta_chunked(q, k, v, g, beta):
    b_, h_, l_, dk = q.shape
    dv = v.shape[-1]
    c = GDN_CHUNK
    n = l_ // c
    q = q.reshape(b_, h_, n, c, dk)
    k = k.reshape(b_, h_, n, c, dk)
    v = v.reshape(b_, h_, n, c, dv)
    g = g.reshape(b_, h_, n, c)
    beta = beta.reshape(b_, h_, n, c)
    cum = jnp.cumsum(g, axis=-1)
    causal = jnp.tril(jnp.ones((c, c), bool))
    strict = jnp.tril(jnp.ones((c, c), bool), -1)
    decay = jnp.exp(jnp.where(causal, cum[..., :, None] - cum[..., None, :], -jnp.inf))
    kb = k * beta[..., None]
    a_mat = jnp.where(strict, jnp.einsum('bhncd,bhnsd->bhncs', kb, k) * decay, 0.0)
    rhs = jnp.concatenate([v * beta[..., None], kb * jnp.exp(cum)[..., None]], axis=-1)
    sol = lax.linalg.triangular_solve(a_mat, rhs, left_side=True, lower=True, unit_diagonal=True)
    u, w = sol[..., :dv], sol[..., dv:]
    qk = jnp.einsum('bhncd,bhnsd->bhncs', q, k) * decay
    q_dec = q * jnp.exp(cum)[..., None]
    k_dec = k * jnp.exp(cum[..., -1:] - cum)[..., None]
    last = jnp.exp(cum[..., -1])

    def step(s, inp):
        qk_n, qd_n, kd_n, u_n, w_n, last_n = inp
        v_new = u_n - jnp.einsum('bhcd,bhde->bhce', w_n, s)
        o = jnp.einsum('bhcd,bhde->bhce', qd_n, s) + jnp.einsum('bhcs,bhse->bhce', qk_n, v_new)
        s = s * last_n[..., None, None] + jnp.einsum('bhcd,bhce->bhde', kd_n, v_new)
        return s, o

    xs = (jnp.moveaxis(qk, 2, 0), jnp.moveaxis(q_dec, 2, 0), jnp.moveaxis(k_dec, 2, 0),
          jnp.moveaxis(u, 2, 0), jnp.moveaxis(w, 2, 0), jnp.moveaxis(last, 2, 0))
    s0 = jnp.zeros((b_, h_, dk, dv), q.dtype)
    _, o = lax.scan(step, s0, xs)
    return jnp.moveaxis(o, 0, 2).reshape(b_, h_, l_, dv)


def gla_chunked(q, k, v, logf):
    b_, h_, l_, dk = q.shape
    dv = v.shape[-1]
    c = HG_CHUNK
    n = l_ // c
    q = q.reshape(b_, h_, n, c, dk)
    k = k.reshape(b_, h_, n, c, dk)
    logf = logf.reshape(b_, h_, n, c, dk)
    v = v.reshape(b_, h_, n, c, dv)
    cum = jnp.cumsum(logf, axis=3)
    q_dec = q * jnp.exp(cum)
    k_inv = k * jnp.exp(-cum)
    k_dec = k * jnp.exp(cum[..., -1:, :] - cum)
    last = jnp.exp(cum[..., -1, :])
    causal = jnp.tril(jnp.ones((c, c), bool))
    attn = jnp.where(causal, jnp.einsum('bhncd,bhnsd->bhncs', q_dec, k_inv), 0.0)
    o_intra = jnp.einsum('bhncs,bhnse->bhnce', attn, v)

    def step(s, inp):
        qd_n, kd_n, v_n, last_n = inp
        o = jnp.einsum('bhcd,bhde->bhce', qd_n, s)
        s = s * last_n[..., None] + jnp.einsum('bhcd,bhce->bhde', kd_n, v_n)
        return s, o

    xs = (jnp.moveaxis(q_dec, 2, 0), jnp.moveaxis(k_dec, 2, 0), jnp.moveaxis(v, 2, 0), jnp.moveaxis(last, 2, 0))
    s0 = jnp.zeros((b_, h_, dk, dv), q.dtype)
    _, o_inter = lax.scan(step, s0, xs)
    return (o_intra + jnp.moveaxis(o_inter, 0, 2)).reshape(b_, h_, l_, dv)


def hybrid_layer(x, w_in, conv_w, a_log, dt_bias, gdn_norm_w, lb, hg_norm_w, w_br_gdn, w_br_hg, w_out,
                 n_pre_mix, n_post_mix, n_pre_ffn, n_post_ffn, w_gate, w_up, w_down):
    bsz, seqlen, _ = x.shape
    f32 = jnp.float32
    h = rms_norm(x, n_pre_mix)
    proj = h @ w_in
    qkv_g, z_g, a_g, b_g, q_h, f_h, i_h, g_h, gate_raw = jnp.split(proj, IN_SPLITS, axis=-1)

    qkv = jax.nn.silu(centred_dwconv(qkv_g, conv_w).astype(f32))
    q, k, v = jnp.split(qkv, [GDN_Q_W, 2 * GDN_Q_W], axis=-1)
    q = l2norm(to_heads(q, GDN_HEADS)) * (GDN_DK ** -0.5)
    k = l2norm(to_heads(k, GDN_HEADS))
    v = to_heads(v, GDN_HEADS)
    a = a_g.astype(f32).reshape(bsz, seqlen, 2, GDN_HEADS)
    g = -jnp.exp(a_log.astype(f32)) * jax.nn.softplus(a + dt_bias.astype(f32))
    g = jnp.transpose(g, (2, 0, 3, 1))
    beta = jnp.transpose(jax.nn.sigmoid(b_g.astype(f32).reshape(bsz, seqlen, 2, GDN_HEADS)), (2, 0, 3, 1))
    o_a = gated_delta_chunked(q, k, v, g[0], beta[0]) + flip_seq(
        gated_delta_chunked(flip_seq(q), flip_seq(k), flip_seq(v), flip_seq(g[1]), flip_seq(beta[1])))
    o_a = o_a.transpose(0, 2, 1, 3)
    o_a = rms_norm(o_a, gdn_norm_w) * jax.nn.silu(z_g.astype(f32).reshape(bsz, seqlen, GDN_HEADS, GDN_DV))
    y_a = o_a.reshape(bsz, seqlen, GDN_V_W).astype(x.dtype) @ w_br_gdn

    qh = to_heads(jax.nn.silu(q_h.astype(f32)), HG_HEADS) * (HG_DK ** -0.5)
    lbf = lb.astype(f32)
    f = lbf + (1.0 - lbf) * jax.nn.sigmoid(f_h.astype(f32).reshape(bsz, seqlen, 2, HG_K_W))
    logf = jnp.transpose(jnp.log(f).reshape(bsz, seqlen, 2, HG_HEADS, HG_DK), (2, 0, 3, 1, 4))
    kh = jnp.transpose((1.0 - f).reshape(bsz, seqlen, 2, HG_HEADS, HG_DK), (2, 0, 3, 1, 4))
    ih = to_heads(i_h.astype(f32), HG_HEADS)
    o_b = gla_chunked(qh, kh[0], ih, logf[0]) + flip_seq(
        gla_chunked(flip_seq(qh), flip_seq(kh[1]), flip_seq(ih), flip_seq(logf[1])))
    o_b = o_b.transpose(0, 2, 1, 3)
    o_b = rms_norm(o_b, hg_norm_w) * jax.nn.silu(g_h.astype(f32).reshape(bsz, seqlen, HG_HEADS, HG_DV))
    y_b = o_b.reshape(bsz, seqlen, HG_V_W).astype(x.dtype) @ w_br_hg

    gates = jax.nn.sigmoid(gate_raw.astype(f32)).reshape(bsz, seqlen, 2, D_MODEL)
    merged = (gates[:, :, 0] * y_a.astype(f32) + gates[:, :, 1] * y_b.astype(f32)).astype(x.dtype)
    x = x + rms_norm(merged @ w_out, n_post_mix)

    h2 = rms_norm(x, n_pre_ffn)
    ff = (jax.nn.silu(h2 @ w_gate) * (h2 @ w_up)) @ w_down
    return x + rms_norm(ff, n_post_ffn)


def setup_inputs(seed: int = 0) -> dict:
    key = jax.random.key(seed)
    ks = jax.random.split(key, 24)
    f32 = jnp.float32

    def nrm(k, shape, fan_in):
        return jax.random.normal(k, shape, f32) * (fan_in ** -0.5)

    def gain(k, shape):
        return 1.0 + 0.05 * jax.random.normal(k, shape, f32)

    dt = jnp.exp(jax.random.uniform(ks[5], (DEPTH, 2, GDN_HEADS), f32, math.log(1e-3), math.log(1e-1)))
    return {
        'x_prompt': jax.random.normal(ks[0], (BATCH, SEQ, D_MODEL), f32),
        'x_sample': jax.random.normal(ks[1], (DEC_BATCH, DEC_SEQ, D_MODEL), f32),
        'w_in': nrm(ks[2], (DEPTH, D_MODEL, IN_WIDTH), D_MODEL),
        'conv_w': nrm(ks[3], (DEPTH, GDN_CONV, GDN_QKV_W), GDN_CONV),
        'gdn_a_log': jnp.log(jax.random.uniform(ks[4], (DEPTH, 2, GDN_HEADS), f32, 1.0, 16.0)),
        'gdn_dt_bias': dt + jnp.log(-jnp.expm1(-dt)),
        'gdn_norm_w': gain(ks[6], (DEPTH, GDN_DV)),
        'hgrn_lb_logits': 0.5 * jax.random.normal(ks[7], (DEPTH, 2, HG_K_W), f32),
        'hgrn_norm_w': gain(ks[8], (DEPTH, HG_DV)),
        'w_branch_gdn': nrm(ks[9], (DEPTH, GDN_V_W, D_MODEL), GDN_V_W),
        'w_branch_hgrn': nrm(ks[10], (DEPTH, HG_V_W, D_MODEL), HG_V_W),
        'w_out': nrm(ks[11], (DEPTH, D_MODEL, D_MODEL), D_MODEL),
        'norm_pre_mix': gain(ks[12], (DEPTH, D_MODEL)),
        'norm_post_mix': gain(ks[13], (DEPTH, D_MODEL)),
        'norm_pre_ffn': gain(ks[14], (DEPTH, D_MODEL)),
        'norm_post_ffn': gain(ks[15], (DEPTH, D_MODEL)),
        'w_ffn_gate': nrm(ks[16], (DEPTH, D_MODEL, D_FF), D_MODEL),
        'w_ffn_up': nrm(ks[17], (DEPTH, D_MODEL, D_FF), D_MODEL),
        'w_ffn_down': nrm(ks[18], (DEPTH, D_FF, D_MODEL), D_FF),
    }


def reference(x_prompt, x_sample, w_in, conv_w, gdn_a_log, gdn_dt_bias, gdn_norm_w, hgrn_lb_logits,
              hgrn_norm_w, w_branch_gdn, w_branch_hgrn, w_out, norm_pre_mix, norm_post_mix,
              norm_pre_ffn, norm_post_ffn, w_ffn_gate, w_ffn_up, w_ffn_down):
    lb_sm = jax.nn.softmax(hgrn_lb_logits.astype(jnp.float32), axis=0)
    lb_all = jnp.cumsum(lb_sm, axis=0) - lb_sm[0:1]
    y_prompt = x_prompt
    y_sample = x_sample
    for l in range(DEPTH):
        layer_params = (w_in[l], conv_w[l], gdn_a_log[l], gdn_dt_bias[l], gdn_norm_w[l], lb_all[l],
                        hgrn_norm_w[l], w_branch_gdn[l], w_branch_hgrn[l], w_out[l], norm_pre_mix[l],
                        norm_post_mix[l], norm_pre_ffn[l], norm_post_ffn[l], w_ffn_gate[l], w_ffn_up[l],
                        w_ffn_down[l])
        y_prompt = hybrid_layer(y_prompt, *layer_params)
        y_sample = hybrid_layer(y_sample, *layer_params)
    return (y_prompt, y_sample)
```

```python
import numpy as np
from contextlib import ExitStack
import concourse.bass as bass
import concourse.mybir as mybir
from concourse.ap import AP
from concourse.bass_utils import run_bass_kernel_spmd

F32 = mybir.dt.float32
BF16 = mybir.dt.bfloat16
AF = mybir.ActivationFunctionType
ALU = mybir.AluOpType
AX = mybir.AxisListType

D = 1024
NH = 4
DK = 128
DFF = 2816
NFF = DFF // 128
INW = 6672
EPS = 1e-6
O_Q, O_K, O_V, O_Z, O_A, O_B, O_QH, O_FH, O_IH, O_GH, O_GATE = 0, 512, 1024, 1536, 2048, 2056, 2064, 2576, 3600, 4112, 4624
TA = 256
NCORES = 8


class Buf:
    __slots__ = ("name", "last_w", "rd_eng", "rd_dma")

    def __init__(self, name=""):
        self.name = name
        self.last_w = None
        self.rd_eng = {}
        self.rd_dma = []


class Op:
    __slots__ = ("eng", "fn", "idx", "marked", "dma", "deps", "sem", "val")

    def __init__(self, eng, fn, dma):
        self.eng = eng
        self.fn = fn
        self.dma = dma
        self.marked = False
        self.deps = ()
        self.sem = None
        self.val = 0


class Sched:
    ENGS = ("pe", "act", "dve", "pool", "sp")
    GEN = 30000
    NDMA = 40

    def __init__(self, nc):
        self.nc = nc
        self.streams = {e: [] for e in self.ENGS}
        self.seen = {e: {} for e in self.ENGS}
        self.seen_dma = {e: set() for e in self.ENGS}
        self.ndma = 0
        self.dma_ops = []
        self.pending = {e: None for e in self.ENGS}

    def barrier(self):
        deps = []
        for e in ("pe", "act", "dve", "pool"):
            if self.streams[e]:
                deps.append(self.streams[e][-1])
        deps.extend(self.dma_ops[-self.NDMA:])
        for e in self.ENGS:
            self.pending[e] = list(deps) + (self.pending[e] or [])

    def add(self, eng, fn, reads=(), writes=(), dma=False):
        op = Op(eng, fn, dma)
        st = self.streams[eng]
        op.idx = len(st)
        st.append(op)
        deps = []
        if self.pending[eng] is not None:
            deps.extend(self.pending[eng])
            self.pending[eng] = None
        for b in reads:
            if b.last_w is not None:
                deps.append(b.last_w)
        for b in writes:
            if b.last_w is not None:
                deps.append(b.last_w)
            deps.extend(b.rd_eng.values())
            deps.extend(b.rd_dma)
        need = {}
        dma_need = []
        seen = self.seen[eng]
        sdma = self.seen_dma[eng]
        for d in deps:
            if d is op:
                continue
            if d.dma:
                if d not in sdma:
                    sdma.add(d)
                    dma_need.append(d)
            else:
                if d.eng == "pe" and eng == "pe" and not dma:
                    continue
                if seen.get(d.eng, -1) >= d.idx:
                    continue
                if need.get(d.eng, -1) < d.idx:
                    need[d.eng] = d.idx
        wl = []
        for e2, k in need.items():
            seen[e2] = k
            dop = self.streams[e2][k]
            dop.marked = True
            wl.append(dop)
        op.deps = wl + dma_need
        if dma:
            i = self.ndma
            self.ndma += 1
            self.dma_ops.append(op)
            if i >= self.NDMA:
                prev = self.dma_ops[i - self.NDMA]
                if prev not in sdma:
                    sdma.add(prev)
                    op.deps.append(prev)
        for b in reads:
            if dma:
                b.rd_dma.append(op)
            else:
                b.rd_eng[eng] = op
        for b in writes:
            b.last_w = op
            b.rd_eng = {}
            b.rd_dma = []
        return op

    def emit(self, es):
        nc = self.nc
        for e in self.ENGS:
            nmark = sum(1 for o in self.streams[e] if o.marked and not o.dma)
            ngen = max(1, -(-nmark // self.GEN))
            sems = [es.enter_context(nc.semaphore(f"s_{e}_{g}")) for g in range(ngen)]
            c = 0
            for o in self.streams[e]:
                if o.marked and not o.dma:
                    o.sem = sems[c // self.GEN]
                    o.val = c % self.GEN + 1
                    c += 1
        nd = min(self.NDMA, max(1, self.ndma))
        dsems = [es.enter_context(nc.semaphore(f"s_dma_{i}")) for i in range(nd)]
        final_dma = {}
        for i, o in enumerate(self.dma_ops):
            o.sem = dsems[i % self.NDMA]
            o.val = 16 * (i // self.NDMA + 1)
            final_dma[i % self.NDMA] = o.val

        def run(e, h, last=False):
            for o in self.streams[e]:
                for d in o.deps:
                    h.wait_ge(d.sem, d.val)
                ins = o.fn(h)
                if o.dma:
                    ins.then_inc(o.sem, 16)
                elif o.marked:
                    ins.then_inc(o.sem, 1)
            if last:
                for i, v in final_dma.items():
                    h.wait_ge(dsems[i], v)

        block = es.enter_context(nc.Block())

        @block.tensor
        def _(h):
            run("pe", h)

        @block.scalar
        def _(h):
            run("act", h)

        @block.vector
        def _(h):
            run("dve", h)

        @block.gpsimd
        def _(h):
            run("pool", h)

        @block.sync
        def _(h):
            run("sp", h, last=True)


class Ring:
    def __init__(self, tiles):
        self.tiles = tiles
        self.i = 0

    def next(self):
        t = self.tiles[self.i % len(self.tiles)]
        self.i += 1
        return t


class DT:
    def __init__(self, ap):
        self.ap = ap
        self.bufs = {}

    def b(self, key):
        r = self.bufs.get(key)
        if r is None:
            r = self.bufs[key] = Buf()
        return r


FA_N = 15360
HA_N = 22528
WAR_N = 22528


def build(seq_lens, depth, dbg=(), phases=None):
    nc = bass.Bass("TRN2", target_bir_lowering=False)
    nseq = len(seq_lens)
    XH = TA + 4
    NCH = TA // 128
    NC64 = TA // 64

    def on(p):
        return phases is None or p in phases

    def din(name, shape, dt=F32):
        return nc.dram_tensor(name, list(shape), dt, kind="ExternalInput").ap()

    def dscr(name, shape, dt=F32):
        kind = "ExternalOutput" if name in dbg else "Internal"
        return DT(nc.dram_tensor(name, list(shape), dt, kind=kind).ap())

    x_in = [din(f"x{s}", [seq_lens[s], D]) for s in range(nseq)]
    y_out = [DT(nc.dram_tensor(f"y{s}", [seq_lens[s], D], F32, kind="ExternalOutput").ap()) for s in range(nseq)]
    w_in = din("w_in", [depth, D, INW])
    conv_w = din("conv_w", [depth, 5, 1536])
    a_log = din("gdn_a_log", [depth, 8])
    dt_bias = din("gdn_dt_bias", [depth, 8])
    gdn_nw = din("gdn_norm_w", [depth, 128])
    lb_logits = din("hgrn_lb_logits", [depth, 2, 512])
    hg_nw = din("hgrn_norm_w", [depth, 128])
    w_bg = din("w_branch_gdn", [depth, 512, D])
    w_bh = din("w_branch_hgrn", [depth, 512, D])
    w_o = din("w_out", [depth, D, D])
    n_pre_mix = din("norm_pre_mix", [depth, D])
    n_post_mix = din("norm_post_mix", [depth, D])
    n_pre_ffn = din("norm_pre_ffn", [depth, D])
    n_post_ffn = din("norm_post_ffn", [depth, D])
    w_g = din("w_ffn_gate", [depth, D, DFF])
    w_u = din("w_ffn_up", [depth, D, DFF])
    w_d = din("w_ffn_down", [depth, DFF, D])
    c_ident = din("c_ident", [128, 128])
    c_mask = din("c_mask", [128, 8, 128])
    c_sel = din("c_sel", [8, 2])

    SL = list(enumerate(seq_lens))
    XT = [[dscr(f"XT{p}_{s}", [8, 128, L]) for p in range(2)] for s, L in SL]
    XM = [dscr(f"XM_{s}", [8, 128, L]) for s, L in SL]
    HN = [dscr(f"HN_{s}", [L // TA, 128, 8 * TA], BF16) for s, L in SL]
    H2 = [dscr(f"H2_{s}", [L // TA, 128, 8 * TA], BF16) for s, L in SL]
    GQK = [dscr(f"GQK_{s}", [L // TA, 128, 8 * TA], BF16) for s, L in SL]
    GKV = [dscr(f"GKV_{s}", [L, 1024], BF16) for s, L in SL]
    GS = [dscr(f"GS_{s}", [3, 8, L]) for s, L in SL]
    GT = [dscr(f"GT_{s}", [L, 24]) for s, L in SL]
    ZT = [dscr(f"ZT_{s}", [L // TA, 128, 4 * TA], BF16) for s, L in SL]
    HQd = [dscr(f"HQ_{s}", [2, L // TA, 128, 4 * TA], BF16) for s, L in SL]
    HKd = [dscr(f"HK_{s}", [2, L // TA, 128, 4 * TA], BF16) for s, L in SL]
    HAB = [dscr(f"HAB_{s}", [L // TA, 128, 64]) for s, L in SL]
    HT = [dscr(f"HT_{s}", [L, 1536], BF16) for s, L in SL]
    GHT = [dscr(f"GHT_{s}", [L // TA, 128, 4 * TA], BF16) for s, L in SL]
    MG = [dscr(f"MG_{s}", [L // TA, 128, 16 * TA], BF16) for s, L in SL]
    OA = [dscr(f"OA_{s}", [2, L // TA, 128, 4 * TA], BF16) for s, L in SL]
    OB = [dscr(f"OB_{s}", [2, L // TA, 128, 4 * TA], BF16) for s, L in SL]
    ACTS = [dscr(f"ACT_{s}", [L // TA, 128, 22 * TA], BF16) for s, L in SL]

    with ExitStack() as es:
        S = Sched(nc)
        cnt = [0]

        def sb(shape, dt=F32, name=None):
            cnt[0] += 1
            nm = name or f"t{cnt[0]}"
            t = es.enter_context(nc.sbuf_tensor(nm, list(shape), dt))
            return t, Buf(nm)

        def pst(shape, dt=F32, name=None):
            t = es.enter_context(nc.psum_tensor(name, list(shape), dt))
            return t, Buf(name)

        farena = es.enter_context(nc.sbuf_tensor("farena", [128, FA_N], F32))
        harena = es.enter_context(nc.sbuf_tensor("harena", [128, HA_N], BF16))
        off = {"f": 0, "h": 0}

        def carve(shape, dt=F32):
            k = "f" if dt == F32 else "h"
            ar = farena if dt == F32 else harena
            n = 1
            for d_ in shape[1:]:
                n *= d_
            o = off[k]
            off[k] = o + n
            assert off[k] <= (FA_N if dt == F32 else HA_N), (k, off[k])
            v = ar[0:shape[0], o:o + n]
            if len(shape) == 3:
                v = v.rearrange("p (a b) -> p a b", a=shape[1])
            elif len(shape) == 4:
                v = v.rearrange("p (a b c) -> p a b c", a=shape[1], b=shape[2])
            return v, Buf()

        def ring(n, shape, dt=F32):
            return Ring([carve(shape, dt) for _ in range(n)])

        def end_phase():
            S.barrier()
            off["f"] = 0
            off["h"] = 0

        def dma(out, in_, reads, writes, nonc=False):
            if nonc:
                S.add("sp", lambda h: h.dma_start(out=out, in_=in_, allow_slow_non_contiguous=True), reads, writes, dma=True)
            else:
                S.add("sp", lambda h: h.dma_start(out=out, in_=in_), reads, writes, dma=True)

        def mm(out, wbuf, pairs, reads):
            n = len(pairs)

            def fn(h):
                ins = None
                for i, (l, r) in enumerate(pairs):
                    ins = h.matmul(out, lhsT=l, rhs=r, start=(i == 0), stop=(i == n - 1))
                return ins
            S.add("pe", fn, reads, [wbuf])

        def mmh(pt, pb, items, reads):
            def fn(h):
                ins = None
                for (o, prs) in items:
                    n = len(prs)
                    for i, (l, r) in enumerate(prs):
                        ins = h.matmul(o, lhsT=l, rhs=r, start=(i == 0), stop=(i == n - 1))
                return ins
            S.add("pe", fn, reads, [pb])

        def act(out, in_, func, reads, writes, **kw):
            S.add("act", lambda h: h.activation(out=out, in_=in_, func=func, **kw), reads, writes)

        def ts(eng, out, in0, s1, s2, op0, op1, reads, writes):
            if s2 is None:
                S.add(eng, lambda h: h.tensor_scalar(out=out, in0=in0, scalar1=s1, scalar2=None, op0=op0), reads, writes)
            else:
                S.add(eng, lambda h: h.tensor_scalar(out=out, in0=in0, scalar1=s1, scalar2=s2, op0=op0, op1=op1), reads, writes)

        def stt(out, in0, sc, in1, op0, op1, reads, writes):
            S.add("dve", lambda h: h.scalar_tensor_tensor(out=out, in0=in0, scalar=sc, in1=in1, op0=op0, op1=op1), reads, writes)

        def tt(eng, out, in0, in1, op, reads, writes):
            S.add(eng, lambda h: h.tensor_tensor(out=out, in0=in0, in1=in1, op=op), reads, writes)

        def cp(eng, out, in_, reads, writes):
            if eng == "act":
                S.add("act", lambda h: h.activation(out=out, in_=in_, func=AF.Copy), reads, writes)
            else:
                S.add(eng, lambda h: h.tensor_copy(out=out, in_=in_), reads, writes)

        def mset(eng, ap, val, writes):
            S.add(eng, lambda h: h.memset(ap, val), [], writes)

        def bc(ap2, n):
            return ap2.unsqueeze(2).to_broadcast([ap2.shape[0], ap2.shape[1], n])

        def bch(ap2, n):
            return ap2.unsqueeze(1).to_broadcast([ap2.shape[0], n, ap2.shape[1]])

        ident_f, ident_fb = sb([128, 128], F32, "ident_f")
        ident_h, ident_hb = sb([128, 128], BF16, "ident_h")
        ones_h, ones_hb = sb([128, 128], BF16, "ones_h")
        mask_f, mask_fb = sb([128, 8, 128], F32, "mask_f")
        sel, selb = sb([8, 2], F32, "sel")
        eps_t, epsb = sb([128, 2], F32, "eps_t")
        dma(ident_f[:], c_ident[:, :], [], [ident_fb])
        dma(mask_f[:], c_mask[:, :, :], [], [mask_fb])
        dma(sel[:], c_sel[:, :], [], [selb])
        cp("dve", ident_h[:], ident_f[:], [ident_fb], [ident_hb])
        mset("dve", ones_h[:], 1.0, [ones_hb])
        mset("dve", eps_t[:, 0:1], float(EPS), [epsb])
        mset("dve", eps_t[:, 1:2], 1.0, [epsb])
        m128, m128b = sb([8, TA], F32, "m128")
        m64, m64b = sb([128, TA], F32, "m64")
        mset("pool", m128[:], 1.0, [m128b])
        mset("pool", m128[:].rearrange("p (c k) -> p c k", k=128)[:, :, 0:1], 0.0, [m128b])
        mset("pool", m64[:], 1.0, [m64b])
        mset("pool", m64[:].rearrange("p (c k) -> p c k", k=64)[:, :, 0:1], 0.0, [m64b])

        psf = Ring([pst([128, 512], F32, f"psf{i}") for i in range(6)])
        psh = Ring([pst([128, 1024], BF16, f"psh{i}") for i in range(2)])

        warena = [sb([128, WAR_N], BF16, f"warena{i}") for i in range(2)]
        wcnt = [0]
        wstage = Ring([sb([128, 520], F32, f"wstage{i}") for i in range(2)])
        cast_rr = [0]

        def next_arena():
            a = warena[wcnt[0] % 2]
            wcnt[0] += 1
            return a

        def load_w(dst, dstb, src, ncols):
            c0 = 0
            while c0 < ncols:
                w = min(520, ncols - c0)
                (st, stb) = wstage.next()
                dma(st[:, 0:w], src[:, c0:c0 + w], [], [stb])
                eng = ("dve", "pool")[cast_rr[0] % 2]
                cast_rr[0] += 1
                cp(eng, dst[:, c0:c0 + w], st[:, 0:w], [stb], [dstb])
                c0 += w

        def vec_cols(src_1d, n, name):
            t, tb = sb([128, n], F32, name)
            dma(t[:], src_1d.rearrange("(c p) -> p c", p=128), [], [tb], nonc=True)
            return t, tb

        def norm_tile(xsrc, L, blk, gain, gainb, halo, xT_r, sq_r, hT_r, rstd_r, tmp_r):
            t0 = blk * TA
            nb = L // TA
            (xt, xb) = xT_r.next()
            lo, hi = (2, 2) if halo else (0, 0)
            W = TA + lo + hi
            a0 = t0 - lo
            a1 = t0 + TA + hi
            o0 = 0
            if a0 < 0:
                mset("pool", xt[:, :, 0:lo], 0.0, [xb])
                o0 = lo
                a0 = 0
            if a1 > L:
                mset("pool", xt[:, :, W - hi:W], 0.0, [xb])
                a1 = L
            rb = [xsrc.b(k) for k in range(max(0, blk - 1), min(nb, blk + 2))] if halo else [xsrc.b(blk)]
            dma(xt[:, :, o0:o0 + (a1 - a0)], xsrc.ap[:, :, a0:a1].rearrange("c p t -> p c t"), rb, [xb])
            (sq, sqb) = sq_r.next()
            act(sq[:, :, 0:W], xt[:, :, 0:W], AF.Square, [xb], [sqb])
            (pt, pb) = psf.next()
            mm(pt[:, 0:W], pb, [(ones_h[:], sq[:, c, 0:W]) for c in range(8)], [ones_hb, sqb])
            (rs, rsb) = rstd_r.next()
            (tmp, tmpb) = tmp_r.next()
            act(tmp[:, 0:W], pt[:, 0:W], AF.Sqrt, [pb, epsb], [tmpb], bias=eps_t[:, 0:1], scale=1.0 / D)
            S.add("dve", lambda h: h.reciprocal(out=rs[:, 0:W], in_=tmp[:, 0:W]), [tmpb], [rsb])
            (ht, hb) = hT_r.next()
            for c in range(8):
                stt(ht[:, c, 0:W], xt[:, c, 0:W], gain[:, c:c + 1], rs[:, 0:W], ALU.mult, ALU.mult, [xb, gainb, rsb], [hb])
            return ht, hb, xt, xb

        def post_norm_add(res_r, xt, xb, val, valb, sq8, sq8b, gain, gainb, rstd_r, tmp_r, f_r):
            (pt, pb) = psf.next()
            mm(pt[:, 0:TA], pb, [(ones_h[:], sq8[:, c, :]) for c in range(8)], [ones_hb, sq8b])
            (rs, rsb) = rstd_r.next()
            (tmp, tmpb) = tmp_r.next()
            act(tmp[:, 0:TA], pt[:, 0:TA], AF.Sqrt, [pb, epsb], [tmpb], bias=eps_t[:, 0:1], scale=1.0 / D)
            S.add("dve", lambda h: h.reciprocal(out=rs[:, 0:TA], in_=tmp[:, 0:TA]), [tmpb], [rsb])
            (res, resb) = res_r.next()
            for c in range(8):
                (t1, t1b) = f_r.next()
                stt(t1[:, 0:TA], val[:, c, :], gain[:, c:c + 1], rs[:, 0:TA], ALU.mult, ALU.mult, [valb, gainb, rsb], [t1b])
                tt("pool", res[:, c, :], xt[:, c, 0:TA], t1[:, 0:TA], ALU.add, [xb, t1b], [resb])
            return res, resb

        lg, lgb = sb([128, depth, 8], F32, "lb_lg")
        lbt, lbb = sb([128, depth, 8], F32, "lb")
        omlt, omlb = sb([128, depth, 8], F32, "oml")
        for l in range(depth):
            for d_ in range(2):
                dma(lg[:, l, d_ * 4:(d_ + 1) * 4], lb_logits[l, d_].rearrange("(h p) -> p h", p=128), [], [lgb], nonc=True)
        mx, mxb = sb([128, 8], F32, "lb_mx")
        cp("dve", mx[:], lg[:, 0, :], [lgb], [mxb])
        for l in range(1, depth):
            tt("dve", mx[:], mx[:], lg[:, l, :], ALU.max, [mxb, lgb], [mxb])
        for l in range(depth):
            tt("dve", lg[:, l, :], lg[:, l, :], mx[:], ALU.subtract, [lgb, mxb], [lgb])
        act(lg[:], lg[:], AF.Exp, [lgb], [lgb])
        cp("dve", mx[:], lg[:, 0, :], [lgb], [mxb])
        for l in range(1, depth):
            tt("dve", mx[:], mx[:], lg[:, l, :], ALU.add, [mxb, lgb], [mxb])
        S.add("dve", lambda h: h.reciprocal(out=mx[:], in_=mx[:]), [mxb], [mxb])
        mset("dve", lbt[:, 0, :], 0.0, [lbb])
        for l in range(1, depth):
            tt("dve", lg[:, l, :], lg[:, l, :], mx[:], ALU.mult, [lgb, mxb], [lgb])
            tt("dve", lbt[:, l, :], lbt[:, l - 1, :], lg[:, l, :], ALU.add, [lbb, lgb], [lbb])
        ts("dve", omlt[:], lbt[:], -1.0, 1.0, ALU.mult, ALU.add, [lbb], [omlb])

        if on("P0"):
            xin_r = ring(2, [128, D], F32)
            xo_r = ring(2, [128, 8, 128], F32)
            for s, L in SL:
                for j in range(L // 128):
                    (xt, xb) = xin_r.next()
                    dma(xt, x_in[s][j * 128:(j + 1) * 128, :], [], [xb])
                    (xo, xob) = xo_r.next()
                    for half in range(2):
                        (pt, pb) = psf.next()
                        mmh(pt, pb, [(pt[:, c * 128:(c + 1) * 128], [(xt[:, (half * 4 + c) * 128:(half * 4 + c + 1) * 128], ident_f[:])]) for c in range(4)], [xb, ident_fb])
                        cp("act" if half == 0 else "dve", xo[:, half * 4:(half + 1) * 4, :], pt[:].rearrange("p (c t) -> p c t", c=4), [pb], [xob])
                    dma(XT[s][0].ap[:, :, j * 128:(j + 1) * 128].rearrange("c p t -> p c t"), xo, [xob], [XT[s][0].b(j * 128 // TA)])
            end_phase()

        for l in range(depth):
            par = l % 2
            g_pm, g_pmb = vec_cols(n_pre_mix[l], 8, f"g_pm{l}")
            g_po, g_pob = vec_cols(n_post_mix[l], 8, f"g_po{l}")
            g_pf, g_pfb = vec_cols(n_pre_ffn[l], 8, f"g_pf{l}")
            g_qf, g_qfb = vec_cols(n_post_ffn[l], 8, f"g_qf{l}")
            nwa, nwab = sb([128, 1], F32, f"nwa{l}")
            nwh, nwhb = sb([128, 1], F32, f"nwh{l}")
            dma(nwa[:], gdn_nw[l].rearrange("(p o) -> p o", o=1), [], [nwab], nonc=True)
            dma(nwh[:], hg_nw[l].rearrange("(p o) -> p o", o=1), [], [nwhb], nonc=True)
            cw, cwb = sb([128, 12, 5], F32, f"cw{l}")
            for j in range(5):
                dma(cw[:, :, j], conv_w[l, j].rearrange("(c p) -> p c", p=128), [], [cwb], nonc=True)
            alog, alogb = sb([8, 1], F32, f"alog{l}")
            dtb, dtbb = sb([8, 1], F32, f"dtb{l}")
            dma(alog[:], a_log[l].rearrange("(p o) -> p o", o=1), [], [alogb], nonc=True)
            dma(dtb[:], dt_bias[l].rearrange("(p o) -> p o", o=1), [], [dtbb], nonc=True)
            negA, negAb = sb([8, 1], F32, f"negA{l}")
            act(negA[:], alog[:], AF.Exp, [alogb], [negAb])
            ts("dve", negA[:], negA[:], -1.0, None, ALU.mult, None, [negAb], [negAb])

            if on("A1"):
                (war, wb) = next_arena()
                wA1 = war[:, 0:8 * 2064].rearrange("p (k c) -> p k c", k=8)
                for kc in range(8):
                    load_w(wA1[:, kc, :], wb, w_in[l, kc * 128:(kc + 1) * 128, 0:2064], 2064)
                xT_r = ring(2, [128, 8, XH])
                sq_r = ring(1, [128, 8, XH], BF16)
                hT_r = ring(2, [128, 8, XH], BF16)
                rstd_r = ring(2, [128, XH])
                tmp_r = ring(2, [128, XH])
                cvin_r = ring(2, [128, XH])
                acc_r = ring(2, [128, TA])
                sil_r = ring(2, [128, TA])
                sqb_r = ring(2, [128, TA], BF16)
                rn_r = ring(2, [128, TA])
                rt_r = ring(2, [128, TA])
                qk_r = ring(2, [128, 8, TA], BF16)
                v_r = ring(2, [128, 4, TA], BF16)
                kvt_r = ring(2, [128, 1024], BF16)
                z_r = ring(2, [128, 4, TA], BF16)
                abt = {k: carve([8, TA]) for k in ("e", "sp", "g", "pfx", "tmp", "sfx")}
                gs_r = ring(2, [8, 3, TA])
                gt_r = ring(2, [128, 24])
                for s, L in SL:
                    for blk in range(L // TA):
                        t0 = blk * TA
                        ht, hb, _, _ = norm_tile(XT[s][par], L, blk, g_pm, g_pmb, True, xT_r, sq_r, hT_r, rstd_r, tmp_r)
                        dma(HN[s].ap[blk].rearrange("p (c t) -> p c t", c=8), ht[:, :, 2:2 + TA], [hb], [HN[s].b(blk)])
                        (qk, qkb) = qk_r.next()
                        (vs, vsb) = v_r.next()
                        for ci in range(12):
                            (pt, pb) = psf.next()
                            mm(pt[:, 0:XH], pb, [(wA1[:, kc, ci * 128:(ci + 1) * 128], ht[:, kc, :]) for kc in range(8)], [wb, hb])
                            (cv, cvb) = cvin_r.next()
                            cp("act", cv, pt[:, 0:XH], [pb], [cvb])
                            (ac, acb) = acc_r.next()
                            ts("dve", ac, cv[:, 0:TA], cw[:, ci, 0:1], None, ALU.mult, None, [cvb, cwb], [acb])
                            for j in range(1, 5):
                                stt(ac, cv[:, j:j + TA], cw[:, ci, j:j + 1], ac, ALU.mult, ALU.add, [cvb, cwb, acb], [acb])
                            if ci < 8:
                                (sl, slb) = sil_r.next()
                                act(sl, ac, AF.Silu, [acb], [slb])
                                (sq2, sq2b) = sqb_r.next()
                                act(sq2, sl, AF.Square, [slb], [sq2b])
                                (p2, p2b) = psf.next()
                                mm(p2[:, 0:TA], p2b, [(ones_h[:], sq2)], [ones_hb, sq2b])
                                (rn, rnb) = rn_r.next()
                                (rt, rtb) = rt_r.next()
                                act(rt, p2[:, 0:TA], AF.Sqrt, [p2b, epsb], [rtb], bias=eps_t[:, 0:1], scale=1.0)
                                S.add("dve", lambda h, rn=rn, rt=rt: h.reciprocal(out=rn, in_=rt), [rtb], [rnb])
                                scl = float(DK ** -0.5) if ci < 4 else 1.0
                                stt(qk[:, ci, :], sl, scl, rn, ALU.mult, ALU.mult, [slb, rnb], [qkb])
                            else:
                                act(vs[:, ci - 8, :], ac, AF.Silu, [acb], [vsb])
                        dma(GQK[s].ap[blk].rearrange("p (c t) -> p c t", c=8), qk, [qkb], [GQK[s].b(blk)])
                        for j in range(NCH):
                            (ph, phb) = psh.next()

                            def fn(h, ph=ph, qk=qk, vs=vs, j=j):
                                ins = None
                                for hh in range(4):
                                    ins = h.transpose(out=ph[:, hh * 128:(hh + 1) * 128], in_=qk[:, 4 + hh, j * 128:(j + 1) * 128], identity=ident_h[:])
                                for hh in range(4):
                                    ins = h.transpose(out=ph[:, 512 + hh * 128:512 + (hh + 1) * 128], in_=vs[:, hh, j * 128:(j + 1) * 128], identity=ident_h[:])
                                return ins
                            S.add("pe", fn, [qkb, vsb, ident_hb], [phb])
                            (kv, kvb) = kvt_r.next()
                            cp("act", kv, ph[:], [phb], [kvb])
                            r0 = t0 + j * 128
                            dma(GKV[s].ap[r0:r0 + 128, :], kv, [kvb], [GKV[s].b(r0 // 128)])
                        (zs, zsb) = z_r.next()
                        for hh in range(4):
                            (pt, pb) = psf.next()
                            mm(pt[:, 0:TA], pb, [(wA1[:, kc, O_Z + hh * 128:O_Z + (hh + 1) * 128], ht[:, kc, 2:2 + TA]) for kc in range(8)], [wb, hb])
                            act(zs[:, hh, :], pt[:, 0:TA], AF.Silu, [pb], [zsb])
                        dma(ZT[s].ap[blk].rearrange("p (c t) -> p c t", c=4), zs, [zsb], [ZT[s].b(blk)])
                        (pa, pab) = psf.next()
                        mm(pa[0:8, 0:TA], pab, [(wA1[:, kc, O_A:O_A + 8], ht[:, kc, 2:2 + TA]) for kc in range(8)], [wb, hb])
                        e_t, e_b = abt["e"]
                        act(e_t, pa[0:8, 0:TA], AF.Exp, [pab, dtbb], [e_b], bias=dtb[:])
                        sp_t, sp_b = abt["sp"]
                        act(sp_t, e_t, AF.Ln, [e_b, epsb], [sp_b], bias=eps_t[0:8, 1:2])
                        g_t, g_b = abt["g"]
                        ts("dve", g_t, sp_t, negA[:, 0:1], None, ALU.mult, None, [sp_b, negAb], [g_b])
                        pf_t, pf_b = abt["pfx"]
                        S.add("dve", lambda h, pf_t=pf_t, g_t=g_t: h.tensor_tensor_scan(out=pf_t, data0=m128[:], data1=g_t, initial=0.0, op0=ALU.mult, op1=ALU.add), [m128b, g_b], [pf_b])
                        tm_t, tm_b = abt["tmp"]
                        pf3 = pf_t.rearrange("p (c k) -> p c k", k=128)
                        tot_bc = pf3[:, :, 127:128].to_broadcast([8, NCH, 128])
                        tt("dve", tm_t.rearrange("p (c k) -> p c k", k=128), tot_bc, pf3, ALU.subtract, [pf_b], [tm_b])
                        sf_t, sf_b = abt["sfx"]
                        tt("dve", sf_t, tm_t, g_t, ALU.add, [tm_b, g_b], [sf_b])
                        (gs, gsb) = gs_r.next()
                        ts("dve", gs[:, 0, :], pf_t, sel[:, 0:1], None, ALU.mult, None, [pf_b, selb], [gsb])
                        stt(gs[:, 0, :], sf_t, sel[:, 1:2], gs[:, 0, :], ALU.mult, ALU.add, [sf_b, selb, gsb], [gsb])
                        (pbb, pbbb) = psf.next()
                        mm(pbb[0:8, 0:TA], pbbb, [(wA1[:, kc, O_B:O_B + 8], ht[:, kc, 2:2 + TA]) for kc in range(8)], [wb, hb])
                        act(gs[:, 1, :], pbb[0:8, 0:TA], AF.Sigmoid, [pbbb], [gsb])
                        tt("dve", gs[:, 2, :].rearrange("p (c k) -> p c k", k=128), tot_bc, gs[:, 0, :].rearrange("p (c k) -> p c k", k=128), ALU.subtract, [pf_b, gsb], [gsb])
                        dma(GS[s].ap[:, :, t0:t0 + TA].rearrange("q r t -> r q t"), gs, [gsb], [GS[s].b(blk)])
                        for j in range(NCH):
                            (pt, pb) = psf.next()
                            mmh(pt, pb, [(pt[:, q * 8:(q + 1) * 8], [(gs[:, q, j * 128:(j + 1) * 128], ident_f[0:8, 0:8])]) for q in range(3)], [gsb, ident_fb])
                            (gt, gtb) = gt_r.next()
                            cp("act", gt, pt[:, 0:24], [pb], [gtb])
                            r0 = t0 + j * 128
                            dma(GT[s].ap[r0:r0 + 128, :], gt, [gtb], [GT[s].b(r0 // 128)])
                end_phase()

            if on("A2a"):
                (war, wb) = next_arena()
                wA = war[:, 0:8 * 2048].rearrange("p (k c) -> p k c", k=8)
                for kc in range(8):
                    load_w(wA[:, kc, :], wb, w_in[l, kc * 128:(kc + 1) * 128, O_QH:O_QH + 2048], 2048)
                CQ, CF, CI = 0, 512, 1536
                hT_r = ring(2, [128, 8, TA], BF16)
                qs_r = ring(2, [128, 4, TA])
                ih_r = ring(2, [128, 4, TA], BF16)
                hq_r = [ring(2, [128, 4, TA], BF16) for _ in range(2)]
                hk_r = [ring(2, [128, 4, TA], BF16) for _ in range(2)]
                ab_r = ring(2, [128, 64])
                f_r = ring(18, [128, TA])
                tok_r = ring(2, [128, 1536], BF16)
                scl = float(DK ** -0.5)
                for s, L in SL:
                    for blk in range(L // TA):
                        t0 = blk * TA
                        (ht, hb) = hT_r.next()
                        dma(ht, HN[s].ap[blk].rearrange("p (c t) -> p c t", c=8), [HN[s].b(blk)], [hb])
                        (qs, qsb) = qs_r.next()
                        (ih, ihb) = ih_r.next()
                        for hh in range(4):
                            (pt, pb) = psf.next()
                            mm(pt[:, 0:TA], pb, [(wA[:, kc, CQ + hh * 128:CQ + (hh + 1) * 128], ht[:, kc, :]) for kc in range(8)], [wb, hb])
                            act(qs[:, hh, :], pt[:, 0:TA], AF.Silu, [pb], [qsb])
                            (pt, pb) = psf.next()
                            mm(pt[:, 0:TA], pb, [(wA[:, kc, CI + hh * 128:CI + (hh + 1) * 128], ht[:, kc, :]) for kc in range(8)], [wb, hb])
                            cp("act", ih[:, hh, :], pt[:, 0:TA], [pb], [ihb])
                        (ab, abb) = ab_r.next()
                        ab5 = ab.rearrange("p (d q h c) -> p d q h c", d=2, q=2, h=4)
                        hqs = [hq_r[d_].next() for d_ in range(2)]
                        hks = [hk_r[d_].next() for d_ in range(2)]
                        for d_ in range(2):
                            (hq, hqb) = hqs[d_]
                            (hk, hkb) = hks[d_]
                            for hh in range(4):
                                dh = d_ * 4 + hh
                                (pt, pb) = psf.next()
                                c0 = CF + d_ * 512 + hh * 128
                                mm(pt[:, 0:TA], pb, [(wA[:, kc, c0:c0 + 128], ht[:, kc, :]) for kc in range(8)], [wb, hb])
                                (sg, sgb) = f_r.next()
                                act(sg, pt[:, 0:TA], AF.Sigmoid, [pb], [sgb])
                                (ff, ffb) = f_r.next()
                                ts("dve", ff, sg, omlt[:, l, dh:dh + 1], lbt[:, l, dh:dh + 1], ALU.mult, ALU.add, [sgb, omlb, lbb], [ffb])
                                (lf, lfb) = f_r.next()
                                act(lf, ff, AF.Ln, [ffb], [lfb])
                                (kk, kkb) = f_r.next()
                                ts("pool", kk, ff, -1.0, 1.0, ALU.mult, ALU.add, [ffb], [kkb])
                                (pf, pfb) = f_r.next()
                                S.add("dve", lambda h, pf=pf, lf=lf: h.tensor_tensor_scan(out=pf, data0=m64[:], data1=lf, initial=0.0, op0=ALU.mult, op1=ALU.add), [m64b, lfb], [pfb])
                                if d_ == 0:
                                    cum, cumb = pf, pfb
                                    ri, ai = 31, 63
                                else:
                                    pf3 = pf.rearrange("p (c k) -> p c k", k=64)
                                    (tm, tmb) = f_r.next()
                                    tt("pool", tm.rearrange("p (c k) -> p c k", k=64), pf3[:, :, 63:64].to_broadcast([128, NC64, 64]), pf3, ALU.subtract, [pfb], [tmb])
                                    (cum, cumb) = f_r.next()
                                    tt("pool", cum, tm, lf, ALU.add, [tmb, lfb], [cumb])
                                    ri, ai = 32, 0
                                cum3 = cum.rearrange("p (c k) -> p c k", k=64)
                                (dd, ddb) = f_r.next()
                                tt("pool", dd.rearrange("p (c k) -> p c k", k=64), cum3, cum3[:, :, ri:ri + 1].to_broadcast([128, NC64, 64]), ALU.subtract, [cumb], [ddb])
                                (eq, eqb) = f_r.next()
                                act(eq, dd, AF.Exp, [ddb], [eqb])
                                (ek, ekb) = f_r.next()
                                act(ek, dd, AF.Exp, [ddb], [ekb], scale=-1.0)
                                stt(hq[:, hh, :], qs[:, hh, :], scl, eq, ALU.mult, ALU.mult, [qsb, eqb], [hqb])
                                tt("pool", hk[:, hh, :], kk, ek, ALU.mult, [kkb, ekb], [hkb])
                                cp("pool", ab5[:, d_, 0, hh, :], eq.rearrange("p (c k) -> p c k", k=64)[:, :, ai], [eqb], [abb])
                                act(ab5[:, d_, 1, hh, :], cum3[:, :, ri], AF.Exp, [cumb], [abb])
                            dma(HQd[s].ap[d_, blk].rearrange("p (c t) -> p c t", c=4), hq, [hqb], [HQd[s].b((d_, blk))])
                            dma(HKd[s].ap[d_, blk].rearrange("p (c t) -> p c t", c=4), hk, [hkb], [HKd[s].b((d_, blk))])
                        dma(HAB[s].ap[blk], ab, [abb], [HAB[s].b(blk)])
                        for j in range(NCH):
                            (ph, phb) = psh.next()
                            (ph2, ph2b) = psh.next()

                            def fn(h, ph=ph, hks=hks, j=j):
                                ins = None
                                for d_ in range(2):
                                    for hh in range(4):
                                        o = (d_ * 4 + hh) * 128
                                        ins = h.transpose(out=ph[:, o:o + 128], in_=hks[d_][0][:, hh, j * 128:(j + 1) * 128], identity=ident_h[:])
                                return ins
                            S.add("pe", fn, [hks[0][1], hks[1][1], ident_hb], [phb])

                            def fn2(h, ph2=ph2, ih=ih, j=j):
                                ins = None
                                for hh in range(4):
                                    ins = h.transpose(out=ph2[:, hh * 128:(hh + 1) * 128], in_=ih[:, hh, j * 128:(j + 1) * 128], identity=ident_h[:])
                                return ins
                            S.add("pe", fn2, [ihb, ident_hb], [ph2b])
                            (tk, tkb) = tok_r.next()
                            cp("act", tk[:, 0:1024], ph[:], [phb], [tkb])
                            cp("dve", tk[:, 1024:1536], ph2[:, 0:512], [ph2b], [tkb])
                            r0 = t0 + j * 128
                            dma(HT[s].ap[r0:r0 + 128, :], tk, [tkb], [HT[s].b(r0 // 128)])
                end_phase()

            if on("A2b"):
                (war, wb) = next_arena()
                wA = war[:, 0:8 * 2560].rearrange("p (k c) -> p k c", k=8)
                for kc in range(8):
                    load_w(wA[:, kc, :], wb, w_in[l, kc * 128:(kc + 1) * 128, O_GH:O_GH + 2560], 2560)
                hT_r = ring(2, [128, 8, TA], BF16)
                gh_r = ring(2, [128, 4, TA], BF16)
                mg_r = ring(2, [128, 8, TA], BF16)
                for s, L in SL:
                    for blk in range(L // TA):
                        (ht, hb) = hT_r.next()
                        dma(ht, HN[s].ap[blk].rearrange("p (c t) -> p c t", c=8), [HN[s].b(blk)], [hb])
                        (gh, ghb) = gh_r.next()
                        for hh in range(4):
                            (pt, pb) = psf.next()
                            mm(pt[:, 0:TA], pb, [(wA[:, kc, hh * 128:(hh + 1) * 128], ht[:, kc, :]) for kc in range(8)], [wb, hb])
                            act(gh[:, hh, :], pt[:, 0:TA], AF.Silu, [pb], [ghb])
                        dma(GHT[s].ap[blk].rearrange("p (c t) -> p c t", c=4), gh, [ghb], [GHT[s].b(blk)])
                        for half in range(2):
                            (mg, mgb) = mg_r.next()
                            for c in range(8):
                                c0 = 512 + (half * 8 + c) * 128
                                (pt, pb) = psf.next()
                                mm(pt[:, 0:TA], pb, [(wA[:, kc, c0:c0 + 128], ht[:, kc, :]) for kc in range(8)], [wb, hb])
                                act(mg[:, c, :], pt[:, 0:TA], AF.Sigmoid, [pb], [mgb])
                            dma(MG[s].ap[blk, :, half * 8 * TA:(half + 1) * 8 * TA].rearrange("p (c t) -> p c t", c=8), mg, [mgb], [MG[s].b((blk, half))])
                end_phase()

            if on("B1"):
                for s, L in SL:
                    nch = L // 128
                    for d_ in range(2):
                        mA, mBi, mBs = (1, 2, 3) if d_ == 0 else (3, 0, 1)
                        li = 127 if d_ == 0 else 0
                        S32, S32b = carve([128, 4, 128])
                        Sbf, Sbfb = carve([128, 4, 128], BF16)
                        mset("pool", S32, 0.0, [S32b])
                        mset("pool", Sbf, 0.0, [Sbfb])
                        qk_r = ring(2, [128, 8, TA], BF16)
                        os_r = ring(2, [128, 4, TA], BF16)
                        kv_r = ring(2, [128, 1024], BF16)
                        gt_r = ring(2, [128, 24])
                        gsb_r = ring(2, [128, 2, 4, 128])
                        sm_r = ring(6, [128, 4])
                        fm_r = ring(16, [128, 4, 128])
                        lf_r = ring(4, [128, 4, 128])
                        hm_r = ring(16, [128, 4, 128], BF16)
                        order = range(nch) if d_ == 0 else range(nch - 1, -1, -1)
                        cur_blk = -1
                        for c in order:
                            blk, cc = c // NCH, c % NCH
                            cs = slice(cc * 128, (cc + 1) * 128)
                            t0 = c * 128
                            if blk != cur_blk:
                                if cur_blk >= 0:
                                    dma(OA[s].ap[d_, cur_blk].rearrange("p (c t) -> p c t", c=4), ost, [ostb], [OA[s].b((d_, cur_blk))])
                                cur_blk = blk
                                (qk, qkb) = qk_r.next()
                                dma(qk, GQK[s].ap[blk].rearrange("p (c t) -> p c t", c=8), [GQK[s].b(blk)], [qkb])
                                (ost, ostb) = os_r.next()
                            (kv, kvb) = kv_r.next()
                            dma(kv, GKV[s].ap[t0:t0 + 128, :], [GKV[s].b(c)], [kvb])
                            (gt, gtb) = gt_r.next()
                            dma(gt, GT[s].ap[t0:t0 + 128, :], [GT[s].b(c)], [gtb])
                            (gsr, gsrb) = gsb_r.next()
                            for q_ in range(2):
                                gsap = AP(GS[s].ap.tensor, (q_ * 8 + d_ * 4) * L + t0, [[0, 128], [L, 4], [1, 128]])
                                dma(gsr[:, q_], gsap, [GS[s].b(t0 // TA)], [gsrb])
                            cumc = gt[:, d_ * 4:d_ * 4 + 4]
                            betac = gt[:, 8 + d_ * 4:12 + d_ * 4]
                            remc = gt[:, 16 + d_ * 4:20 + d_ * 4]
                            kT = qk[:, 4:8, cs]
                            qT = qk[:, 0:4, cs]
                            kv3k = kv[:, 0:512].rearrange("p (h d) -> p h d", h=4)
                            kv3v = kv[:, 512:1024].rearrange("p (h d) -> p h d", h=4)
                            (ecum, ecumb) = sm_r.next()
                            act(ecum, cumc, AF.Exp, [gtb], [ecumb])
                            (bec, becb) = sm_r.next()
                            tt("dve", bec, betac, ecum, ALU.mult, [gtb, ecumb], [becb])
                            (erem, eremb) = sm_r.next()
                            act(erem, remc, AF.Exp, [gtb], [eremb])
                            (erow, erowb) = lf_r.next()
                            act(erow, gsr[:, 0], AF.Exp, [gsrb], [erowb])
                            (pkk, pkkb) = psf.next()
                            mmh(pkk, pkkb, [(pkk[:, h_ * 128:(h_ + 1) * 128], [(kT[:, h_, :], kT[:, h_, :])]) for h_ in range(4)], [qkb])
                            (pqk, pqkb) = psf.next()
                            mmh(pqk, pqkb, [(pqk[:, h_ * 128:(h_ + 1) * 128], [(kT[:, h_, :], qT[:, h_, :])]) for h_ in range(4)], [qkb])
                            pkk3 = pkk[:].rearrange("p (h t) -> p h t", h=4)
                            pqk3 = pqk[:].rearrange("p (h t) -> p h t", h=4)
                            (da, dab) = fm_r.next()
                            stt(da, gsr[:, 0], -1.0, bc(cumc, 128), ALU.mult, ALU.add, [gsrb, gtb], [dab])
                            tt("pool", da, da, bch(mask_f[:, 4 + mA, :], 4), ALU.add, [dab, mask_fb], [dab])
                            (ea, eab) = fm_r.next()
                            act(ea, da, AF.Exp, [dab], [eab])
                            (db, dbb) = fm_r.next()
                            stt(db, bc(cumc, 128), -1.0, gsr[:, 0], ALU.mult, ALU.add, [gsrb, gtb], [dbb])
                            tt("pool", db, db, bch(mask_f[:, 4 + mBi, :], 4), ALU.add, [dbb, mask_fb], [dbb])
                            (eb, ebb) = fm_r.next()
                            act(eb, db, AF.Exp, [dbb], [ebb])
                            (A0, A0b) = fm_r.next()
                            tt("dve", A0, pkk3, ea, ALU.mult, [pkkb, eab], [A0b])
                            (Am, Amb) = fm_r.next()
                            tt("pool", Am, A0, bc(betac, 128), ALU.mult, [A0b, gtb], [Amb])
                            (B0, B0b) = fm_r.next()
                            tt("dve", B0, pkk3, eb, ALU.mult, [pkkb, ebb], [B0b])
                            (bm, bmb) = fm_r.next()
                            tt("pool", bm, gsr[:, 1], bch(mask_f[:, mBs, :], 4), ALU.mult, [gsrb, mask_fb], [bmb])
                            (Bm, Bmb) = fm_r.next()
                            tt("pool", Bm, B0, bm, ALU.mult, [B0b, bmb], [Bmb])
                            (qkT, qkTb) = hm_r.next()
                            tt("dve", qkT, pqk3, eb, ALU.mult, [pqkb, ebb], [qkTb])
                            (Y, Yb) = fm_r.next()
                            tt("pool", Y, bch(ident_f[:], 4), Bm, ALU.subtract, [ident_fb, Bmb], [Yb])
                            P, Pb, Pt, Ptb = Am, Amb, Bm, Bmb
                            TTbf = None
                            for k in range(1, 7):
                                (pp, ppb) = psf.next()
                                mmh(pp, ppb, [(pp[:, h_ * 128:(h_ + 1) * 128], [(Pt[:, h_, :], P[:, h_, :])]) for h_ in range(4)], [Pb, Ptb])
                                (Pn, Pnb) = fm_r.next()
                                cp("act", Pn, pp[:].rearrange("p (h t) -> p h t", h=4), [ppb], [Pnb])
                                if k < 6:
                                    (pp2, pp2b) = psf.next()
                                    mmh(pp2, pp2b, [(pp2[:, h_ * 128:(h_ + 1) * 128], [(P[:, h_, :], Pt[:, h_, :])]) for h_ in range(4)], [Pb, Ptb])
                                    (Ptn, Ptnb) = fm_r.next()
                                    cp("act", Ptn, pp2[:].rearrange("p (h t) -> p h t", h=4), [pp2b], [Ptnb])
                                (pp3, pp3b) = psf.next()
                                mmh(pp3, pp3b, [(pp3[:, h_ * 128:(h_ + 1) * 128], [(Pn[:, h_, :], Y[:, h_, :])]) for h_ in range(4)], [Pnb, Yb])
                                if k < 6:
                                    (Yn, Ynb) = fm_r.next()
                                    tt("dve", Yn, pp3[:].rearrange("p (h t) -> p h t", h=4), Y, ALU.add, [pp3b, Yb], [Ynb])
                                    Y, Yb = Yn, Ynb
                                    P, Pb, Pt, Ptb = Pn, Pnb, Ptn, Ptnb
                                else:
                                    (TTbf, TTb) = hm_r.next()
                                    tt("dve", TTbf, pp3[:].rearrange("p (h t) -> p h t", h=4), Y, ALU.add, [pp3b, Yb], [TTb])
                            (bv, bvb) = hm_r.next()
                            tt("pool", bv, kv3v, bc(betac, 128), ALU.mult, [kvb, gtb], [bvb])
                            (kbe, kbeb) = hm_r.next()
                            tt("pool", kbe, kv3k, bc(bec, 128), ALU.mult, [kvb, becb], [kbeb])
                            (kdt, kdtb) = hm_r.next()
                            tt("pool", kdt, kv3k, bc(erem, 128), ALU.mult, [kvb, eremb], [kdtb])
                            (pu, pub) = psf.next()
                            mmh(pu, pub, [(pu[:, h_ * 128:(h_ + 1) * 128], [(TTbf[:, h_, :], bv[:, h_, :])]) for h_ in range(4)], [TTb, bvb])
                            (u, ub) = lf_r.next()
                            cp("act", u, pu[:].rearrange("p (h t) -> p h t", h=4), [pub], [ub])
                            (pw, pwb) = psf.next()
                            mmh(pw, pwb, [(pw[:, h_ * 128:(h_ + 1) * 128], [(kbe[:, h_, :], TTbf[:, h_, :])]) for h_ in range(4)], [TTb, kbeb])
                            (wT, wTb) = hm_r.next()
                            cp("act", wT, pw[:].rearrange("p (h t) -> p h t", h=4), [pwb], [wTb])
                            (qdT, qdTb) = hm_r.next()
                            tt("dve", qdT, qT, erow, ALU.mult, [qkb, erowb], [qdTb])
                            (pv, pvb) = psf.next()
                            mmh(pv, pvb, [(pv[:, h_ * 128:(h_ + 1) * 128], [(wT[:, h_, :], Sbf[:, h_, :])]) for h_ in range(4)], [wTb, Sbfb])
                            (vn, vnb) = hm_r.next()
                            tt("dve", vn, u, pv[:].rearrange("p (h t) -> p h t", h=4), ALU.subtract, [ub, pvb], [vnb])
                            (po, pob) = psf.next()
                            mmh(po, pob, [(po[:, h_ * 128:(h_ + 1) * 128], [(Sbf[:, h_, :], qdT[:, h_, :]), (vn[:, h_, :], qkT[:, h_, :])]) for h_ in range(4)], [Sbfb, qdTb, vnb, qkTb])
                            cp("act", ost[:, :, cs], po[:].rearrange("p (h t) -> p h t", h=4), [pob], [ostb])
                            (ps_, psb) = psf.next()
                            mmh(ps_, psb, [(ps_[:, h_ * 128:(h_ + 1) * 128], [(kdt[:, h_, :], vn[:, h_, :])]) for h_ in range(4)], [kdtb, vnb])
                            for h_ in range(4):
                                stt(S32[:, h_, :], S32[:, h_, :], erow[:, h_, li:li + 1], ps_[:, h_ * 128:(h_ + 1) * 128], ALU.mult, ALU.add, [S32b, erowb, psb], [S32b])
                            cp("pool", Sbf, S32, [S32b], [Sbfb])
                        dma(OA[s].ap[d_, cur_blk].rearrange("p (c t) -> p c t", c=4), ost, [ostb], [OA[s].b((d_, cur_blk))])
                        end_phase()

            if on("B2"):
                for s, L in SL:
                    n64 = L // 64
                    nb = L // TA
                    for d_ in range(2):
                        mI = 2 if d_ == 0 else 0
                        abt_, abtb = carve([128, nb, 64])
                        dma(abt_, HAB[s].ap.rearrange("b p x -> p b x"), [HAB[s].b(k) for k in range(nb)], [abtb])
                        ab6 = abt_.rearrange("p b (d q h c) -> p b d q h c", d=2, q=2, h=4)
                        Aall, Aallb = carve([128, 4, n64])
                        Ball, Ballb = carve([128, 4, n64])
                        rr, rrb = carve([128, 4, n64])
                        for h_ in range(4):
                            cp("dve", Aall[:, h_, :].rearrange("p (b c) -> p b c", c=NC64), ab6[:, :, d_, 0, h_, :], [abtb], [Aallb])
                            cp("dve", Ball[:, h_, :].rearrange("p (b c) -> p b c", c=NC64), ab6[:, :, d_, 1, h_, :], [abtb], [Ballb])
                        if d_ == 0:
                            tt("dve", rr[:, :, 0:n64 - 1], Aall[:, :, 0:n64 - 1], Ball[:, :, 1:n64], ALU.mult, [Aallb, Ballb], [rrb])
                        else:
                            tt("dve", rr[:, :, 1:n64], Aall[:, :, 1:n64], Ball[:, :, 0:n64 - 1], ALU.mult, [Aallb, Ballb], [rrb])
                        S32, S32b = carve([128, 4, 128])
                        Sbf, Sbfb = carve([128, 4, 128], BF16)
                        mset("pool", S32, 0.0, [S32b])
                        mset("pool", Sbf, 0.0, [Sbfb])
                        qd_r = ring(2, [128, 4, TA], BF16)
                        kd_r = ring(2, [128, 4, TA], BF16)
                        os_r = ring(2, [128, 4, TA], BF16)
                        tok_r = ring(3, [64, 1536], BF16)
                        am_r = ring(2, [64, 4, 64], BF16)
                        tmp_r = ring(2, [128, 4, 128])
                        order = list(range(n64)) if d_ == 0 else list(range(n64 - 1, -1, -1))
                        cur_blk = -1
                        for ci_, c in enumerate(order):
                            blk, cc = c // NC64, c % NC64
                            cs = slice(cc * 64, (cc + 1) * 64)
                            if blk != cur_blk:
                                if cur_blk >= 0:
                                    dma(OB[s].ap[d_, cur_blk].rearrange("p (c t) -> p c t", c=4), ost, [ostb], [OB[s].b((d_, cur_blk))])
                                cur_blk = blk
                                (qd, qdb) = qd_r.next()
                                dma(qd, HQd[s].ap[d_, blk].rearrange("p (c t) -> p c t", c=4), [HQd[s].b((d_, blk))], [qdb])
                                (kd, kdb) = kd_r.next()
                                dma(kd, HKd[s].ap[d_, blk].rearrange("p (c t) -> p c t", c=4), [HKd[s].b((d_, blk))], [kdb])
                                (ost, ostb) = os_r.next()
                            (tk, tkb) = tok_r.next()
                            dma(tk, HT[s].ap[c * 64:(c + 1) * 64, :], [HT[s].b(c // 2)], [tkb])
                            (pa, pab) = psf.next()
                            mmh(pa, pab, [(pa[0:64, h_ * 64:(h_ + 1) * 64], [(kd[:, h_, cs], qd[:, h_, cs])]) for h_ in range(4)], [kdb, qdb])
                            (am, amb) = am_r.next()
                            tt("dve", am, pa[0:64, 0:256].rearrange("p (h t) -> p h t", h=4), bch(mask_f[0:64, mI, 0:64], 4), ALU.mult, [pab, mask_fb], [amb])
                            (po, pob) = psf.next()
                            mmh(po, pob, [(po[:, h_ * 64:(h_ + 1) * 64], [(Sbf[:, h_, :], qd[:, h_, cs]), (tk[:, 1024 + h_ * 128:1024 + (h_ + 1) * 128], am[:, h_, :])]) for h_ in range(4)], [Sbfb, qdb, tkb, amb])
                            cp("act", ost[:, :, cs], po[:, 0:256].rearrange("p (h t) -> p h t", h=4), [pob], [ostb])
                            if ci_ < n64 - 1:
                                (pp, ppb) = psf.next()
                                mmh(pp, ppb, [(pp[:, h_ * 128:(h_ + 1) * 128], [(tk[:, d_ * 512 + h_ * 128:d_ * 512 + (h_ + 1) * 128], tk[:, 1024 + h_ * 128:1024 + (h_ + 1) * 128])]) for h_ in range(4)], [tkb])
                                (tm, tmb) = tmp_r.next()
                                tt("dve", tm, pp[:].rearrange("p (h t) -> p h t", h=4), S32, ALU.add, [ppb, S32b], [tmb])
                                tt("pool", S32, tm, bc(rr[:, :, c], 128), ALU.mult, [tmb, rrb], [S32b])
                                tt("dve", Sbf, tm, bc(rr[:, :, c], 128), ALU.mult, [tmb, rrb], [Sbfb])
                        dma(OB[s].ap[d_, cur_blk].rearrange("p (c t) -> p c t", c=4), ost, [ostb], [OB[s].b((d_, cur_blk))])
                        end_phase()

            if on("C1"):
                (war, wb) = next_arena()
                wbg_ = war[:, 0:4096].rearrange("p (k c) -> p k c", k=4)
                wbh_ = war[:, 4096:8192].rearrange("p (k c) -> p k c", k=4)
                wo_ = war[:, 8192:16384].rearrange("p (k c) -> p k c", k=8)
                for kc in range(4):
                    load_w(wbg_[:, kc, :], wb, w_bg[l, kc * 128:(kc + 1) * 128, :], 1024)
                    load_w(wbh_[:, kc, :], wb, w_bh[l, kc * 128:(kc + 1) * 128, :], 1024)
                for kc in range(8):
                    load_w(wo_[:, kc, :], wb, w_o[l, kc * 128:(kc + 1) * 128, :], 1024)
                in_r = ring(6, [128, 4, TA], BF16)
                mg_r = ring(2, [128, 8, TA], BF16)
                x_r = ring(1, [128, 8, TA])
                o_r = ring(2, [128, 4, TA])
                rs4_r = ring(1, [128, 4, TA])
                rt4_r = ring(1, [128, 4, TA])
                on_r = ring(1, [128, 4, TA])
                sq4_r = ring(1, [128, 4, TA], BF16)
                onz_r = ring(2, [128, 4, TA], BF16)
                mrg_r = ring(1, [128, 8, TA], BF16)
                f_r = ring(6, [128, TA])
                mo_r = ring(1, [128, 8, TA])
                sq8_r = ring(1, [128, 8, TA], BF16)
                res_r = ring(1, [128, 8, TA])
                rstd_r = ring(2, [128, TA])
                tmp_r = ring(2, [128, TA])
                for s, L in SL:
                    for blk in range(L // TA):
                        def ld4(dt_, key):
                            (t_, tb_) = in_r.next()
                            dma(t_, dt_.rearrange("p (c t) -> p c t", c=4), [key], [tb_])
                            return t_, tb_
                        onz = []
                        for (OX, GX, nw_, nwb_) in ((OA, ZT, nwa, nwab), (OB, GHT, nwh, nwhb)):
                            (of, ofb) = ld4(OX[s].ap[0, blk], OX[s].b((0, blk)))
                            (ob_, obb) = ld4(OX[s].ap[1, blk], OX[s].b((1, blk)))
                            (gz, gzb) = ld4(GX[s].ap[blk], GX[s].b(blk))
                            (o, ob2) = o_r.next()
                            tt("dve", o, of, ob_, ALU.add, [ofb, obb], [ob2])
                            (sq4, sq4b) = sq4_r.next()
                            act(sq4, o, AF.Square, [ob2], [sq4b])
                            (rs4, rs4b) = rs4_r.next()
                            (rt4, rt4b) = rt4_r.next()
                            for hp in range(2):
                                (pt, pb) = psf.next()
                                mmh(pt, pb, [(pt[:, q * TA:(q + 1) * TA], [(ones_h[:], sq4[:, hp * 2 + q, :])]) for q in range(2)], [ones_hb, sq4b])
                                act(rt4[:, hp * 2:hp * 2 + 2, :], pt[:].rearrange("p (h t) -> p h t", h=2), AF.Sqrt, [pb, epsb], [rt4b], bias=eps_t[:, 0:1], scale=1.0 / 128)
                            S.add("dve", lambda h, rs4=rs4, rt4=rt4: h.reciprocal(out=rs4, in_=rt4), [rt4b], [rs4b])
                            (on_, onb) = on_r.next()
                            stt(on_, o, nw_[:, 0:1], rs4, ALU.mult, ALU.mult, [ob2, nwb_, rs4b], [onb])
                            (oz, ozb) = onz_r.next()
                            tt("pool", oz, on_, gz, ALU.mult, [onb, gzb], [ozb])
                            onz.append((oz, ozb))
                        mgs = []
                        for half in range(2):
                            (mg, mgb) = mg_r.next()
                            dma(mg, MG[s].ap[blk, :, half * 8 * TA:(half + 1) * 8 * TA].rearrange("p (c t) -> p c t", c=8), [MG[s].b((blk, half))], [mgb])
                            mgs.append((mg, mgb))
                        (xt, xb) = x_r.next()
                        dma(xt, XT[s][par].ap[:, :, blk * TA:(blk + 1) * TA].rearrange("c p t -> p c t"), [XT[s][par].b(blk)], [xb])
                        (mrg, mrgb) = mrg_r.next()
                        for oc in range(8):
                            (pa, pab) = psf.next()
                            mm(pa[:, 0:TA], pab, [(wbg_[:, kc, oc * 128:(oc + 1) * 128], onz[0][0][:, kc, :]) for kc in range(4)], [wb, onz[0][1]])
                            (pb_, pbb) = psf.next()
                            mm(pb_[:, 0:TA], pbb, [(wbh_[:, kc, oc * 128:(oc + 1) * 128], onz[1][0][:, kc, :]) for kc in range(4)], [wb, onz[1][1]])
                            (t1, t1b) = f_r.next()
                            tt("dve", t1, pa[:, 0:TA], mgs[0][0][:, oc, :], ALU.mult, [pab, mgs[0][1]], [t1b])
                            (t2, t2b) = f_r.next()
                            tt("dve", t2, pb_[:, 0:TA], mgs[1][0][:, oc, :], ALU.mult, [pbb, mgs[1][1]], [t2b])
                            tt("pool", mrg[:, oc, :], t1, t2, ALU.add, [t1b, t2b], [mrgb])
                        (mo, mob) = mo_r.next()
                        (sq8, sq8b) = sq8_r.next()
                        for oc in range(8):
                            (pt, pb) = psf.next()
                            mm(pt[:, 0:TA], pb, [(wo_[:, kc, oc * 128:(oc + 1) * 128], mrg[:, kc, :]) for kc in range(8)], [wb, mrgb])
                            cp("act", mo[:, oc, :], pt[:, 0:TA], [pb], [mob])
                            act(sq8[:, oc, :], pt[:, 0:TA], AF.Square, [pb], [sq8b])
                        res, resb = post_norm_add(res_r, xt, xb, mo, mob, sq8, sq8b, g_po, g_pob, rstd_r, tmp_r, f_r)
                        dma(XM[s].ap[:, :, blk * TA:(blk + 1) * TA].rearrange("c p t -> p c t"), res, [resb], [XM[s].b(blk)])
                end_phase()

            for half in range(2):
                if not on("C2a"):
                    continue
                (war, wb) = next_arena()
                NF = 11 * 128
                wg_ = war[:, 0:8 * NF].rearrange("p (k c) -> p k c", k=8)
                wu_ = war[:, 8 * NF:16 * NF].rearrange("p (k c) -> p k c", k=8)
                for kc in range(8):
                    load_w(wg_[:, kc, :], wb, w_g[l, kc * 128:(kc + 1) * 128, half * NF:(half + 1) * NF], NF)
                    load_w(wu_[:, kc, :], wb, w_u[l, kc * 128:(kc + 1) * 128, half * NF:(half + 1) * NF], NF)
                xT_r = ring(2, [128, 8, TA])
                sq_r = ring(1, [128, 8, TA], BF16)
                hT_r = ring(2, [128, 8, TA], BF16)
                rstd_r = ring(2, [128, TA])
                tmp_r = ring(2, [128, TA])
                sg_r = ring(3, [128, TA])
                a_r = ring(2, [128, 11, TA], BF16)
                for s, L in SL:
                    for blk in range(L // TA):
                        if half == 0:
                            ht, hb, _, _ = norm_tile(XM[s], L, blk, g_pf, g_pfb, False, xT_r, sq_r, hT_r, rstd_r, tmp_r)
                            dma(H2[s].ap[blk].rearrange("p (c t) -> p c t", c=8), ht, [hb], [H2[s].b(blk)])
                        else:
                            (ht, hb) = hT_r.next()
                            dma(ht, H2[s].ap[blk].rearrange("p (c t) -> p c t", c=8), [H2[s].b(blk)], [hb])
                        (a_, ab_) = a_r.next()
                        for fc in range(11):
                            (pg, pgb) = psf.next()
                            mm(pg[:, 0:TA], pgb, [(wg_[:, kc, fc * 128:(fc + 1) * 128], ht[:, kc, 0:TA]) for kc in range(8)], [wb, hb])
                            (pu, pub) = psf.next()
                            mm(pu[:, 0:TA], pub, [(wu_[:, kc, fc * 128:(fc + 1) * 128], ht[:, kc, 0:TA]) for kc in range(8)], [wb, hb])
                            (sg, sgb) = sg_r.next()
                            act(sg, pg[:, 0:TA], AF.Silu, [pgb], [sgb])
                            tt("dve", a_[:, fc, :], sg, pu[:, 0:TA], ALU.mult, [sgb, pub], [ab_])
                        dma(ACTS[s].ap[blk, :, half * 11 * TA:(half + 1) * 11 * TA].rearrange("p (c t) -> p c t", c=11), a_, [ab_], [ACTS[s].b((blk, half))])
                end_phase()

            if on("C2b"):
                (war, wb) = next_arena()
                wd_ = war[:, 0:22 * 1024].rearrange("p (k c) -> p k c", k=22)
                for fc in range(22):
                    load_w(wd_[:, fc, :], wb, w_d[l, fc * 128:(fc + 1) * 128, :], 1024)
                a_r = ring(2, [128, 22, TA], BF16)
                x_r = ring(2, [128, 8, TA])
                ff_r = ring(1, [128, 8, TA])
                sq8_r = ring(1, [128, 8, TA], BF16)
                res_r = ring(2, [128, 8, TA])
                rstd_r = ring(2, [128, TA])
                tmp_r = ring(2, [128, TA])
                f_r = ring(4, [128, TA])
                yt_r = ring(2, [128, D])
                last = (l == depth - 1)
                for s, L in SL:
                    for blk in range(L // TA):
                        (a_, ab_) = a_r.next()
                        dma(a_, ACTS[s].ap[blk].rearrange("p (c t) -> p c t", c=22), [ACTS[s].b((blk, 0)), ACTS[s].b((blk, 1))], [ab_])
                        (xt, xb) = x_r.next()
                        dma(xt, XM[s].ap[:, :, blk * TA:(blk + 1) * TA].rearrange("c p t -> p c t"), [XM[s].b(blk)], [xb])
                        (ffo, ffob) = ff_r.next()
                        (sq8, sq8b) = sq8_r.next()
                        for oc in range(8):
                            (pt, pb) = psf.next()
                            mm(pt[:, 0:TA], pb, [(wd_[:, fc, oc * 128:(oc + 1) * 128], a_[:, fc, :]) for fc in range(22)], [wb, ab_])
                            cp("act", ffo[:, oc, :], pt[:, 0:TA], [pb], [ffob])
                            act(sq8[:, oc, :], pt[:, 0:TA], AF.Square, [pb], [sq8b])
                        res, resb = post_norm_add(res_r, xt, xb, ffo, ffob, sq8, sq8b, g_qf, g_qfb, rstd_r, tmp_r, f_r)
                        if not last:
                            dma(XT[s][1 - par].ap[:, :, blk * TA:(blk + 1) * TA].rearrange("c p t -> p c t"), res, [resb], [XT[s][1 - par].b(blk)])
                        else:
                            for j in range(NCH):
                                (yt, ytb) = yt_r.next()
                                for hf in range(2):
                                    (pt, pb) = psf.next()
                                    mmh(pt, pb, [(pt[:, c * 128:(c + 1) * 128], [(res[:, hf * 4 + c, j * 128:(j + 1) * 128], ident_f[:])]) for c in range(4)], [resb, ident_fb])
                                    cp("act" if hf == 0 else "dve", yt[:, hf * 512:(hf + 1) * 512], pt[:], [pb], [ytb])
                                r0 = blk * TA + j * 128
                                dma(y_out[s].ap[r0:r0 + 128, :], yt, [ytb], [y_out[s].b(r0 // 128)])
                end_phase()
        S.emit(es)
    return nc


def host_consts():
    ident = np.eye(128, dtype=np.float32)
    p = np.arange(128)[:, None]
    f = np.arange(128)[None, :]
    m = np.zeros((128, 8, 128), np.float32)
    m[:, 0] = (f <= p)
    m[:, 1] = (f < p)
    m[:, 2] = (f >= p)
    m[:, 3] = (f > p)
    m[:, 4:8] = (m[:, 0:4] - 1.0) * 30000.0
    sel = np.zeros((8, 2), np.float32)
    sel[0:4, 0] = 1.0
    sel[4:8, 1] = 1.0
    return {"c_ident": ident, "c_mask": m, "c_sel": sel}


W_KEYS = ["w_in", "conv_w", "gdn_norm_w", "hgrn_norm_w", "w_branch_gdn", "w_branch_hgrn", "w_out", "norm_pre_mix",
          "norm_post_mix", "norm_pre_ffn", "norm_post_ffn", "w_ffn_gate", "w_ffn_up", "w_ffn_down", "hgrn_lb_logits"]


def make_in_map(inputs, xs, depth):
    m = {f"x{s}": np.ascontiguousarray(x, dtype=np.float32) for s, x in enumerate(xs)}
    for k in W_KEYS:
        m[k] = np.ascontiguousarray(np.asarray(inputs[k], dtype=np.float32)[:depth])
    m["gdn_a_log"] = np.ascontiguousarray(np.asarray(inputs["gdn_a_log"], dtype=np.float32)[:depth].reshape(depth, 8))
    m["gdn_dt_bias"] = np.ascontiguousarray(np.asarray(inputs["gdn_dt_bias"], dtype=np.float32)[:depth].reshape(depth, 8))
    m.update(host_consts())
    return m


_NC_CACHE = {}


def kernel(**inputs):
    xp = np.asarray(inputs["x_prompt"], dtype=np.float32)
    xs = np.asarray(inputs["x_sample"], dtype=np.float32)
    depth = int(np.asarray(inputs["w_in"]).shape[0])
    seq_lens = (xp.shape[1], xs.shape[1])
    key = (seq_lens, depth)
    if key not in _NC_CACHE:
        _NC_CACHE[key] = build(list(seq_lens), depth)
    nc = _NC_CACHE[key]
    in_maps = [make_in_map(inputs, [xp[c], xs[c]], depth) for c in range(NCORES)]
    res = run_bass_kernel_spmd(nc, in_maps, core_ids=list(range(NCORES)))
    yp = np.stack([np.asarray(res.results[c]["y0"], dtype=np.float32) for c in range(NCORES)], axis=0)
    ys = np.stack([np.asarray(res.results[c]["y1"], dtype=np.float32) for c in range(NCORES)], axis=0)
    return (yp, ys)
```

```python
import numpy as np
from contextlib import ExitStack
import concourse.bass as bass
import concourse.mybir as mybir
from concourse.ap import AP
from concourse.bass_utils import run_bass_kernel_spmd

F32 = mybir.dt.float32
BF16 = mybir.dt.bfloat16
AF = mybir.ActivationFunctionType
ALU = mybir.AluOpType
AX = mybir.AxisListType

D = 1024
NH = 4
DK = 128
DFF = 2816
NFF = DFF // 128
INW = 6672
EPS = 1e-6
O_Q, O_K, O_V, O_Z, O_A, O_B, O_QH, O_FH, O_IH, O_GH, O_GATE = 0, 512, 1024, 1536, 2048, 2056, 2064, 2576, 3600, 4112, 4624
TA = 256
NCORES = 8


class Buf:
    __slots__ = ("name", "last_w", "rd_eng", "rd_dma")

    def __init__(self, name=""):
        self.name = name
        self.last_w = None
        self.rd_eng = {}
        self.rd_dma = []


class Op:
    __slots__ = ("eng", "fn", "idx", "marked", "dma", "deps", "sem", "val")

    def __init__(self, eng, fn, dma):
        self.eng = eng
        self.fn = fn
        self.dma = dma
        self.marked = False
        self.deps = ()
        self.sem = None
        self.val = 0


class Sched:
    ENGS = ("pe", "act", "dve", "pool", "sp")
    GEN = 30000
    NDMA = 40

    def __init__(self, nc):
        self.nc = nc
        self.streams = {e: [] for e in self.ENGS}
        self.seen = {e: {} for e in self.ENGS}
        self.seen_dma = {e: set() for e in self.ENGS}
        self.ndma = 0
        self.dma_ops = []
        self.pending = {e: None for e in self.ENGS}

    def barrier(self):
        deps = []
        for e in ("pe", "act", "dve", "pool"):
            if self.streams[e]:
                deps.append(self.streams[e][-1])
        deps.extend(self.dma_ops[-self.NDMA:])
        for e in self.ENGS:
            self.pending[e] = list(deps) + (self.pending[e] or [])

    def add(self, eng, fn, reads=(), writes=(), dma=False):
        op = Op(eng, fn, dma)
        st = self.streams[eng]
        op.idx = len(st)
        st.append(op)
        deps = []
        if self.pending[eng] is not None:
            deps.extend(self.pending[eng])
            self.pending[eng] = None
        for b in reads:
            if b.last_w is not None:
                deps.append(b.last_w)
        for b in writes:
            if b.last_w is not None:
                deps.append(b.last_w)
            deps.extend(b.rd_eng.values())
            deps.extend(b.rd_dma)
        need = {}
        dma_need = []
        seen = self.seen[eng]
        sdma = self.seen_dma[eng]
        for d in deps:
            if d is op:
                continue
            if d.dma:
                if d not in sdma:
                    sdma.add(d)
                    dma_need.append(d)
            else:
                if d.eng == "pe" and eng == "pe" and not dma:
                    continue
                if seen.get(d.eng, -1) >= d.idx:
                    continue
                if need.get(d.eng, -1) < d.idx:
                    need[d.eng] = d.idx
        wl = []
        for e2, k in need.items():
            seen[e2] = k
            dop = self.streams[e2][k]
            dop.marked = True
            wl.append(dop)
        op.deps = wl + dma_need
        if dma:
            i = self.ndma
            self.ndma += 1
            self.dma_ops.append(op)
            if i >= self.NDMA:
                prev = self.dma_ops[i - self.NDMA]
                if prev not in sdma:
                    sdma.add(prev)
                    op.deps.append(prev)
        for b in reads:
            if dma:
                b.rd_dma.append(op)
            else:
                b.rd_eng[eng] = op
        for b in writes:
            b.last_w = op
            b.rd_eng = {}
            b.rd_dma = []
        return op

    def emit(self, es):
        nc = self.nc
        for e in self.ENGS:
            nmark = sum(1 for o in self.streams[e] if o.marked and not o.dma)
            ngen = max(1, -(-nmark // self.GEN))
            sems = [es.enter_context(nc.semaphore(f"s_{e}_{g}")) for g in range(ngen)]
            c = 0
            for o in self.streams[e]:
                if o.marked and not o.dma:
                    o.sem = sems[c // self.GEN]
                    o.val = c % self.GEN + 1
                    c += 1
        nd = min(self.NDMA, max(1, self.ndma))
        dsems = [es.enter_context(nc.semaphore(f"s_dma_{i}")) for i in range(nd)]
        final_dma = {}
        for i, o in enumerate(self.dma_ops):
            o.sem = dsems[i % self.NDMA]
            o.val = 16 * (i // self.NDMA + 1)
            final_dma[i % self.NDMA] = o.val

        def run(e, h, last=False):
            for o in self.streams[e]:
                for d in o.deps:
                    h.wait_ge(d.sem, d.val)
                ins = o.fn(h)
                if o.dma:
                    ins.then_inc(o.sem, 16)
                elif o.marked:
                    ins.then_inc(o.sem, 1)
            if last:
                for i, v in final_dma.items():
                    h.wait_ge(dsems[i], v)

        block = es.enter_context(nc.Block())

        @block.tensor
        def _(h):
            run("pe", h)

        @block.scalar
        def _(h):
            run("act", h)

        @block.vector
        def _(h):
            run("dve", h)

        @block.gpsimd
        def _(h):
            run("pool", h)

        @block.sync
        def _(h):
            run("sp", h, last=True)


class Ring:
    def __init__(self, tiles):
        self.tiles = tiles
        self.i = 0

    def next(self):
        t = self.tiles[self.i % len(self.tiles)]
        self.i += 1
        return t


class DT:
    def __init__(self, ap):
        self.ap = ap
        self.bufs = {}

    def b(self, key):
        r = self.bufs.get(key)
        if r is None:
            r = self.bufs[key] = Buf()
        return r


FA_N = 15360
HA_N = 22528
WAR_N = 22528


def build(seq_lens, depth, dbg=(), phases=None):
    nc = bass.Bass("TRN2", target_bir_lowering=False)
    nseq = len(seq_lens)
    XH = TA + 4
    NCH = TA // 128
    NC64 = TA // 64

    def on(p):
        return phases is None or p in phases

    def din(name, shape, dt=F32):
        return nc.dram_tensor(name, list(shape), dt, kind="ExternalInput").ap()

    def dscr(name, shape, dt=F32):
        kind = "ExternalOutput" if name in dbg else "Internal"
        return DT(nc.dram_tensor(name, list(shape), dt, kind=kind).ap())

    x_in = [din(f"x{s}", [seq_lens[s], D]) for s in range(nseq)]
    y_out = [DT(nc.dram_tensor(f"y{s}", [seq_lens[s], D], F32, kind="ExternalOutput").ap()) for s in range(nseq)]
    w_in = din("w_in", [depth, D, INW])
    conv_w = din("conv_w", [depth, 5, 1536])
    a_log = din("gdn_a_log", [depth, 8])
    dt_bias = din("gdn_dt_bias", [depth, 8])
    gdn_nw = din("gdn_norm_w", [depth, 128])
    lb_logits = din("hgrn_lb_logits", [depth, 2, 512])
    hg_nw = din("hgrn_norm_w", [depth, 128])
    w_bg = din("w_branch_gdn", [depth, 512, D])
    w_bh = din("w_branch_hgrn", [depth, 512, D])
    w_o = din("w_out", [depth, D, D])
    n_pre_mix = din("norm_pre_mix", [depth, D])
    n_post_mix = din("norm_post_mix", [depth, D])
    n_pre_ffn = din("norm_pre_ffn", [depth, D])
    n_post_ffn = din("norm_post_ffn", [depth, D])
    w_g = din("w_ffn_gate", [depth, D, DFF])
    w_u = din("w_ffn_up", [depth, D, DFF])
    w_d = din("w_ffn_down", [depth, DFF, D])
    c_ident = din("c_ident", [128, 128])
    c_mask = din("c_mask", [128, 8, 128])
    c_sel = din("c_sel", [8, 2])

    SL = list(enumerate(seq_lens))
    XT = [[dscr(f"XT{p}_{s}", [8, 128, L]) for p in range(2)] for s, L in SL]
    XM = [dscr(f"XM_{s}", [8, 128, L]) for s, L in SL]
    HN = [dscr(f"HN_{s}", [L // TA, 128, 8 * TA], BF16) for s, L in SL]
    H2 = [dscr(f"H2_{s}", [L // TA, 128, 8 * TA], BF16) for s, L in SL]
    GQK = [dscr(f"GQK_{s}", [L // TA, 128, 8 * TA], BF16) for s, L in SL]
    GKV = [dscr(f"GKV_{s}", [L, 1024], BF16) for s, L in SL]
    GS = [dscr(f"GS_{s}", [3, 8, L]) for s, L in SL]
    GT = [dscr(f"GT_{s}", [L, 24]) for s, L in SL]
    ZT = [dscr(f"ZT_{s}", [L // TA, 128, 4 * TA], BF16) for s, L in SL]
    HQd = [dscr(f"HQ_{s}", [2, L // TA, 128, 4 * TA], BF16) for s, L in SL]
    HKd = [dscr(f"HK_{s}", [2, L // TA, 128, 4 * TA], BF16) for s, L in SL]
    HAB = [dscr(f"HAB_{s}", [L // TA, 128, 64]) for s, L in SL]
    HT = [dscr(f"HT_{s}", [L, 1536], BF16) for s, L in SL]
    GHT = [dscr(f"GHT_{s}", [L // TA, 128, 4 * TA], BF16) for s, L in SL]
    MG = [dscr(f"MG_{s}", [L // TA, 128, 16 * TA], BF16) for s, L in SL]
    OA = [dscr(f"OA_{s}", [2, L // TA, 128, 4 * TA], BF16) for s, L in SL]
    OB = [dscr(f"OB_{s}", [2, L // TA, 128, 4 * TA], BF16) for s, L in SL]
    ACTS = [dscr(f"ACT_{s}", [L // TA, 128, 22 * TA], BF16) for s, L in SL]

    with ExitStack() as es:
        S = Sched(nc)
        cnt = [0]

        def sb(shape, dt=F32, name=None):
            cnt[0] += 1
            nm = name or f"t{cnt[0]}"
            t = es.enter_context(nc.sbuf_tensor(nm, list(shape), dt))
            return t, Buf(nm)

        def pst(shape, dt=F32, name=None):
            t = es.enter_context(nc.psum_tensor(name, list(shape), dt))
            return t, Buf(name)

        farena = es.enter_context(nc.sbuf_tensor("farena", [128, FA_N], F32))
        harena = es.enter_context(nc.sbuf_tensor("harena", [128, HA_N], BF16))
        off = {"f": 0, "h": 0, "w0": 0, "w1": 0}

        def carve(shape, dt=F32, where=None):
            k = where or ("f" if dt == F32 else "h")
            ar = {"f": farena, "h": harena, "w0": warena[0][0] if warena else None, "w1": warena[1][0] if warena else None}[k]
            cap = {"f": FA_N, "h": HA_N, "w0": WAR_N, "w1": WAR_N}[k]
            n = 1
            for d_ in shape[1:]:
                n *= d_
            o = off[k]
            if k in ("w0", "w1") and dt == F32:
                off[k] = o + 2 * n
                assert off[k] <= cap, (k, off[k])
                v = ar[0:shape[0], o:o + 2 * n].bitcast(F32)
            else:
                off[k] = o + n
                assert off[k] <= cap, (k, off[k])
                v = ar[0:shape[0], o:o + n]
            if len(shape) == 3:
                v = v.rearrange("p (a b) -> p a b", a=shape[1])
            elif len(shape) == 4:
                v = v.rearrange("p (a b c) -> p a b c", a=shape[1], b=shape[2])
            return v, Buf()

        def ring(n, shape, dt=F32, where=None):
            return Ring([carve(shape, dt, where) for _ in range(n)])

        def end_phase():
            S.barrier()
            for k_ in off:
                off[k_] = 0

        def run_chains(gens, window=None):
            gens = list(gens)
            active = []
            while gens or active:
                while gens and (window is None or len(active) < window):
                    active.append(gens.pop(0))
                for g in list(active):
                    try:
                        next(g)
                    except StopIteration:
                        active.remove(g)

        def pipeline(items, load_fn, compute_fn):
            nxt = load_fn(items[0])
            for i, it in enumerate(items):
                cur = nxt
                nxt = load_fn(items[i + 1]) if i + 1 < len(items) else None
                compute_fn(it, cur)

        warena = []

        def dma(out, in_, reads, writes, nonc=False):
            if nonc:
                S.add("sp", lambda h: h.dma_start(out=out, in_=in_, allow_slow_non_contiguous=True), reads, writes, dma=True)
            else:
                S.add("sp", lambda h: h.dma_start(out=out, in_=in_), reads, writes, dma=True)

        def mm(out, wbuf, pairs, reads):
            n = len(pairs)

            def fn(h):
                ins = None
                for i, (l, r) in enumerate(pairs):
                    ins = h.matmul(out, lhsT=l, rhs=r, start=(i == 0), stop=(i == n - 1))
                return ins
            S.add("pe", fn, reads, [wbuf])

        def mmh(pt, pb, items, reads):
            def fn(h):
                ins = None
                for (o, prs) in items:
                    n = len(prs)
                    for i, (l, r) in enumerate(prs):
                        ins = h.matmul(o, lhsT=l, rhs=r, start=(i == 0), stop=(i == n - 1))
                return ins
            S.add("pe", fn, reads, [pb])

        def act(out, in_, func, reads, writes, **kw):
            S.add("act", lambda h: h.activation(out=out, in_=in_, func=func, **kw), reads, writes)

        def ts(eng, out, in0, s1, s2, op0, op1, reads, writes):
            if s2 is None:
                S.add(eng, lambda h: h.tensor_scalar(out=out, in0=in0, scalar1=s1, scalar2=None, op0=op0), reads, writes)
            else:
                S.add(eng, lambda h: h.tensor_scalar(out=out, in0=in0, scalar1=s1, scalar2=s2, op0=op0, op1=op1), reads, writes)

        def stt(out, in0, sc, in1, op0, op1, reads, writes):
            S.add("dve", lambda h: h.scalar_tensor_tensor(out=out, in0=in0, scalar=sc, in1=in1, op0=op0, op1=op1), reads, writes)

        def tt(eng, out, in0, in1, op, reads, writes):
            S.add(eng, lambda h: h.tensor_tensor(out=out, in0=in0, in1=in1, op=op), reads, writes)

        def cp(eng, out, in_, reads, writes):
            if eng == "act":
                S.add("act", lambda h: h.activation(out=out, in_=in_, func=AF.Copy), reads, writes)
            else:
                S.add(eng, lambda h: h.tensor_copy(out=out, in_=in_), reads, writes)

        def mset(eng, ap, val, writes):
            S.add(eng, lambda h: h.memset(ap, val), [], writes)

        def bc(ap2, n):
            return ap2.unsqueeze(2).to_broadcast([ap2.shape[0], ap2.shape[1], n])

        def bch(ap2, n):
            return ap2.unsqueeze(1).to_broadcast([ap2.shape[0], n, ap2.shape[1]])

        ident_f, ident_fb = sb([128, 128], F32, "ident_f")
        ident_h, ident_hb = sb([128, 128], BF16, "ident_h")
        ones_h, ones_hb = sb([128, 128], BF16, "ones_h")
        mask_f, mask_fb = sb([128, 8, 128], F32, "mask_f")
        sel, selb = sb([8, 2], F32, "sel")
        eps_t, epsb = sb([128, 2], F32, "eps_t")
        dma(ident_f[:], c_ident[:, :], [], [ident_fb])
        dma(mask_f[:], c_mask[:, :, :], [], [mask_fb])
        dma(sel[:], c_sel[:, :], [], [selb])
        cp("dve", ident_h[:], ident_f[:], [ident_fb], [ident_hb])
        mset("dve", ones_h[:], 1.0, [ones_hb])
        mset("dve", eps_t[:, 0:1], float(EPS), [epsb])
        mset("dve", eps_t[:, 1:2], 1.0, [epsb])
        m128, m128b = sb([8, TA], F32, "m128")
        m64, m64b = sb([128, TA], F32, "m64")
        mset("pool", m128[:], 1.0, [m128b])
        mset("pool", m128[:].rearrange("p (c k) -> p c k", k=128)[:, :, 0:1], 0.0, [m128b])
        mset("pool", m64[:], 1.0, [m64b])
        mset("pool", m64[:].rearrange("p (c k) -> p c k", k=64)[:, :, 0:1], 0.0, [m64b])

        psf = Ring([pst([128, 512], F32, f"psf{i}") for i in range(6)])
        psh = Ring([pst([128, 1024], BF16, f"psh{i}") for i in range(2)])

        warena.extend([sb([128, WAR_N], BF16, f"warena{i}") for i in range(2)])
        wcnt = [0]
        wstage = Ring([sb([128, 520], F32, f"wstage{i}") for i in range(2)])
        cast_rr = [0]

        def next_arena():
            a = warena[wcnt[0] % 2]
            wcnt[0] += 1
            return a

        def load_w(dst, dstb, src, ncols):
            c0 = 0
            while c0 < ncols:
                w = min(520, ncols - c0)
                (st, stb) = wstage.next()
                dma(st[:, 0:w], src[:, c0:c0 + w], [], [stb])
                eng = ("dve", "pool")[cast_rr[0] % 2]
                cast_rr[0] += 1
                cp(eng, dst[:, c0:c0 + w], st[:, 0:w], [stb], [dstb])
                c0 += w

        def vec_cols(src_1d, n, name):
            t, tb = sb([128, n], F32, name)
            dma(t[:], src_1d.rearrange("(c p) -> p c", p=128), [], [tb], nonc=True)
            return t, tb

        def x_load(xsrc, L, blk, halo, xT_r):
            t0 = blk * TA
            nb = L // TA
            (xt, xb) = xT_r.next()
            lo, hi = (2, 2) if halo else (0, 0)
            W = TA + lo + hi
            a0 = t0 - lo
            a1 = t0 + TA + hi
            o0 = 0
            if a0 < 0:
                mset("pool", xt[:, :, 0:lo], 0.0, [xb])
                o0 = lo
                a0 = 0
            if a1 > L:
                mset("pool", xt[:, :, W - hi:W], 0.0, [xb])
                a1 = L
            rb = [xsrc.b(k) for k in range(max(0, blk - 1), min(nb, blk + 2))] if halo else [xsrc.b(blk)]
            dma(xt[:, :, o0:o0 + (a1 - a0)], xsrc.ap[:, :, a0:a1].rearrange("c p t -> p c t"), rb, [xb])
            return xt, xb

        def norm_compute(xt, xb, gain, gainb, halo, sq_r, hT_r, rstd_r, tmp_r):
            W = TA + (4 if halo else 0)
            (sq, sqb) = sq_r.next()
            act(sq[:, :, 0:W], xt[:, :, 0:W], AF.Square, [xb], [sqb])
            (pt, pb) = psf.next()
            mm(pt[:, 0:W], pb, [(ones_h[:], sq[:, c, 0:W]) for c in range(8)], [ones_hb, sqb])
            (rs, rsb) = rstd_r.next()
            (tmp, tmpb) = tmp_r.next()
            act(tmp[:, 0:W], pt[:, 0:W], AF.Sqrt, [pb, epsb], [tmpb], bias=eps_t[:, 0:1], scale=1.0 / D)
            S.add("dve", lambda h: h.reciprocal(out=rs[:, 0:W], in_=tmp[:, 0:W]), [tmpb], [rsb])
            (ht, hb) = hT_r.next()
            for c in range(8):
                stt(ht[:, c, 0:W], xt[:, c, 0:W], gain[:, c:c + 1], rs[:, 0:W], ALU.mult, ALU.mult, [xb, gainb, rsb], [hb])
            return ht, hb

        def post_norm_add(res_r, xt, xb, val, valb, sq8, sq8b, gain, gainb, rstd_r, tmp_r, f_r):
            (pt, pb) = psf.next()
            mm(pt[:, 0:TA], pb, [(ones_h[:], sq8[:, c, :]) for c in range(8)], [ones_hb, sq8b])
            (rs, rsb) = rstd_r.next()
            (tmp, tmpb) = tmp_r.next()
            act(tmp[:, 0:TA], pt[:, 0:TA], AF.Sqrt, [pb, epsb], [tmpb], bias=eps_t[:, 0:1], scale=1.0 / D)
            S.add("dve", lambda h: h.reciprocal(out=rs[:, 0:TA], in_=tmp[:, 0:TA]), [tmpb], [rsb])
            (res, resb) = res_r.next()
            for c in range(8):
                (t1, t1b) = f_r.next()
                stt(t1[:, 0:TA], val[:, c, :], gain[:, c:c + 1], rs[:, 0:TA], ALU.mult, ALU.mult, [valb, gainb, rsb], [t1b])
                tt("pool", res[:, c, :], xt[:, c, 0:TA], t1[:, 0:TA], ALU.add, [xb, t1b], [resb])
            return res, resb

        lg, lgb = sb([128, depth, 8], F32, "lb_lg")
        lbt, lbb = sb([128, depth, 8], F32, "lb")
        omlt, omlb = sb([128, depth, 8], F32, "oml")
        for l in range(depth):
            for d_ in range(2):
                dma(lg[:, l, d_ * 4:(d_ + 1) * 4], lb_logits[l, d_].rearrange("(h p) -> p h", p=128), [], [lgb], nonc=True)
        mx, mxb = sb([128, 8], F32, "lb_mx")
        cp("dve", mx[:], lg[:, 0, :], [lgb], [mxb])
        for l in range(1, depth):
            tt("dve", mx[:], mx[:], lg[:, l, :], ALU.max, [mxb, lgb], [mxb])
        for l in range(depth):
            tt("dve", lg[:, l, :], lg[:, l, :], mx[:], ALU.subtract, [lgb, mxb], [lgb])
        act(lg[:], lg[:], AF.Exp, [lgb], [lgb])
        cp("dve", mx[:], lg[:, 0, :], [lgb], [mxb])
        for l in range(1, depth):
            tt("dve", mx[:], mx[:], lg[:, l, :], ALU.add, [mxb, lgb], [mxb])
        S.add("dve", lambda h: h.reciprocal(out=mx[:], in_=mx[:]), [mxb], [mxb])
        mset("dve", lbt[:, 0, :], 0.0, [lbb])
        for l in range(1, depth):
            tt("dve", lg[:, l, :], lg[:, l, :], mx[:], ALU.mult, [lgb, mxb], [lgb])
            tt("dve", lbt[:, l, :], lbt[:, l - 1, :], lg[:, l, :], ALU.add, [lbb, lgb], [lbb])
        ts("dve", omlt[:], lbt[:], -1.0, 1.0, ALU.mult, ALU.add, [lbb], [omlb])

        if on("P0"):
            xin_r = ring(2, [128, D], F32)
            xo_r = ring(2, [128, 8, 128], F32)
            for s, L in SL:
                for j in range(L // 128):
                    (xt, xb) = xin_r.next()
                    dma(xt, x_in[s][j * 128:(j + 1) * 128, :], [], [xb])
                    (xo, xob) = xo_r.next()
                    for half in range(2):
                        (pt, pb) = psf.next()
                        mmh(pt, pb, [(pt[:, c * 128:(c + 1) * 128], [(xt[:, (half * 4 + c) * 128:(half * 4 + c + 1) * 128], ident_f[:])]) for c in range(4)], [xb, ident_fb])
                        cp("act" if half == 0 else "dve", xo[:, half * 4:(half + 1) * 4, :], pt[:].rearrange("p (c t) -> p c t", c=4), [pb], [xob])
                    dma(XT[s][0].ap[:, :, j * 128:(j + 1) * 128].rearrange("c p t -> p c t"), xo, [xob], [XT[s][0].b(j * 128 // TA)])
            end_phase()

        for l in range(depth):
            par = l % 2
            g_pm, g_pmb = vec_cols(n_pre_mix[l], 8, f"g_pm{l}")
            g_po, g_pob = vec_cols(n_post_mix[l], 8, f"g_po{l}")
            g_pf, g_pfb = vec_cols(n_pre_ffn[l], 8, f"g_pf{l}")
            g_qf, g_qfb = vec_cols(n_post_ffn[l], 8, f"g_qf{l}")
            nwa, nwab = sb([128, 1], F32, f"nwa{l}")
            nwh, nwhb = sb([128, 1], F32, f"nwh{l}")
            dma(nwa[:], gdn_nw[l].rearrange("(p o) -> p o", o=1), [], [nwab], nonc=True)
            dma(nwh[:], hg_nw[l].rearrange("(p o) -> p o", o=1), [], [nwhb], nonc=True)
            cw, cwb = sb([128, 12, 5], F32, f"cw{l}")
            for j in range(5):
                dma(cw[:, :, j], conv_w[l, j].rearrange("(c p) -> p c", p=128), [], [cwb], nonc=True)
            alog, alogb = sb([8, 1], F32, f"alog{l}")
            dtb, dtbb = sb([8, 1], F32, f"dtb{l}")
            dma(alog[:], a_log[l].rearrange("(p o) -> p o", o=1), [], [alogb], nonc=True)
            dma(dtb[:], dt_bias[l].rearrange("(p o) -> p o", o=1), [], [dtbb], nonc=True)
            negA, negAb = sb([8, 1], F32, f"negA{l}")
            act(negA[:], alog[:], AF.Exp, [alogb], [negAb])
            ts("dve", negA[:], negA[:], -1.0, None, ALU.mult, None, [negAb], [negAb])

            if on("A1"):
                (war, wb) = next_arena()
                wA1 = war[:, 0:8 * 2064].rearrange("p (k c) -> p k c", k=8)
                for kc in range(8):
                    load_w(wA1[:, kc, :], wb, w_in[l, kc * 128:(kc + 1) * 128, 0:2064], 2064)
                xT_r = ring(2, [128, 8, XH])
                sq_r = ring(1, [128, 8, XH], BF16)
                hT_r = ring(2, [128, 8, XH], BF16)
                rstd_r = ring(2, [128, XH])
                tmp_r = ring(2, [128, XH])
                cvin_r = ring(4, [128, XH])
                acc_r = ring(4, [128, TA])
                sil_r = ring(4, [128, TA])
                sqb_r = ring(4, [128, TA], BF16)
                rn_r = ring(4, [128, TA])
                rt_r = ring(4, [128, TA])
                qk_r = ring(2, [128, 8, TA], BF16)
                v_r = ring(2, [128, 4, TA], BF16)
                kvt_r = ring(2, [128, 1024], BF16)
                z_r = ring(2, [128, 4, TA], BF16)
                abt = {k: carve([8, TA]) for k in ("e", "sp", "g", "pfx", "tmp", "sfx")}
                gs_r = ring(2, [8, 3, TA])
                gt_r = ring(2, [128, 24])

                def a1_chunk(ci, ht, hb, qk, qkb, vs, vsb):
                    (pt, pb) = psf.next()
                    mm(pt[:, 0:XH], pb, [(wA1[:, kc, ci * 128:(ci + 1) * 128], ht[:, kc, :]) for kc in range(8)], [wb, hb])
                    (cv, cvb) = cvin_r.next()
                    cp("act", cv, pt[:, 0:XH], [pb], [cvb])
                    yield
                    (ac, acb) = acc_r.next()
                    ts("dve", ac, cv[:, 0:TA], cw[:, ci, 0:1], None, ALU.mult, None, [cvb, cwb], [acb])
                    for j in range(1, 5):
                        stt(ac, cv[:, j:j + TA], cw[:, ci, j:j + 1], ac, ALU.mult, ALU.add, [cvb, cwb, acb], [acb])
                    yield
                    if ci < 8:
                        (sl, slb) = sil_r.next()
                        act(sl, ac, AF.Silu, [acb], [slb])
                        (sq2, sq2b) = sqb_r.next()
                        act(sq2, sl, AF.Square, [slb], [sq2b])
                        yield
                        (p2, p2b) = psf.next()
                        mm(p2[:, 0:TA], p2b, [(ones_h[:], sq2)], [ones_hb, sq2b])
                        yield
                        (rt, rtb) = rt_r.next()
                        act(rt, p2[:, 0:TA], AF.Sqrt, [p2b, epsb], [rtb], bias=eps_t[:, 0:1], scale=1.0)
                        yield
                        (rn, rnb) = rn_r.next()
                        S.add("dve", lambda h, rn=rn, rt=rt: h.reciprocal(out=rn, in_=rt), [rtb], [rnb])
                        scl = float(DK ** -0.5) if ci < 4 else 1.0
                        stt(qk[:, ci, :], sl, scl, rn, ALU.mult, ALU.mult, [slb, rnb], [qkb])
                    else:
                        act(vs[:, ci - 8, :], ac, AF.Silu, [acb], [vsb])

                def a1_load(it):
                    s, L, blk = it
                    return x_load(XT[s][par], L, blk, True, xT_r)

                def a1_compute(it, cur):
                    s, L, blk = it
                    t0 = blk * TA
                    (xt, xb) = cur
                    ht, hb = norm_compute(xt, xb, g_pm, g_pmb, True, sq_r, hT_r, rstd_r, tmp_r)
                    dma(HN[s].ap[blk].rearrange("p (c t) -> p c t", c=8), ht[:, :, 2:2 + TA], [hb], [HN[s].b(blk)])
                    (qk, qkb) = qk_r.next()
                    (vs, vsb) = v_r.next()
                    run_chains([a1_chunk(ci, ht, hb, qk, qkb, vs, vsb) for ci in range(12)], window=3)
                    dma(GQK[s].ap[blk].rearrange("p (c t) -> p c t", c=8), qk, [qkb], [GQK[s].b(blk)])
                    for j in range(NCH):
                        (ph, phb) = psh.next()

                        def fn(h, ph=ph, qk=qk, vs=vs, j=j):
                            ins = None
                            for hh in range(4):
                                ins = h.transpose(out=ph[:, hh * 128:(hh + 1) * 128], in_=qk[:, 4 + hh, j * 128:(j + 1) * 128], identity=ident_h[:])
                            for hh in range(4):
                                ins = h.transpose(out=ph[:, 512 + hh * 128:512 + (hh + 1) * 128], in_=vs[:, hh, j * 128:(j + 1) * 128], identity=ident_h[:])
                            return ins
                        S.add("pe", fn, [qkb, vsb, ident_hb], [phb])
                        (kv, kvb) = kvt_r.next()
                        cp("act", kv, ph[:], [phb], [kvb])
                        r0 = t0 + j * 128
                        dma(GKV[s].ap[r0:r0 + 128, :], kv, [kvb], [GKV[s].b(r0 // 128)])
                    (zs, zsb) = z_r.next()
                    for hh in range(4):
                        (pt, pb) = psf.next()
                        mm(pt[:, 0:TA], pb, [(wA1[:, kc, O_Z + hh * 128:O_Z + (hh + 1) * 128], ht[:, kc, 2:2 + TA]) for kc in range(8)], [wb, hb])
                        act(zs[:, hh, :], pt[:, 0:TA], AF.Silu, [pb], [zsb])
                    dma(ZT[s].ap[blk].rearrange("p (c t) -> p c t", c=4), zs, [zsb], [ZT[s].b(blk)])
                    (pa, pab) = psf.next()
                    mm(pa[0:8, 0:TA], pab, [(wA1[:, kc, O_A:O_A + 8], ht[:, kc, 2:2 + TA]) for kc in range(8)], [wb, hb])
                    (pbb, pbbb) = psf.next()
                    mm(pbb[0:8, 0:TA], pbbb, [(wA1[:, kc, O_B:O_B + 8], ht[:, kc, 2:2 + TA]) for kc in range(8)], [wb, hb])
                    e_t, e_b = abt["e"]
                    act(e_t, pa[0:8, 0:TA], AF.Exp, [pab, dtbb], [e_b], bias=dtb[:])
                    sp_t, sp_b = abt["sp"]
                    act(sp_t, e_t, AF.Ln, [e_b, epsb], [sp_b], bias=eps_t[0:8, 1:2])
                    g_t, g_b = abt["g"]
                    ts("dve", g_t, sp_t, negA[:, 0:1], None, ALU.mult, None, [sp_b, negAb], [g_b])
                    pf_t, pf_b = abt["pfx"]
                    S.add("dve", lambda h, pf_t=pf_t, g_t=g_t: h.tensor_tensor_scan(out=pf_t, data0=m128[:], data1=g_t, initial=0.0, op0=ALU.mult, op1=ALU.add), [m128b, g_b], [pf_b])
                    tm_t, tm_b = abt["tmp"]
                    pf3 = pf_t.rearrange("p (c k) -> p c k", k=128)
                    tot_bc = pf3[:, :, 127:128].to_broadcast([8, NCH, 128])
                    tt("dve", tm_t.rearrange("p (c k) -> p c k", k=128), tot_bc, pf3, ALU.subtract, [pf_b], [tm_b])
                    sf_t, sf_b = abt["sfx"]
                    tt("dve", sf_t, tm_t, g_t, ALU.add, [tm_b, g_b], [sf_b])
                    (gs, gsb) = gs_r.next()
                    ts("dve", gs[:, 0, :], pf_t, sel[:, 0:1], None, ALU.mult, None, [pf_b, selb], [gsb])
                    stt(gs[:, 0, :], sf_t, sel[:, 1:2], gs[:, 0, :], ALU.mult, ALU.add, [sf_b, selb, gsb], [gsb])
                    act(gs[:, 1, :], pbb[0:8, 0:TA], AF.Sigmoid, [pbbb], [gsb])
                    tt("dve", gs[:, 2, :].rearrange("p (c k) -> p c k", k=128), tot_bc, gs[:, 0, :].rearrange("p (c k) -> p c k", k=128), ALU.subtract, [pf_b, gsb], [gsb])
                    dma(GS[s].ap[:, :, t0:t0 + TA].rearrange("q r t -> r q t"), gs, [gsb], [GS[s].b(blk)])
                    for j in range(NCH):
                        (pt, pb) = psf.next()
                        mmh(pt, pb, [(pt[:, q * 8:(q + 1) * 8], [(gs[:, q, j * 128:(j + 1) * 128], ident_f[0:8, 0:8])]) for q in range(3)], [gsb, ident_fb])
                        (gt, gtb) = gt_r.next()
                        cp("act", gt, pt[:, 0:24], [pb], [gtb])
                        r0 = t0 + j * 128
                        dma(GT[s].ap[r0:r0 + 128, :], gt, [gtb], [GT[s].b(r0 // 128)])

                pipeline([(s, L, blk) for s, L in SL for blk in range(L // TA)], a1_load, a1_compute)
                end_phase()

            if on("A2a"):
                (war, wb) = next_arena()
                wA = war[:, 0:8 * 2048].rearrange("p (k c) -> p k c", k=8)
                for kc in range(8):
                    load_w(wA[:, kc, :], wb, w_in[l, kc * 128:(kc + 1) * 128, O_QH:O_QH + 2048], 2048)
                CQ, CF, CI = 0, 512, 1536
                hT_r = ring(2, [128, 8, TA], BF16)
                qs_r = ring(2, [128, 4, TA])
                ih_r = ring(2, [128, 4, TA], BF16)
                hq_r = [ring(2, [128, 4, TA], BF16) for _ in range(2)]
                hk_r = [ring(2, [128, 4, TA], BF16) for _ in range(2)]
                ab_r = ring(2, [128, 64])
                f_r = ring(44, [128, TA])
                tok_r = ring(2, [128, 1536], BF16)
                scl = float(DK ** -0.5)

                def a2_chain(d_, hh, ht, hb, qs, qsb, hq, hqb, hk, hkb, ab5, abb):
                    dh = d_ * 4 + hh
                    (pt, pb) = psf.next()
                    c0 = CF + d_ * 512 + hh * 128
                    mm(pt[:, 0:TA], pb, [(wA[:, kc, c0:c0 + 128], ht[:, kc, :]) for kc in range(8)], [wb, hb])
                    (sg, sgb) = f_r.next()
                    act(sg, pt[:, 0:TA], AF.Sigmoid, [pb], [sgb])
                    yield
                    (ff, ffb) = f_r.next()
                    ts("dve", ff, sg, omlt[:, l, dh:dh + 1], lbt[:, l, dh:dh + 1], ALU.mult, ALU.add, [sgb, omlb, lbb], [ffb])
                    yield
                    (lf, lfb) = f_r.next()
                    act(lf, ff, AF.Ln, [ffb], [lfb])
                    (kk, kkb) = f_r.next()
                    ts("pool", kk, ff, -1.0, 1.0, ALU.mult, ALU.add, [ffb], [kkb])
                    yield
                    (pf, pfb) = f_r.next()
                    S.add("dve", lambda h, pf=pf, lf=lf: h.tensor_tensor_scan(out=pf, data0=m64[:], data1=lf, initial=0.0, op0=ALU.mult, op1=ALU.add), [m64b, lfb], [pfb])
                    yield
                    if d_ == 0:
                        cum, cumb = pf, pfb
                        ri, ai = 31, 63
                    else:
                        pf3 = pf.rearrange("p (c k) -> p c k", k=64)
                        (tm, tmb) = f_r.next()
                        tt("pool", tm.rearrange("p (c k) -> p c k", k=64), pf3[:, :, 63:64].to_broadcast([128, NC64, 64]), pf3, ALU.subtract, [pfb], [tmb])
                        yield
                        (cum, cumb) = f_r.next()
                        tt("pool", cum, tm, lf, ALU.add, [tmb, lfb], [cumb])
                        yield
                        ri, ai = 32, 0
                    cum3 = cum.rearrange("p (c k) -> p c k", k=64)
                    (dd, ddb) = f_r.next()
                    tt("pool", dd.rearrange("p (c k) -> p c k", k=64), cum3, cum3[:, :, ri:ri + 1].to_broadcast([128, NC64, 64]), ALU.subtract, [cumb], [ddb])
                    yield
                    (eq, eqb) = f_r.next()
                    act(eq, dd, AF.Exp, [ddb], [eqb])
                    (ek, ekb) = f_r.next()
                    act(ek, dd, AF.Exp, [ddb], [ekb], scale=-1.0)
                    act(ab5[:, d_, 1, hh, :], cum3[:, :, ri], AF.Exp, [cumb], [abb])
                    yield
                    stt(hq[:, hh, :], qs[:, hh, :], scl, eq, ALU.mult, ALU.mult, [qsb, eqb], [hqb])
                    tt("pool", hk[:, hh, :], kk, ek, ALU.mult, [kkb, ekb], [hkb])
                    cp("pool", ab5[:, d_, 0, hh, :], eq.rearrange("p (c k) -> p c k", k=64)[:, :, ai], [eqb], [abb])

                def a2_load(it):
                    s, L, blk = it
                    (ht, hb) = hT_r.next()
                    dma(ht, HN[s].ap[blk].rearrange("p (c t) -> p c t", c=8), [HN[s].b(blk)], [hb])
                    return ht, hb

                def a2_compute(it, cur):
                    s, L, blk = it
                    t0 = blk * TA
                    (ht, hb) = cur
                    (qs, qsb) = qs_r.next()
                    (ih, ihb) = ih_r.next()
                    for hh in range(4):
                        (pt, pb) = psf.next()
                        mm(pt[:, 0:TA], pb, [(wA[:, kc, CQ + hh * 128:CQ + (hh + 1) * 128], ht[:, kc, :]) for kc in range(8)], [wb, hb])
                        act(qs[:, hh, :], pt[:, 0:TA], AF.Silu, [pb], [qsb])
                        (pt, pb) = psf.next()
                        mm(pt[:, 0:TA], pb, [(wA[:, kc, CI + hh * 128:CI + (hh + 1) * 128], ht[:, kc, :]) for kc in range(8)], [wb, hb])
                        cp("act", ih[:, hh, :], pt[:, 0:TA], [pb], [ihb])
                    (ab, abb) = ab_r.next()
                    ab5 = ab.rearrange("p (d q h c) -> p d q h c", d=2, q=2, h=4)
                    hqs = [hq_r[d_].next() for d_ in range(2)]
                    hks = [hk_r[d_].next() for d_ in range(2)]
                    run_chains([a2_chain(d_, hh, ht, hb, qs, qsb, hqs[d_][0], hqs[d_][1], hks[d_][0], hks[d_][1], ab5, abb) for d_ in range(2) for hh in range(4)], window=3)
                    for d_ in range(2):
                        dma(HQd[s].ap[d_, blk].rearrange("p (c t) -> p c t", c=4), hqs[d_][0], [hqs[d_][1]], [HQd[s].b((d_, blk))])
                        dma(HKd[s].ap[d_, blk].rearrange("p (c t) -> p c t", c=4), hks[d_][0], [hks[d_][1]], [HKd[s].b((d_, blk))])
                    dma(HAB[s].ap[blk], ab, [abb], [HAB[s].b(blk)])
                    for j in range(NCH):
                        (ph, phb) = psh.next()
                        (ph2, ph2b) = psh.next()

                        def fn(h, ph=ph, hks=hks, j=j):
                            ins = None
                            for d_ in range(2):
                                for hh in range(4):
                                    o = (d_ * 4 + hh) * 128
                                    ins = h.transpose(out=ph[:, o:o + 128], in_=hks[d_][0][:, hh, j * 128:(j + 1) * 128], identity=ident_h[:])
                            return ins
                        S.add("pe", fn, [hks[0][1], hks[1][1], ident_hb], [phb])

                        def fn2(h, ph2=ph2, ih=ih, j=j):
                            ins = None
                            for hh in range(4):
                                ins = h.transpose(out=ph2[:, hh * 128:(hh + 1) * 128], in_=ih[:, hh, j * 128:(j + 1) * 128], identity=ident_h[:])
                            return ins
                        S.add("pe", fn2, [ihb, ident_hb], [ph2b])
                        (tk, tkb) = tok_r.next()
                        cp("act", tk[:, 0:1024], ph[:], [phb], [tkb])
                        cp("dve", tk[:, 1024:1536], ph2[:, 0:512], [ph2b], [tkb])
                        r0 = t0 + j * 128
                        dma(HT[s].ap[r0:r0 + 128, :], tk, [tkb], [HT[s].b(r0 // 128)])

                pipeline([(s, L, blk) for s, L in SL for blk in range(L // TA)], a2_load, a2_compute)
                end_phase()

            if on("A2b"):
                (war, wb) = next_arena()
                wA = war[:, 0:8 * 2560].rearrange("p (k c) -> p k c", k=8)
                for kc in range(8):
                    load_w(wA[:, kc, :], wb, w_in[l, kc * 128:(kc + 1) * 128, O_GH:O_GH + 2560], 2560)
                hT_r = ring(2, [128, 8, TA], BF16)
                gh_r = ring(2, [128, 4, TA], BF16)
                mg_r = ring(2, [128, 8, TA], BF16)

                def a2b_load(it):
                    s, L, blk = it
                    (ht, hb) = hT_r.next()
                    dma(ht, HN[s].ap[blk].rearrange("p (c t) -> p c t", c=8), [HN[s].b(blk)], [hb])
                    return ht, hb

                def a2b_compute(it, cur):
                    s, L, blk = it
                    (ht, hb) = cur
                    (gh, ghb) = gh_r.next()
                    for hh in range(4):
                        (pt, pb) = psf.next()
                        mm(pt[:, 0:TA], pb, [(wA[:, kc, hh * 128:(hh + 1) * 128], ht[:, kc, :]) for kc in range(8)], [wb, hb])
                        act(gh[:, hh, :], pt[:, 0:TA], AF.Silu, [pb], [ghb])
                    dma(GHT[s].ap[blk].rearrange("p (c t) -> p c t", c=4), gh, [ghb], [GHT[s].b(blk)])
                    for half in range(2):
                        (mg, mgb) = mg_r.next()
                        for c in range(8):
                            c0 = 512 + (half * 8 + c) * 128
                            (pt, pb) = psf.next()
                            mm(pt[:, 0:TA], pb, [(wA[:, kc, c0:c0 + 128], ht[:, kc, :]) for kc in range(8)], [wb, hb])
                            if c % 2 == 0:
                                act(mg[:, c, :], pt[:, 0:TA], AF.Sigmoid, [pb], [mgb])
                            else:
                                act(mg[:, c, :], pt[:, 0:TA], AF.Sigmoid, [pb], [mgb])
                        dma(MG[s].ap[blk, :, half * 8 * TA:(half + 1) * 8 * TA].rearrange("p (c t) -> p c t", c=8), mg, [mgb], [MG[s].b((blk, half))])

                pipeline([(s, L, blk) for s, L in SL for blk in range(L // TA)], a2b_load, a2b_compute)
                end_phase()

            if on("B1"):
                def b1_chain(s, L, d_, wh):
                    nch = L // 128
                    mA, mBi, mBs = (1, 2, 3) if d_ == 0 else (3, 0, 1)
                    li = 127 if d_ == 0 else 0
                    S32, S32b = carve([128, 4, 128])
                    Sbf, Sbfb = carve([128, 4, 128], BF16)
                    mset("pool", S32, 0.0, [S32b])
                    mset("pool", Sbf, 0.0, [Sbfb])
                    qk_r = ring(2, [128, 8, 128], BF16)
                    os_r = ring(2, [128, 4, 128], BF16)
                    kv_r = ring(2, [128, 1024], BF16)
                    gt_r = ring(2, [128, 24])
                    gsb_r = ring(2, [128, 2, 4, 128])
                    sm_r = ring(6, [128, 4])
                    fm_r = ring(5, [128, 4, 128])
                    lf_r = ring(4, [128, 4, 128])
                    hm_r = ring(16, [128, 4, 128], BF16, where=wh)
                    iv_r = ring(12, [128, 4, 128], F32, where=wh)
                    order = list(range(nch)) if d_ == 0 else list(range(nch - 1, -1, -1))

                    def issue(c):
                        blk, cc = c // NCH, c % NCH
                        t0 = c * 128
                        (qk, qkb) = qk_r.next()
                        dma(qk, GQK[s].ap[blk].rearrange("p (c t) -> p c t", c=8)[:, :, cc * 128:(cc + 1) * 128], [GQK[s].b(blk)], [qkb])
                        (kv, kvb) = kv_r.next()
                        dma(kv, GKV[s].ap[t0:t0 + 128, :], [GKV[s].b(c)], [kvb])
                        (gt, gtb) = gt_r.next()
                        dma(gt, GT[s].ap[t0:t0 + 128, :], [GT[s].b(c)], [gtb])
                        (gsr, gsrb) = gsb_r.next()
                        for q_ in range(2):
                            gsap = AP(GS[s].ap.tensor, (q_ * 8 + d_ * 4) * L + t0, [[0, 128], [L, 4], [1, 128]])
                            dma(gsr[:, q_], gsap, [GS[s].b(t0 // TA)], [gsrb])
                        return qk, qkb, kv, kvb, gt, gtb, gsr, gsrb

                    nxt = issue(order[0])
                    for idx, c in enumerate(order):
                        (qk, qkb, kv, kvb, gt, gtb, gsr, gsrb) = nxt
                        if idx + 1 < len(order):
                            nxt = issue(order[idx + 1])
                        blk, cc = c // NCH, c % NCH
                        cumc = gt[:, d_ * 4:d_ * 4 + 4]
                        betac = gt[:, 8 + d_ * 4:12 + d_ * 4]
                        remc = gt[:, 16 + d_ * 4:20 + d_ * 4]
                        kT = qk[:, 4:8, :]
                        qT = qk[:, 0:4, :]
                        kv3k = kv[:, 0:512].rearrange("p (h d) -> p h d", h=4)
                        kv3v = kv[:, 512:1024].rearrange("p (h d) -> p h d", h=4)
                        (ecum, ecumb) = sm_r.next()
                        act(ecum, cumc, AF.Exp, [gtb], [ecumb])
                        (erem, eremb) = sm_r.next()
                        act(erem, remc, AF.Exp, [gtb], [eremb])
                        (erow, erowb) = lf_r.next()
                        act(erow, gsr[:, 0], AF.Exp, [gsrb], [erowb])
                        (pkk, pkkb) = psf.next()
                        mmh(pkk, pkkb, [(pkk[:, h_ * 128:(h_ + 1) * 128], [(kT[:, h_, :], kT[:, h_, :])]) for h_ in range(4)], [qkb])
                        (pqk, pqkb) = psf.next()
                        mmh(pqk, pqkb, [(pqk[:, h_ * 128:(h_ + 1) * 128], [(kT[:, h_, :], qT[:, h_, :])]) for h_ in range(4)], [qkb])
                        pkk3 = pkk[:].rearrange("p (h t) -> p h t", h=4)
                        pqk3 = pqk[:].rearrange("p (h t) -> p h t", h=4)
                        (da, dab) = fm_r.next()
                        stt(da, gsr[:, 0], -1.0, bc(cumc, 128), ALU.mult, ALU.add, [gsrb, gtb], [dab])
                        (db, dbb) = fm_r.next()
                        stt(db, bc(cumc, 128), -1.0, gsr[:, 0], ALU.mult, ALU.add, [gsrb, gtb], [dbb])
                        yield
                        (bec, becb) = sm_r.next()
                        tt("pool", bec, betac, ecum, ALU.mult, [gtb, ecumb], [becb])
                        tt("pool", da, da, bch(mask_f[:, 4 + mA, :], 4), ALU.add, [dab, mask_fb], [dab])
                        tt("pool", db, db, bch(mask_f[:, 4 + mBi, :], 4), ALU.add, [dbb, mask_fb], [dbb])
                        (bm, bmb) = fm_r.next()
                        tt("pool", bm, gsr[:, 1], bch(mask_f[:, mBs, :], 4), ALU.mult, [gsrb, mask_fb], [bmb])
                        yield
                        act(da, da, AF.Exp, [dab], [dab])
                        act(db, db, AF.Exp, [dbb], [dbb])
                        yield
                        (A0, A0b) = fm_r.next()
                        tt("dve", A0, pkk3, da, ALU.mult, [pkkb, dab], [A0b])
                        (B0, B0b) = fm_r.next()
                        tt("dve", B0, pkk3, db, ALU.mult, [pkkb, dbb], [B0b])
                        (qkT, qkTb) = hm_r.next()
                        tt("dve", qkT, pqk3, db, ALU.mult, [pqkb, dbb], [qkTb])
                        yield
                        (Am, Amb) = iv_r.next()
                        tt("pool", Am, A0, bc(betac, 128), ALU.mult, [A0b, gtb], [Amb])
                        (Bm, Bmb) = iv_r.next()
                        tt("pool", Bm, B0, bm, ALU.mult, [B0b, bmb], [Bmb])
                        yield
                        (Y, Yb) = iv_r.next()
                        tt("pool", Y, bch(ident_f[:], 4), Bm, ALU.subtract, [ident_fb, Bmb], [Yb])
                        (bv, bvb) = hm_r.next()
                        tt("pool", bv, kv3v, bc(betac, 128), ALU.mult, [kvb, gtb], [bvb])
                        (kbe, kbeb) = hm_r.next()
                        tt("pool", kbe, kv3k, bc(bec, 128), ALU.mult, [kvb, becb], [kbeb])
                        (kdt, kdtb) = hm_r.next()
                        tt("pool", kdt, kv3k, bc(erem, 128), ALU.mult, [kvb, eremb], [kdtb])
                        (qdT, qdTb) = hm_r.next()
                        tt("dve", qdT, qT, erow, ALU.mult, [qkb, erowb], [qdTb])
                        yield
                        P, Pb, Pt, Ptb = Am, Amb, Bm, Bmb
                        TTbf = None
                        for k in range(1, 7):
                            (pp, ppb) = psf.next()
                            mmh(pp, ppb, [(pp[:, h_ * 128:(h_ + 1) * 128], [(Pt[:, h_, :], P[:, h_, :])]) for h_ in range(4)], [Pb, Ptb])
                            if k < 6:
                                (pp2, pp2b) = psf.next()
                                mmh(pp2, pp2b, [(pp2[:, h_ * 128:(h_ + 1) * 128], [(P[:, h_, :], Pt[:, h_, :])]) for h_ in range(4)], [Pb, Ptb])
                            yield
                            (Pn, Pnb) = iv_r.next()
                            cp("act", Pn, pp[:].rearrange("p (h t) -> p h t", h=4), [ppb], [Pnb])
                            if k < 6:
                                (Ptn, Ptnb) = iv_r.next()
                                cp("dve", Ptn, pp2[:].rearrange("p (h t) -> p h t", h=4), [pp2b], [Ptnb])
                            yield
                            (pp3, pp3b) = psf.next()
                            mmh(pp3, pp3b, [(pp3[:, h_ * 128:(h_ + 1) * 128], [(Pn[:, h_, :], Y[:, h_, :])]) for h_ in range(4)], [Pnb, Yb])
                            yield
                            if k < 6:
                                (Yn, Ynb) = iv_r.next()
                                tt("dve", Yn, pp3[:].rearrange("p (h t) -> p h t", h=4), Y, ALU.add, [pp3b, Yb], [Ynb])
                                Y, Yb = Yn, Ynb
                                P, Pb, Pt, Ptb = Pn, Pnb, Ptn, Ptnb
                            else:
                                (TTbf, TTb) = hm_r.next()
                                tt("dve", TTbf, pp3[:].rearrange("p (h t) -> p h t", h=4), Y, ALU.add, [pp3b, Yb], [TTb])
                            yield
                        (pu, pub) = psf.next()
                        mmh(pu, pub, [(pu[:, h_ * 128:(h_ + 1) * 128], [(TTbf[:, h_, :], bv[:, h_, :])]) for h_ in range(4)], [TTb, bvb])
                        (pw, pwb) = psf.next()
                        mmh(pw, pwb, [(pw[:, h_ * 128:(h_ + 1) * 128], [(kbe[:, h_, :], TTbf[:, h_, :])]) for h_ in range(4)], [TTb, kbeb])
                        yield
                        (u, ub) = lf_r.next()
                        cp("act", u, pu[:].rearrange("p (h t) -> p h t", h=4), [pub], [ub])
                        (wT, wTb) = hm_r.next()
                        cp("act", wT, pw[:].rearrange("p (h t) -> p h t", h=4), [pwb], [wTb])
                        yield
                        (pv, pvb) = psf.next()
                        mmh(pv, pvb, [(pv[:, h_ * 128:(h_ + 1) * 128], [(wT[:, h_, :], Sbf[:, h_, :])]) for h_ in range(4)], [wTb, Sbfb])
                        yield
                        (vn, vnb) = hm_r.next()
                        tt("dve", vn, u, pv[:].rearrange("p (h t) -> p h t", h=4), ALU.subtract, [ub, pvb], [vnb])
                        yield
                        (po, pob) = psf.next()
                        mmh(po, pob, [(po[:, h_ * 128:(h_ + 1) * 128], [(Sbf[:, h_, :], qdT[:, h_, :]), (vn[:, h_, :], qkT[:, h_, :])]) for h_ in range(4)], [Sbfb, qdTb, vnb, qkTb])
                        (ps_, psb) = psf.next()
                        mmh(ps_, psb, [(ps_[:, h_ * 128:(h_ + 1) * 128], [(kdt[:, h_, :], vn[:, h_, :])]) for h_ in range(4)], [kdtb, vnb])
                        yield
                        (ost, ostb) = os_r.next()
                        cp("act", ost, po[:].rearrange("p (h t) -> p h t", h=4), [pob], [ostb])
                        dma(OA[s].ap[d_, blk].rearrange("p (c t) -> p c t", c=4)[:, :, cc * 128:(cc + 1) * 128], ost, [ostb], [OA[s].b((d_, blk))])
                        for h_ in range(4):
                            stt(S32[:, h_, :], S32[:, h_, :], erow[:, h_, li:li + 1], ps_[:, h_ * 128:(h_ + 1) * 128], ALU.mult, ALU.add, [S32b, erowb, psb], [S32b])
                        yield
                        cp("pool", Sbf, S32, [S32b], [Sbfb])
                        yield

                for s, L in SL:
                    run_chains([b1_chain(s, L, 0, "w0"), b1_chain(s, L, 1, "w1")])
                    end_phase()

            if on("B2"):
                def b2_chain(s, L, d_):
                    n64 = L // 64
                    nb = L // TA
                    mI = 2 if d_ == 0 else 0
                    abt_, abtb = carve([128, nb, 64])
                    dma(abt_, HAB[s].ap.rearrange("b p x -> p b x"), [HAB[s].b(k) for k in range(nb)], [abtb])
                    ab6 = abt_.rearrange("p b (d q h c) -> p b d q h c", d=2, q=2, h=4)
                    Aall, Aallb = carve([128, 4, n64])
                    Ball, Ballb = carve([128, 4, n64])
                    rr, rrb = carve([128, 4, n64])
                    for h_ in range(4):
                        cp("dve", Aall[:, h_, :].rearrange("p (b c) -> p b c", c=NC64), ab6[:, :, d_, 0, h_, :], [abtb], [Aallb])
                        cp("dve", Ball[:, h_, :].rearrange("p (b c) -> p b c", c=NC64), ab6[:, :, d_, 1, h_, :], [abtb], [Ballb])
                    if d_ == 0:
                        tt("dve", rr[:, :, 0:n64 - 1], Aall[:, :, 0:n64 - 1], Ball[:, :, 1:n64], ALU.mult, [Aallb, Ballb], [rrb])
                    else:
                        tt("dve", rr[:, :, 1:n64], Aall[:, :, 1:n64], Ball[:, :, 0:n64 - 1], ALU.mult, [Aallb, Ballb], [rrb])
                    S32, S32b = carve([128, 4, 128])
                    Sbf, Sbfb = carve([128, 4, 128], BF16)
                    mset("pool", S32, 0.0, [S32b])
                    mset("pool", Sbf, 0.0, [Sbfb])
                    qd_r = ring(2, [128, 4, 64], BF16)
                    kd_r = ring(2, [128, 4, 64], BF16)
                    os_r = ring(2, [128, 4, 64], BF16)
                    tok_r = ring(3, [64, 1536], BF16)
                    am_r = ring(2, [64, 4, 64], BF16)
                    tmp_r = ring(2, [128, 4, 128])
                    order = list(range(n64)) if d_ == 0 else list(range(n64 - 1, -1, -1))

                    def issue(c):
                        blk, cc = c // NC64, c % NC64
                        (qd, qdb) = qd_r.next()
                        dma(qd, HQd[s].ap[d_, blk].rearrange("p (c t) -> p c t", c=4)[:, :, cc * 64:(cc + 1) * 64], [HQd[s].b((d_, blk))], [qdb])
                        (kd, kdb) = kd_r.next()
                        dma(kd, HKd[s].ap[d_, blk].rearrange("p (c t) -> p c t", c=4)[:, :, cc * 64:(cc + 1) * 64], [HKd[s].b((d_, blk))], [kdb])
                        (tk, tkb) = tok_r.next()
                        dma(tk, HT[s].ap[c * 64:(c + 1) * 64, :], [HT[s].b(c // 2)], [tkb])
                        return qd, qdb, kd, kdb, tk, tkb

                    nxt = issue(order[0])
                    for ci_, c in enumerate(order):
                        (qd, qdb, kd, kdb, tk, tkb) = nxt
                        if ci_ + 1 < n64:
                            nxt = issue(order[ci_ + 1])
                        blk, cc = c // NC64, c % NC64
                        (pa, pab) = psf.next()
                        mmh(pa, pab, [(pa[0:64, h_ * 64:(h_ + 1) * 64], [(kd[:, h_, :], qd[:, h_, :])]) for h_ in range(4)], [kdb, qdb])
                        if ci_ < n64 - 1:
                            (pp, ppb) = psf.next()
                            mmh(pp, ppb, [(pp[:, h_ * 128:(h_ + 1) * 128], [(tk[:, d_ * 512 + h_ * 128:d_ * 512 + (h_ + 1) * 128], tk[:, 1024 + h_ * 128:1024 + (h_ + 1) * 128])]) for h_ in range(4)], [tkb])
                        yield
                        (am, amb) = am_r.next()
                        tt("dve", am, pa[0:64, 0:256].rearrange("p (h t) -> p h t", h=4), bch(mask_f[0:64, mI, 0:64], 4), ALU.mult, [pab, mask_fb], [amb])
                        yield
                        (po, pob) = psf.next()
                        mmh(po, pob, [(po[:, h_ * 64:(h_ + 1) * 64], [(Sbf[:, h_, :], qd[:, h_, :]), (tk[:, 1024 + h_ * 128:1024 + (h_ + 1) * 128], am[:, h_, :])]) for h_ in range(4)], [Sbfb, qdb, tkb, amb])
                        yield
                        (ost, ostb) = os_r.next()
                        cp("act", ost, po[:, 0:256].rearrange("p (h t) -> p h t", h=4), [pob], [ostb])
                        dma(OB[s].ap[d_, blk].rearrange("p (c t) -> p c t", c=4)[:, :, cc * 64:(cc + 1) * 64], ost, [ostb], [OB[s].b((d_, blk))])
                        if ci_ < n64 - 1:
                            (tm, tmb) = tmp_r.next()
                            tt("dve", tm, pp[:].rearrange("p (h t) -> p h t", h=4), S32, ALU.add, [ppb, S32b], [tmb])
                            yield
                            tt("pool", S32, tm, bc(rr[:, :, c], 128), ALU.mult, [tmb, rrb], [S32b])
                            tt("dve", Sbf, tm, bc(rr[:, :, c], 128), ALU.mult, [tmb, rrb], [Sbfb])
                        yield

                for s, L in SL:
                    run_chains([b2_chain(s, L, 0), b2_chain(s, L, 1)])
                    end_phase()

            if on("C1"):
                (war, wb) = next_arena()
                wbg_ = war[:, 0:4096].rearrange("p (k c) -> p k c", k=4)
                wbh_ = war[:, 4096:8192].rearrange("p (k c) -> p k c", k=4)
                wo_ = war[:, 8192:16384].rearrange("p (k c) -> p k c", k=8)
                for kc in range(4):
                    load_w(wbg_[:, kc, :], wb, w_bg[l, kc * 128:(kc + 1) * 128, :], 1024)
                    load_w(wbh_[:, kc, :], wb, w_bh[l, kc * 128:(kc + 1) * 128, :], 1024)
                for kc in range(8):
                    load_w(wo_[:, kc, :], wb, w_o[l, kc * 128:(kc + 1) * 128, :], 1024)
                in_r = ring(12, [128, 4, TA], BF16, where=('w1' if war is warena[0][0] else 'w0'))
                mg_r = ring(4, [128, 8, TA], BF16)
                x_r = ring(2, [128, 8, TA])
                o_r = ring(1, [128, 4, TA])
                rs4_r = ring(1, [128, 4, TA])
                rt4_r = ring(1, [128, 4, TA])
                on_r = ring(1, [128, 4, TA])
                sq4_r = ring(1, [128, 4, TA], BF16)
                onz_r = ring(2, [128, 4, TA], BF16)
                mrg_r = ring(1, [128, 8, TA], BF16)
                f_r = ring(6, [128, TA])
                mo_r = ring(1, [128, 8, TA])
                sq8_r = ring(1, [128, 8, TA], BF16)
                res_r = ring(1, [128, 8, TA])
                rstd_r = ring(2, [128, TA])
                tmp_r = ring(2, [128, TA])
                def c1_load(it):
                    s, L, blk = it
                    ins_ = []
                    for (OX, GX) in ((OA, ZT), (OB, GHT)):
                        for (dt_, key) in ((OX[s].ap[0, blk], OX[s].b((0, blk))), (OX[s].ap[1, blk], OX[s].b((1, blk))), (GX[s].ap[blk], GX[s].b(blk))):
                            (t_, tb_) = in_r.next()
                            dma(t_, dt_.rearrange("p (c t) -> p c t", c=4), [key], [tb_])
                            ins_.append((t_, tb_))
                    mgs = []
                    for half in range(2):
                        (mg, mgb) = mg_r.next()
                        dma(mg, MG[s].ap[blk, :, half * 8 * TA:(half + 1) * 8 * TA].rearrange("p (c t) -> p c t", c=8), [MG[s].b((blk, half))], [mgb])
                        mgs.append((mg, mgb))
                    (xt, xb) = x_r.next()
                    dma(xt, XT[s][par].ap[:, :, blk * TA:(blk + 1) * TA].rearrange("c p t -> p c t"), [XT[s][par].b(blk)], [xb])
                    return ins_, mgs, xt, xb

                def c1_compute(it, cur):
                    s, L, blk = it
                    (ins_, mgs, xt, xb) = cur
                    onz = []
                    for bi, (nw_, nwb_) in enumerate(((nwa, nwab), (nwh, nwhb))):
                        (of, ofb), (ob_, obb), (gz, gzb) = ins_[bi * 3:bi * 3 + 3]
                        (o, ob2) = o_r.next()
                        tt("dve", o, of, ob_, ALU.add, [ofb, obb], [ob2])
                        (sq4, sq4b) = sq4_r.next()
                        act(sq4, o, AF.Square, [ob2], [sq4b])
                        (rs4, rs4b) = rs4_r.next()
                        (rt4, rt4b) = rt4_r.next()
                        for hp in range(2):
                            (pt, pb) = psf.next()
                            mmh(pt, pb, [(pt[:, q * TA:(q + 1) * TA], [(ones_h[:], sq4[:, hp * 2 + q, :])]) for q in range(2)], [ones_hb, sq4b])
                            act(rt4[:, hp * 2:hp * 2 + 2, :], pt[:].rearrange("p (h t) -> p h t", h=2), AF.Sqrt, [pb, epsb], [rt4b], bias=eps_t[:, 0:1], scale=1.0 / 128)
                        S.add("dve", lambda h, rs4=rs4, rt4=rt4: h.reciprocal(out=rs4, in_=rt4), [rt4b], [rs4b])
                        (on_, onb) = on_r.next()
                        stt(on_, o, nw_[:, 0:1], rs4, ALU.mult, ALU.mult, [ob2, nwb_, rs4b], [onb])
                        (oz, ozb) = onz_r.next()
                        tt("pool", oz, on_, gz, ALU.mult, [onb, gzb], [ozb])
                        onz.append((oz, ozb))
                    (mrg, mrgb) = mrg_r.next()
                    for oc in range(8):
                        (pa, pab) = psf.next()
                        mm(pa[:, 0:TA], pab, [(wbg_[:, kc, oc * 128:(oc + 1) * 128], onz[0][0][:, kc, :]) for kc in range(4)], [wb, onz[0][1]])
                        (pb_, pbb) = psf.next()
                        mm(pb_[:, 0:TA], pbb, [(wbh_[:, kc, oc * 128:(oc + 1) * 128], onz[1][0][:, kc, :]) for kc in range(4)], [wb, onz[1][1]])
                        (t1, t1b) = f_r.next()
                        tt("dve", t1, pa[:, 0:TA], mgs[0][0][:, oc, :], ALU.mult, [pab, mgs[0][1]], [t1b])
                        (t2, t2b) = f_r.next()
                        tt("dve", t2, pb_[:, 0:TA], mgs[1][0][:, oc, :], ALU.mult, [pbb, mgs[1][1]], [t2b])
                        tt("pool", mrg[:, oc, :], t1, t2, ALU.add, [t1b, t2b], [mrgb])
                    (mo, mob) = mo_r.next()
                    (sq8, sq8b) = sq8_r.next()
                    for oc in range(8):
                        (pt, pb) = psf.next()
                        mm(pt[:, 0:TA], pb, [(wo_[:, kc, oc * 128:(oc + 1) * 128], mrg[:, kc, :]) for kc in range(8)], [wb, mrgb])
                        cp("act", mo[:, oc, :], pt[:, 0:TA], [pb], [mob])
                        act(sq8[:, oc, :], pt[:, 0:TA], AF.Square, [pb], [sq8b])
                    res, resb = post_norm_add(res_r, xt, xb, mo, mob, sq8, sq8b, g_po, g_pob, rstd_r, tmp_r, f_r)
                    dma(XM[s].ap[:, :, blk * TA:(blk + 1) * TA].rearrange("c p t -> p c t"), res, [resb], [XM[s].b(blk)])

                pipeline([(s, L, blk) for s, L in SL for blk in range(L // TA)], c1_load, c1_compute)
                end_phase()

            for half in range(2):
                if not on("C2a"):
                    continue
                (war, wb) = next_arena()
                NF = 11 * 128
                wg_ = war[:, 0:8 * NF].rearrange("p (k c) -> p k c", k=8)
                wu_ = war[:, 8 * NF:16 * NF].rearrange("p (k c) -> p k c", k=8)
                for kc in range(8):
                    load_w(wg_[:, kc, :], wb, w_g[l, kc * 128:(kc + 1) * 128, half * NF:(half + 1) * NF], NF)
                    load_w(wu_[:, kc, :], wb, w_u[l, kc * 128:(kc + 1) * 128, half * NF:(half + 1) * NF], NF)
                xT_r = ring(2, [128, 8, TA])
                sq_r = ring(1, [128, 8, TA], BF16)
                hT_r = ring(2, [128, 8, TA], BF16)
                rstd_r = ring(2, [128, TA])
                tmp_r = ring(2, [128, TA])
                sg_r = ring(4, [128, TA])
                a_r = ring(2, [128, 11, TA], BF16)

                def c2a_load(it, half=half):
                    s, L, blk = it
                    if half == 0:
                        return x_load(XM[s], L, blk, False, xT_r)
                    (ht, hb) = hT_r.next()
                    dma(ht, H2[s].ap[blk].rearrange("p (c t) -> p c t", c=8), [H2[s].b(blk)], [hb])
                    return ht, hb

                def c2a_compute(it, cur, half=half):
                    s, L, blk = it
                    if half == 0:
                        ht, hb = norm_compute(cur[0], cur[1], g_pf, g_pfb, False, sq_r, hT_r, rstd_r, tmp_r)
                        dma(H2[s].ap[blk].rearrange("p (c t) -> p c t", c=8), ht, [hb], [H2[s].b(blk)])
                    else:
                        (ht, hb) = cur
                    (a_, ab_) = a_r.next()
                    for fc in range(11):
                        (pg, pgb) = psf.next()
                        mm(pg[:, 0:TA], pgb, [(wg_[:, kc, fc * 128:(fc + 1) * 128], ht[:, kc, 0:TA]) for kc in range(8)], [wb, hb])
                        (pu, pub) = psf.next()
                        mm(pu[:, 0:TA], pub, [(wu_[:, kc, fc * 128:(fc + 1) * 128], ht[:, kc, 0:TA]) for kc in range(8)], [wb, hb])
                        (sg, sgb) = sg_r.next()
                        act(sg, pg[:, 0:TA], AF.Silu, [pgb], [sgb])
                        tt("dve", a_[:, fc, :], sg, pu[:, 0:TA], ALU.mult, [sgb, pub], [ab_])
                    dma(ACTS[s].ap[blk, :, half * 11 * TA:(half + 1) * 11 * TA].rearrange("p (c t) -> p c t", c=11), a_, [ab_], [ACTS[s].b((blk, half))])

                pipeline([(s, L, blk) for s, L in SL for blk in range(L // TA)], c2a_load, c2a_compute)
                end_phase()

            if on("C2b"):
                (war, wb) = next_arena()
                wd_ = war[:, 0:22 * 1024].rearrange("p (k c) -> p k c", k=22)
                for fc in range(22):
                    load_w(wd_[:, fc, :], wb, w_d[l, fc * 128:(fc + 1) * 128, :], 1024)
                a_r = ring(2, [128, 22, TA], BF16)
                x_r = ring(2, [128, 8, TA])
                ff_r = ring(1, [128, 8, TA])
                sq8_r = ring(1, [128, 8, TA], BF16)
                res_r = ring(2, [128, 8, TA])
                rstd_r = ring(2, [128, TA])
                tmp_r = ring(2, [128, TA])
                f_r = ring(4, [128, TA])
                yt_r = ring(2, [128, D])
                last = (l == depth - 1)

                def c2b_load(it):
                    s, L, blk = it
                    (a_, ab_) = a_r.next()
                    dma(a_, ACTS[s].ap[blk].rearrange("p (c t) -> p c t", c=22), [ACTS[s].b((blk, 0)), ACTS[s].b((blk, 1))], [ab_])
                    (xt, xb) = x_r.next()
                    dma(xt, XM[s].ap[:, :, blk * TA:(blk + 1) * TA].rearrange("c p t -> p c t"), [XM[s].b(blk)], [xb])
                    return a_, ab_, xt, xb

                def c2b_compute(it, cur):
                    s, L, blk = it
                    (a_, ab_, xt, xb) = cur
                    (ffo, ffob) = ff_r.next()
                    (sq8, sq8b) = sq8_r.next()
                    for oc in range(8):
                        (pt, pb) = psf.next()
                        mm(pt[:, 0:TA], pb, [(wd_[:, fc, oc * 128:(oc + 1) * 128], a_[:, fc, :]) for fc in range(22)], [wb, ab_])
                        cp("act", ffo[:, oc, :], pt[:, 0:TA], [pb], [ffob])
                        act(sq8[:, oc, :], pt[:, 0:TA], AF.Square, [pb], [sq8b])
                    res, resb = post_norm_add(res_r, xt, xb, ffo, ffob, sq8, sq8b, g_qf, g_qfb, rstd_r, tmp_r, f_r)
                    if not last:
                        dma(XT[s][1 - par].ap[:, :, blk * TA:(blk + 1) * TA].rearrange("c p t -> p c t"), res, [resb], [XT[s][1 - par].b(blk)])
                    else:
                        for j in range(NCH):
                            (yt, ytb) = yt_r.next()
                            for hf in range(2):
                                (pt, pb) = psf.next()
                                mmh(pt, pb, [(pt[:, c * 128:(c + 1) * 128], [(res[:, hf * 4 + c, j * 128:(j + 1) * 128], ident_f[:])]) for c in range(4)], [resb, ident_fb])
                                cp("act" if hf == 0 else "dve", yt[:, hf * 512:(hf + 1) * 512], pt[:], [pb], [ytb])
                            r0 = blk * TA + j * 128
                            dma(y_out[s].ap[r0:r0 + 128, :], yt, [ytb], [y_out[s].b(r0 // 128)])

                pipeline([(s, L, blk) for s, L in SL for blk in range(L // TA)], c2b_load, c2b_compute)
                end_phase()
        S.emit(es)
    return nc


def host_consts():
    ident = np.eye(128, dtype=np.float32)
    p = np.arange(128)[:, None]
    f = np.arange(128)[None, :]
    m = np.zeros((128, 8, 128), np.float32)
    m[:, 0] = (f <= p)
    m[:, 1] = (f < p)
    m[:, 2] = (f >= p)
    m[:, 3] = (f > p)
    m[:, 4:8] = (m[:, 0:4] - 1.0) * 30000.0
    sel = np.zeros((8, 2), np.float32)
    sel[0:4, 0] = 1.0
    sel[4:8, 1] = 1.0
    return {"c_ident": ident, "c_mask": m, "c_sel": sel}


W_KEYS = ["w_in", "conv_w", "gdn_norm_w", "hgrn_norm_w", "w_branch_gdn", "w_branch_hgrn", "w_out", "norm_pre_mix",
          "norm_post_mix", "norm_pre_ffn", "norm_post_ffn", "w_ffn_gate", "w_ffn_up", "w_ffn_down", "hgrn_lb_logits"]


def make_in_map(inputs, xs, depth):
    m = {f"x{s}": np.ascontiguousarray(x, dtype=np.float32) for s, x in enumerate(xs)}
    for k in W_KEYS:
        m[k] = np.ascontiguousarray(np.asarray(inputs[k], dtype=np.float32)[:depth])
    m["gdn_a_log"] = np.ascontiguousarray(np.asarray(inputs["gdn_a_log"], dtype=np.float32)[:depth].reshape(depth, 8))
    m["gdn_dt_bias"] = np.ascontiguousarray(np.asarray(inputs["gdn_dt_bias"], dtype=np.float32)[:depth].reshape(depth, 8))
    m.update(host_consts())
    return m


_NC_CACHE = {}


def kernel(**inputs):
    xp = np.asarray(inputs["x_prompt"], dtype=np.float32)
    xs = np.asarray(inputs["x_sample"], dtype=np.float32)
    depth = int(np.asarray(inputs["w_in"]).shape[0])
    seq_lens = (xp.shape[1], xs.shape[1])
    key = (seq_lens, depth)
    if key not in _NC_CACHE:
        _NC_CACHE[key] = build(list(seq_lens), depth)
    nc = _NC_CACHE[key]
    in_maps = [make_in_map(inputs, [xp[c], xs[c]], depth) for c in range(NCORES)]
    res = run_bass_kernel_spmd(nc, in_maps, core_ids=list(range(NCORES)))
    yp = np.stack([np.asarray(res.results[c]["y0"], dtype=np.float32) for c in range(NCORES)], axis=0)
    ys = np.stack([np.asarray(res.results[c]["y1"], dtype=np.float32) for c in range(NCORES)], axis=0)
    return (yp, ys)
```

```python
import numpy as np
from contextlib import ExitStack
import concourse.bass as bass
import concourse.mybir as mybir
from concourse.ap import AP
from concourse.bass_utils import run_bass_kernel_spmd

F32 = mybir.dt.float32
BF16 = mybir.dt.bfloat16
AF = mybir.ActivationFunctionType
ALU = mybir.AluOpType
AX = mybir.AxisListType

D = 1024
NH = 4
DK = 128
DFF = 2816
NFF = DFF // 128
INW = 6672
EPS = 1e-6
O_Q, O_K, O_V, O_Z, O_A, O_B, O_QH, O_FH, O_IH, O_GH, O_GATE = 0, 512, 1024, 1536, 2048, 2056, 2064, 2576, 3600, 4112, 4624
TA = 256
NCORES = 8


class Buf:
    __slots__ = ("name", "last_w", "rd_eng", "rd_dma")

    def __init__(self, name=""):
        self.name = name
        self.last_w = None
        self.rd_eng = {}
        self.rd_dma = []


class Op:
    __slots__ = ("eng", "fn", "idx", "marked", "dma", "deps", "sem", "val")

    def __init__(self, eng, fn, dma):
        self.eng = eng
        self.fn = fn
        self.dma = dma
        self.marked = False
        self.deps = ()
        self.sem = None
        self.val = 0


class Sched:
    ENGS = ("pe", "act", "dve", "pool", "sp")
    GEN = 30000
    NDMA = 40

    def __init__(self, nc):
        self.nc = nc
        self.streams = {e: [] for e in self.ENGS}
        self.seen = {e: {} for e in self.ENGS}
        self.seen_dma = {e: set() for e in self.ENGS}
        self.ndma = 0
        self.dma_ops = []
        self.pending = {e: None for e in self.ENGS}

    def barrier(self):
        deps = []
        for e in ("pe", "act", "dve", "pool"):
            if self.streams[e]:
                deps.append(self.streams[e][-1])
        deps.extend(self.dma_ops[-self.NDMA:])
        for e in self.ENGS:
            self.pending[e] = list(deps) + (self.pending[e] or [])

    def add(self, eng, fn, reads=(), writes=(), dma=False):
        op = Op(eng, fn, dma)
        st = self.streams[eng]
        op.idx = len(st)
        st.append(op)
        deps = []
        if self.pending[eng] is not None:
            deps.extend(self.pending[eng])
            self.pending[eng] = None
        for b in reads:
            if b.last_w is not None:
                deps.append(b.last_w)
        for b in writes:
            if b.last_w is not None:
                deps.append(b.last_w)
            deps.extend(b.rd_eng.values())
            deps.extend(b.rd_dma)
        need = {}
        dma_need = []
        seen = self.seen[eng]
        sdma = self.seen_dma[eng]
        for d in deps:
            if d is op:
                continue
            if d.dma:
                if d not in sdma:
                    sdma.add(d)
                    dma_need.append(d)
            else:
                if d.eng == "pe" and eng == "pe" and not dma:
                    continue
                if seen.get(d.eng, -1) >= d.idx:
                    continue
                if need.get(d.eng, -1) < d.idx:
                    need[d.eng] = d.idx
        wl = []
        for e2, k in need.items():
            seen[e2] = k
            dop = self.streams[e2][k]
            dop.marked = True
            wl.append(dop)
        op.deps = wl + dma_need
        if dma:
            i = self.ndma
            self.ndma += 1
            self.dma_ops.append(op)
            if i >= self.NDMA:
                prev = self.dma_ops[i - self.NDMA]
                if prev not in sdma:
                    sdma.add(prev)
                    op.deps.append(prev)
        for b in reads:
            if dma:
                b.rd_dma.append(op)
            else:
                b.rd_eng[eng] = op
        for b in writes:
            b.last_w = op
            b.rd_eng = {}
            b.rd_dma = []
        return op

    def emit(self, es):
        nc = self.nc
        for e in self.ENGS:
            nmark = sum(1 for o in self.streams[e] if o.marked and not o.dma)
            ngen = max(1, -(-nmark // self.GEN))
            sems = [es.enter_context(nc.semaphore(f"s_{e}_{g}")) for g in range(ngen)]
            c = 0
            for o in self.streams[e]:
                if o.marked and not o.dma:
                    o.sem = sems[c // self.GEN]
                    o.val = c % self.GEN + 1
                    c += 1
        nd = min(self.NDMA, max(1, self.ndma))
        dsems = [es.enter_context(nc.semaphore(f"s_dma_{i}")) for i in range(nd)]
        final_dma = {}
        for i, o in enumerate(self.dma_ops):
            o.sem = dsems[i % self.NDMA]
            o.val = 16 * (i // self.NDMA + 1)
            final_dma[i % self.NDMA] = o.val

        def run(e, h, last=False):
            for o in self.streams[e]:
                for d in o.deps:
                    h.wait_ge(d.sem, d.val)
                ins = o.fn(h)
                if o.dma:
                    ins.then_inc(o.sem, 16)
                elif o.marked:
                    ins.then_inc(o.sem, 1)
            if last:
                for i, v in final_dma.items():
                    h.wait_ge(dsems[i], v)

        block = es.enter_context(nc.Block())

        @block.tensor
        def _(h):
            run("pe", h)

        @block.scalar
        def _(h):
            run("act", h)

        @block.vector
        def _(h):
            run("dve", h)

        @block.gpsimd
        def _(h):
            run("pool", h)

        @block.sync
        def _(h):
            run("sp", h, last=True)


class Ring:
    def __init__(self, tiles):
        self.tiles = tiles
        self.i = 0

    def next(self):
        t = self.tiles[self.i % len(self.tiles)]
        self.i += 1
        return t


class DT:
    def __init__(self, ap):
        self.ap = ap
        self.bufs = {}

    def b(self, key):
        r = self.bufs.get(key)
        if r is None:
            r = self.bufs[key] = Buf()
        return r


FA_N = 15360
HA_N = 22528
WAR_N = 22528


def build(seq_lens, depth, dbg=(), phases=None):
    nc = bass.Bass("TRN2", target_bir_lowering=False)
    nseq = len(seq_lens)
    XH = TA + 4
    NCH = TA // 128
    NC64 = TA // 64

    def on(p):
        return phases is None or p in phases

    def din(name, shape, dt=F32):
        return nc.dram_tensor(name, list(shape), dt, kind="ExternalInput").ap()

    def dscr(name, shape, dt=F32):
        kind = "ExternalOutput" if name in dbg else "Internal"
        return DT(nc.dram_tensor(name, list(shape), dt, kind=kind).ap())

    x_in = [din(f"x{s}", [seq_lens[s], D]) for s in range(nseq)]
    y_out = [DT(nc.dram_tensor(f"y{s}", [seq_lens[s], D], F32, kind="ExternalOutput").ap()) for s in range(nseq)]
    w_in = din("w_in", [depth, D, INW])
    conv_w = din("conv_w", [depth, 5, 1536])
    a_log = din("gdn_a_log", [depth, 8])
    dt_bias = din("gdn_dt_bias", [depth, 8])
    gdn_nw = din("gdn_norm_w", [depth, 128])
    lb_logits = din("hgrn_lb_logits", [depth, 2, 512])
    hg_nw = din("hgrn_norm_w", [depth, 128])
    w_bg = din("w_branch_gdn", [depth, 512, D])
    w_bh = din("w_branch_hgrn", [depth, 512, D])
    w_o = din("w_out", [depth, D, D])
    n_pre_mix = din("norm_pre_mix", [depth, D])
    n_post_mix = din("norm_post_mix", [depth, D])
    n_pre_ffn = din("norm_pre_ffn", [depth, D])
    n_post_ffn = din("norm_post_ffn", [depth, D])
    w_g = din("w_ffn_gate", [depth, D, DFF])
    w_u = din("w_ffn_up", [depth, D, DFF])
    w_d = din("w_ffn_down", [depth, DFF, D])
    c_ident = din("c_ident", [128, 128])
    c_mask = din("c_mask", [128, 8, 128])
    c_sel = din("c_sel", [8, 2])

    SL = list(enumerate(seq_lens))
    XT = [[dscr(f"XT{p}_{s}", [8, 128, L]) for p in range(2)] for s, L in SL]
    XM = [dscr(f"XM_{s}", [8, 128, L]) for s, L in SL]
    HN = [dscr(f"HN_{s}", [L // TA, 128, 8 * TA], BF16) for s, L in SL]
    H2 = [dscr(f"H2_{s}", [L // TA, 128, 8 * TA], BF16) for s, L in SL]
    GQK = [dscr(f"GQK_{s}", [L // TA, 128, 8 * TA], BF16) for s, L in SL]
    GKV = [dscr(f"GKV_{s}", [L, 1024], BF16) for s, L in SL]
    GS = [dscr(f"GS_{s}", [3, 8, L]) for s, L in SL]
    GT = [dscr(f"GT_{s}", [L, 24]) for s, L in SL]
    ZT = [dscr(f"ZT_{s}", [L // TA, 128, 4 * TA], BF16) for s, L in SL]
    HQd = [dscr(f"HQ_{s}", [2, L // TA, 128, 4 * TA], BF16) for s, L in SL]
    HKd = [dscr(f"HK_{s}", [2, L // TA, 128, 4 * TA], BF16) for s, L in SL]
    HAB = [dscr(f"HAB_{s}", [L // TA, 128, 64]) for s, L in SL]
    HT = [dscr(f"HT_{s}", [L, 1536], BF16) for s, L in SL]
    GHT = [dscr(f"GHT_{s}", [L // TA, 128, 4 * TA], BF16) for s, L in SL]
    MG = [dscr(f"MG_{s}", [L // TA, 128, 16 * TA], BF16) for s, L in SL]
    OA = [dscr(f"OA_{s}", [2, L // TA, 128, 4 * TA], BF16) for s, L in SL]
    OB = [dscr(f"OB_{s}", [2, L // TA, 128, 4 * TA], BF16) for s, L in SL]
    ACTS = [dscr(f"ACT_{s}", [L // TA, 128, 22 * TA], BF16) for s, L in SL]

    with ExitStack() as es:
        S = Sched(nc)
        cnt = [0]

        def sb(shape, dt=F32, name=None):
            cnt[0] += 1
            nm = name or f"t{cnt[0]}"
            t = es.enter_context(nc.sbuf_tensor(nm, list(shape), dt))
            return t, Buf(nm)

        def pst(shape, dt=F32, name=None):
            t = es.enter_context(nc.psum_tensor(name, list(shape), dt))
            return t, Buf(name)

        farena = es.enter_context(nc.sbuf_tensor("farena", [128, FA_N], F32))
        harena = es.enter_context(nc.sbuf_tensor("harena", [128, HA_N], BF16))
        off = {"f": 0, "h": 0, "w0": 0, "w1": 0}

        def carve(shape, dt=F32, where=None):
            k = where or ("f" if dt == F32 else "h")
            ar = {"f": farena, "h": harena, "w0": warena[0][0] if warena else None, "w1": warena[1][0] if warena else None}[k]
            cap = {"f": FA_N, "h": HA_N, "w0": WAR_N, "w1": WAR_N}[k]
            n = 1
            for d_ in shape[1:]:
                n *= d_
            o = off[k]
            if k in ("w0", "w1") and dt == F32:
                off[k] = o + 2 * n
                assert off[k] <= cap, (k, off[k])
                v = ar[0:shape[0], o:o + 2 * n].bitcast(F32)
            else:
                off[k] = o + n
                assert off[k] <= cap, (k, off[k])
                v = ar[0:shape[0], o:o + n]
            if len(shape) == 3:
                v = v.rearrange("p (a b) -> p a b", a=shape[1])
            elif len(shape) == 4:
                v = v.rearrange("p (a b c) -> p a b c", a=shape[1], b=shape[2])
            return v, Buf()

        def ring(n, shape, dt=F32, where=None):
            return Ring([carve(shape, dt, where) for _ in range(n)])

        def end_phase():
            S.barrier()
            for k_ in off:
                off[k_] = 0

        def run_chains(gens, window=None):
            gens = list(gens)
            active = []
            while gens or active:
                while gens and (window is None or len(active) < window):
                    active.append(gens.pop(0))
                for g in list(active):
                    try:
                        next(g)
                    except StopIteration:
                        active.remove(g)

        def pipeline(items, load_fn, compute_fn):
            nxt = load_fn(items[0])
            for i, it in enumerate(items):
                cur = nxt
                nxt = load_fn(items[i + 1]) if i + 1 < len(items) else None
                wtick(2)
                compute_fn(it, cur)

        warena = []

        def dma(out, in_, reads, writes, nonc=False):
            if nonc:
                S.add("sp", lambda h: h.dma_start(out=out, in_=in_, allow_slow_non_contiguous=True), reads, writes, dma=True)
            else:
                S.add("sp", lambda h: h.dma_start(out=out, in_=in_), reads, writes, dma=True)

        def mm(out, wbuf, pairs, reads):
            n = len(pairs)

            def fn(h):
                ins = None
                for i, (l, r) in enumerate(pairs):
                    ins = h.matmul(out, lhsT=l, rhs=r, start=(i == 0), stop=(i == n - 1))
                return ins
            S.add("pe", fn, reads, [wbuf])

        def mmh(pt, pb, items, reads):
            def fn(h):
                ins = None
                for (o, prs) in items:
                    n = len(prs)
                    for i, (l, r) in enumerate(prs):
                        ins = h.matmul(o, lhsT=l, rhs=r, start=(i == 0), stop=(i == n - 1))
                return ins
            S.add("pe", fn, reads, [pb])

        def act(out, in_, func, reads, writes, **kw):
            S.add("act", lambda h: h.activation(out=out, in_=in_, func=func, **kw), reads, writes)

        def ts(eng, out, in0, s1, s2, op0, op1, reads, writes):
            if s2 is None:
                S.add(eng, lambda h: h.tensor_scalar(out=out, in0=in0, scalar1=s1, scalar2=None, op0=op0), reads, writes)
            else:
                S.add(eng, lambda h: h.tensor_scalar(out=out, in0=in0, scalar1=s1, scalar2=s2, op0=op0, op1=op1), reads, writes)

        def stt(out, in0, sc, in1, op0, op1, reads, writes):
            S.add("dve", lambda h: h.scalar_tensor_tensor(out=out, in0=in0, scalar=sc, in1=in1, op0=op0, op1=op1), reads, writes)

        def tt(eng, out, in0, in1, op, reads, writes):
            S.add(eng, lambda h: h.tensor_tensor(out=out, in0=in0, in1=in1, op=op), reads, writes)

        def cp(eng, out, in_, reads, writes):
            if eng == "act":
                S.add("act", lambda h: h.activation(out=out, in_=in_, func=AF.Copy), reads, writes)
            else:
                S.add(eng, lambda h: h.tensor_copy(out=out, in_=in_), reads, writes)

        def mset(eng, ap, val, writes):
            S.add(eng, lambda h: h.memset(ap, val), [], writes)

        def bc(ap2, n):
            return ap2.unsqueeze(2).to_broadcast([ap2.shape[0], ap2.shape[1], n])

        def bch(ap2, n):
            return ap2.unsqueeze(1).to_broadcast([ap2.shape[0], n, ap2.shape[1]])

        ident_f, ident_fb = sb([128, 128], F32, "ident_f")
        ident_h, ident_hb = sb([128, 128], BF16, "ident_h")
        ones_h, ones_hb = sb([128, 128], BF16, "ones_h")
        mask_f, mask_fb = sb([128, 8, 128], F32, "mask_f")
        sel, selb = sb([8, 2], F32, "sel")
        eps_t, epsb = sb([128, 2], F32, "eps_t")
        dma(ident_f[:], c_ident[:, :], [], [ident_fb])
        dma(mask_f[:], c_mask[:, :, :], [], [mask_fb])
        dma(sel[:], c_sel[:, :], [], [selb])
        cp("dve", ident_h[:], ident_f[:], [ident_fb], [ident_hb])
        mset("dve", ones_h[:], 1.0, [ones_hb])
        mset("dve", eps_t[:, 0:1], float(EPS), [epsb])
        mset("dve", eps_t[:, 1:2], 1.0, [epsb])
        m128, m128b = sb([8, TA], F32, "m128")
        m64, m64b = sb([128, TA], F32, "m64")
        mset("pool", m128[:], 1.0, [m128b])
        mset("pool", m128[:].rearrange("p (c k) -> p c k", k=128)[:, :, 0:1], 0.0, [m128b])
        mset("pool", m64[:], 1.0, [m64b])
        mset("pool", m64[:].rearrange("p (c k) -> p c k", k=64)[:, :, 0:1], 0.0, [m64b])

        psf = Ring([pst([128, 512], F32, f"psf{i}") for i in range(6)])
        psh = Ring([pst([128, 1024], BF16, f"psh{i}") for i in range(2)])

        warena.extend([sb([128, WAR_N], BF16, f"warena{i}") for i in range(2)])
        wcnt = [0]
        wstage = Ring([sb([128, 320], F32, f"wstage{i}") for i in range(4)])
        cast_rr = [0]

        def next_arena():
            a = warena[wcnt[0] % 2]
            wcnt[0] += 1
            return a

        def w_gen(specs_, wb):
            pend = []
            for (dst, src, ncols) in specs_:
                c0 = 0
                while c0 < ncols:
                    w = min(320, ncols - c0)
                    (st, stb) = wstage.next()
                    dma(st[:, 0:w], src[:, c0:c0 + w], [], [stb])
                    pend.append((dst[:, c0:c0 + w], st[:, 0:w], stb))
                    if len(pend) > 2:
                        (d__, s__, sb__) = pend.pop(0)
                        cp("pool", d__, s__, [sb__], [wb])
                    c0 += w
                    yield
            for (d__, s__, sb__) in pend:
                cp("pool", d__, s__, [sb__], [wb])

        def w_specs(phase, l, war):
            if phase == "A1":
                v = war[:, 0:8 * 2064].rearrange("p (k c) -> p k c", k=8)
                return [(v[:, kc, :], w_in[l, kc * 128:(kc + 1) * 128, 0:2064], 2064) for kc in range(8)]
            if phase == "A2a":
                v = war[:, 0:8 * 2048].rearrange("p (k c) -> p k c", k=8)
                return [(v[:, kc, :], w_in[l, kc * 128:(kc + 1) * 128, O_QH:O_QH + 2048], 2048) for kc in range(8)]
            if phase == "A2b":
                v = war[:, 0:8 * 2560].rearrange("p (k c) -> p k c", k=8)
                return [(v[:, kc, :], w_in[l, kc * 128:(kc + 1) * 128, O_GH:O_GH + 2560], 2560) for kc in range(8)]
            if phase == "C1":
                a_ = war[:, 0:4096].rearrange("p (k c) -> p k c", k=4)
                b_ = war[:, 4096:8192].rearrange("p (k c) -> p k c", k=4)
                c_ = war[:, 8192:16384].rearrange("p (k c) -> p k c", k=8)
                r = []
                for kc in range(4):
                    r.append((a_[:, kc, :], w_bg[l, kc * 128:(kc + 1) * 128, :], 1024))
                    r.append((b_[:, kc, :], w_bh[l, kc * 128:(kc + 1) * 128, :], 1024))
                for kc in range(8):
                    r.append((c_[:, kc, :], w_o[l, kc * 128:(kc + 1) * 128, :], 1024))
                return r
            if phase in ("C2a0", "C2a1"):
                half = int(phase[-1])
                NF = 11 * 128
                g_ = war[:, 0:8 * NF].rearrange("p (k c) -> p k c", k=8)
                u_ = war[:, 8 * NF:16 * NF].rearrange("p (k c) -> p k c", k=8)
                r = []
                for kc in range(8):
                    r.append((g_[:, kc, :], w_g[l, kc * 128:(kc + 1) * 128, half * NF:(half + 1) * NF], NF))
                    r.append((u_[:, kc, :], w_u[l, kc * 128:(kc + 1) * 128, half * NF:(half + 1) * NF], NF))
                return r
            if phase == "C2b":
                v = war[:, 0:22 * 1024].rearrange("p (k c) -> p k c", k=22)
                return [(v[:, fc, :], w_d[l, fc * 128:(fc + 1) * 128, :], 1024) for fc in range(22)]
            raise ValueError(phase)

        wpre = {}
        wcur = [None]

        def prefetch(phase, l):
            if l >= depth or not on(phase[:3] if phase.startswith("C2a") else phase):
                return
            (war, wb) = next_arena()
            g = w_gen(w_specs(phase, l, war), wb)
            wpre[(phase, l)] = (war, wb, g)
            wcur[0] = g

        def wtick(n=2):
            g = wcur[0]
            if g is None:
                return
            for _ in range(n):
                try:
                    next(g)
                except StopIteration:
                    wcur[0] = None
                    return

        def acquire(phase, l):
            if (phase, l) in wpre:
                (war, wb, g) = wpre.pop((phase, l))
            else:
                (war, wb) = next_arena()
                g = w_gen(w_specs(phase, l, war), wb)
            for _ in g:
                pass
            if wcur[0] is g:
                wcur[0] = None
            return war, wb

        def vec_cols(src_1d, n, name):
            t, tb = sb([128, n], F32, name)
            dma(t[:], src_1d.rearrange("(c p) -> p c", p=128), [], [tb], nonc=True)
            return t, tb

        def x_load(xsrc, L, blk, halo, xT_r):
            t0 = blk * TA
            nb = L // TA
            (xt, xb) = xT_r.next()
            lo, hi = (2, 2) if halo else (0, 0)
            W = TA + lo + hi
            a0 = t0 - lo
            a1 = t0 + TA + hi
            o0 = 0
            if a0 < 0:
                mset("pool", xt[:, :, 0:lo], 0.0, [xb])
                o0 = lo
                a0 = 0
            if a1 > L:
                mset("pool", xt[:, :, W - hi:W], 0.0, [xb])
                a1 = L
            rb = [xsrc.b(k) for k in range(max(0, blk - 1), min(nb, blk + 2))] if halo else [xsrc.b(blk)]
            dma(xt[:, :, o0:o0 + (a1 - a0)], xsrc.ap[:, :, a0:a1].rearrange("c p t -> p c t"), rb, [xb])
            return xt, xb

        def norm_compute(xt, xb, gain, gainb, halo, sq_r, hT_r, rstd_r, tmp_r):
            W = TA + (4 if halo else 0)
            (sq, sqb) = sq_r.next()
            act(sq[:, :, 0:W], xt[:, :, 0:W], AF.Square, [xb], [sqb])
            (pt, pb) = psf.next()
            mm(pt[:, 0:W], pb, [(ones_h[:], sq[:, c, 0:W]) for c in range(8)], [ones_hb, sqb])
            (rs, rsb) = rstd_r.next()
            (tmp, tmpb) = tmp_r.next()
            act(tmp[:, 0:W], pt[:, 0:W], AF.Sqrt, [pb, epsb], [tmpb], bias=eps_t[:, 0:1], scale=1.0 / D)
            S.add("dve", lambda h: h.reciprocal(out=rs[:, 0:W], in_=tmp[:, 0:W]), [tmpb], [rsb])
            (ht, hb) = hT_r.next()
            for c in range(8):
                stt(ht[:, c, 0:W], xt[:, c, 0:W], gain[:, c:c + 1], rs[:, 0:W], ALU.mult, ALU.mult, [xb, gainb, rsb], [hb])
            return ht, hb

        def post_norm_add(res_r, xt, xb, val, valb, sq8, sq8b, gain, gainb, rstd_r, tmp_r, f_r):
            (pt, pb) = psf.next()
            mm(pt[:, 0:TA], pb, [(ones_h[:], sq8[:, c, :]) for c in range(8)], [ones_hb, sq8b])
            (rs, rsb) = rstd_r.next()
            (tmp, tmpb) = tmp_r.next()
            act(tmp[:, 0:TA], pt[:, 0:TA], AF.Sqrt, [pb, epsb], [tmpb], bias=eps_t[:, 0:1], scale=1.0 / D)
            S.add("dve", lambda h: h.reciprocal(out=rs[:, 0:TA], in_=tmp[:, 0:TA]), [tmpb], [rsb])
            (res, resb) = res_r.next()
            for c in range(8):
                (t1, t1b) = f_r.next()
                stt(t1[:, 0:TA], val[:, c, :], gain[:, c:c + 1], rs[:, 0:TA], ALU.mult, ALU.mult, [valb, gainb, rsb], [t1b])
                tt("dve", res[:, c, :], xt[:, c, 0:TA], t1[:, 0:TA], ALU.add, [xb, t1b], [resb])
            return res, resb

        lg, lgb = sb([128, depth, 8], F32, "lb_lg")
        lbt, lbb = sb([128, depth, 8], F32, "lb")
        omlt, omlb = sb([128, depth, 8], F32, "oml")
        for l in range(depth):
            for d_ in range(2):
                dma(lg[:, l, d_ * 4:(d_ + 1) * 4], lb_logits[l, d_].rearrange("(h p) -> p h", p=128), [], [lgb], nonc=True)
        mx, mxb = sb([128, 8], F32, "lb_mx")
        cp("dve", mx[:], lg[:, 0, :], [lgb], [mxb])
        for l in range(1, depth):
            tt("dve", mx[:], mx[:], lg[:, l, :], ALU.max, [mxb, lgb], [mxb])
        for l in range(depth):
            tt("dve", lg[:, l, :], lg[:, l, :], mx[:], ALU.subtract, [lgb, mxb], [lgb])
        act(lg[:], lg[:], AF.Exp, [lgb], [lgb])
        cp("dve", mx[:], lg[:, 0, :], [lgb], [mxb])
        for l in range(1, depth):
            tt("dve", mx[:], mx[:], lg[:, l, :], ALU.add, [mxb, lgb], [mxb])
        S.add("dve", lambda h: h.reciprocal(out=mx[:], in_=mx[:]), [mxb], [mxb])
        mset("dve", lbt[:, 0, :], 0.0, [lbb])
        for l in range(1, depth):
            tt("dve", lg[:, l, :], lg[:, l, :], mx[:], ALU.mult, [lgb, mxb], [lgb])
            tt("dve", lbt[:, l, :], lbt[:, l - 1, :], lg[:, l, :], ALU.add, [lbb, lgb], [lbb])
        ts("dve", omlt[:], lbt[:], -1.0, 1.0, ALU.mult, ALU.add, [lbb], [omlb])

        if on("P0"):
            xin_r = ring(2, [128, D], F32)
            xo_r = ring(2, [128, 8, 128], F32)
            for s, L in SL:
                for j in range(L // 128):
                    (xt, xb) = xin_r.next()
                    dma(xt, x_in[s][j * 128:(j + 1) * 128, :], [], [xb])
                    (xo, xob) = xo_r.next()
                    for half in range(2):
                        (pt, pb) = psf.next()
                        mmh(pt, pb, [(pt[:, c * 128:(c + 1) * 128], [(xt[:, (half * 4 + c) * 128:(half * 4 + c + 1) * 128], ident_f[:])]) for c in range(4)], [xb, ident_fb])
                        cp("act" if half == 0 else "dve", xo[:, half * 4:(half + 1) * 4, :], pt[:].rearrange("p (c t) -> p c t", c=4), [pb], [xob])
                    dma(XT[s][0].ap[:, :, j * 128:(j + 1) * 128].rearrange("c p t -> p c t"), xo, [xob], [XT[s][0].b(j * 128 // TA)])
            end_phase()

        for l in range(depth):
            par = l % 2
            g_pm, g_pmb = vec_cols(n_pre_mix[l], 8, f"g_pm{l}")
            g_po, g_pob = vec_cols(n_post_mix[l], 8, f"g_po{l}")
            g_pf, g_pfb = vec_cols(n_pre_ffn[l], 8, f"g_pf{l}")
            g_qf, g_qfb = vec_cols(n_post_ffn[l], 8, f"g_qf{l}")
            nwa, nwab = sb([128, 1], F32, f"nwa{l}")
            nwh, nwhb = sb([128, 1], F32, f"nwh{l}")
            dma(nwa[:], gdn_nw[l].rearrange("(p o) -> p o", o=1), [], [nwab], nonc=True)
            dma(nwh[:], hg_nw[l].rearrange("(p o) -> p o", o=1), [], [nwhb], nonc=True)
            cw, cwb = sb([128, 12, 5], F32, f"cw{l}")
            for j in range(5):
                dma(cw[:, :, j], conv_w[l, j].rearrange("(c p) -> p c", p=128), [], [cwb], nonc=True)
            alog, alogb = sb([8, 1], F32, f"alog{l}")
            dtb, dtbb = sb([8, 1], F32, f"dtb{l}")
            dma(alog[:], a_log[l].rearrange("(p o) -> p o", o=1), [], [alogb], nonc=True)
            dma(dtb[:], dt_bias[l].rearrange("(p o) -> p o", o=1), [], [dtbb], nonc=True)
            negA, negAb = sb([8, 1], F32, f"negA{l}")
            act(negA[:], alog[:], AF.Exp, [alogb], [negAb])
            ts("dve", negA[:], negA[:], -1.0, None, ALU.mult, None, [negAb], [negAb])

            if on("A1"):
                (war, wb) = acquire("A1", l)
                wA1 = war[:, 0:8 * 2064].rearrange("p (k c) -> p k c", k=8)
                prefetch("A2a", l)
                xT_r = ring(2, [128, 8, XH])
                sq_r = ring(1, [128, 8, XH], BF16)
                hT_r = ring(2, [128, 8, XH], BF16)
                rstd_r = ring(2, [128, XH])
                tmp_r = ring(2, [128, XH])
                cvin_r = ring(6, [128, XH])
                acc_r = ring(6, [128, TA])
                sil_r = ring(6, [128, TA])
                sqb_r = ring(6, [128, TA], BF16)
                rn_r = ring(6, [128, TA])
                ptmp_r = ring(2, [128, TA])
                qk_r = ring(2, [128, 8, TA], BF16)
                v_r = ring(2, [128, 4, TA], BF16)
                kvt_r = ring(2, [128, 1024], BF16)
                z_r = ring(2, [128, 4, TA], BF16)
                abt = {k: carve([8, TA]) for k in ("e", "sp", "g", "pfx", "tmp", "sfx")}
                gs_r = ring(2, [8, 3, TA])
                gt_r = ring(2, [128, 24])

                def a1_chunk(ci, ht, hb, qk, qkb, vs, vsb):
                    (pt, pb) = psf.next()
                    mm(pt[:, 0:XH], pb, [(wA1[:, kc, ci * 128:(ci + 1) * 128], ht[:, kc, :]) for kc in range(8)], [wb, hb])
                    (cv, cvb) = cvin_r.next()
                    cp("act", cv, pt[:, 0:XH], [pb], [cvb])
                    yield
                    (ac, acb) = acc_r.next()
                    if False:
                        ts("pool", ac, cv[:, 0:TA], cw[:, ci, 0:1], None, ALU.mult, None, [cvb, cwb], [acb])
                        for j in range(1, 5):
                            (ptm, ptmb) = ptmp_r.next()
                            ts("pool", ptm, cv[:, j:j + TA], cw[:, ci, j:j + 1], None, ALU.mult, None, [cvb, cwb], [ptmb])
                            tt("dve", ac, ac, ptm, ALU.add, [acb, ptmb], [acb])
                    else:
                        ts("dve", ac, cv[:, 0:TA], cw[:, ci, 0:1], None, ALU.mult, None, [cvb, cwb], [acb])
                        for j in range(1, 5):
                            stt(ac, cv[:, j:j + TA], cw[:, ci, j:j + 1], ac, ALU.mult, ALU.add, [cvb, cwb, acb], [acb])
                    yield
                    if ci < 8:
                        (sl, slb) = sil_r.next()
                        act(sl, ac, AF.Silu, [acb], [slb])
                        (sq2, sq2b) = sqb_r.next()
                        act(sq2, sl, AF.Square, [slb], [sq2b])
                        yield
                        (p2, p2b) = psf.next()
                        mm(p2[:, 0:TA], p2b, [(ones_h[:], sq2)], [ones_hb, sq2b])
                        yield
                        (rn, rnb) = rn_r.next()
                        act(rn, p2[:, 0:TA], AF.Sqrt, [p2b, epsb], [rnb], bias=eps_t[:, 0:1], scale=1.0)
                        yield
                        S.add("dve", lambda h, rn=rn: h.reciprocal(out=rn, in_=rn), [rnb], [rnb])
                        scl = float(DK ** -0.5) if ci < 4 else 1.0
                        stt(qk[:, ci, :], sl, scl, rn, ALU.mult, ALU.mult, [slb, rnb], [qkb])
                    else:
                        act(vs[:, ci - 8, :], ac, AF.Silu, [acb], [vsb])

                def a1_load(it):
                    s, L, blk = it
                    return x_load(XT[s][par], L, blk, True, xT_r)

                def a1_compute(it, cur):
                    s, L, blk = it
                    t0 = blk * TA
                    (xt, xb) = cur
                    ht, hb = norm_compute(xt, xb, g_pm, g_pmb, True, sq_r, hT_r, rstd_r, tmp_r)
                    dma(HN[s].ap[blk].rearrange("p (c t) -> p c t", c=8), ht[:, :, 2:2 + TA], [hb], [HN[s].b(blk)])
                    (qk, qkb) = qk_r.next()
                    (vs, vsb) = v_r.next()
                    run_chains([a1_chunk(ci, ht, hb, qk, qkb, vs, vsb) for ci in range(12)], window=5)
                    dma(GQK[s].ap[blk].rearrange("p (c t) -> p c t", c=8), qk, [qkb], [GQK[s].b(blk)])
                    for j in range(NCH):
                        (ph, phb) = psh.next()

                        def fn(h, ph=ph, qk=qk, vs=vs, j=j):
                            ins = None
                            for hh in range(4):
                                ins = h.transpose(out=ph[:, hh * 128:(hh + 1) * 128], in_=qk[:, 4 + hh, j * 128:(j + 1) * 128], identity=ident_h[:])
                            for hh in range(4):
                                ins = h.transpose(out=ph[:, 512 + hh * 128:512 + (hh + 1) * 128], in_=vs[:, hh, j * 128:(j + 1) * 128], identity=ident_h[:])
                            return ins
                        S.add("pe", fn, [qkb, vsb, ident_hb], [phb])
                        (kv, kvb) = kvt_r.next()
                        cp("act", kv, ph[:], [phb], [kvb])
                        r0 = t0 + j * 128
                        dma(GKV[s].ap[r0:r0 + 128, :], kv, [kvb], [GKV[s].b(r0 // 128)])
                    (zs, zsb) = z_r.next()
                    for hh in range(4):
                        (pt, pb) = psf.next()
                        mm(pt[:, 0:TA], pb, [(wA1[:, kc, O_Z + hh * 128:O_Z + (hh + 1) * 128], ht[:, kc, 2:2 + TA]) for kc in range(8)], [wb, hb])
                        act(zs[:, hh, :], pt[:, 0:TA], AF.Silu, [pb], [zsb])
                    dma(ZT[s].ap[blk].rearrange("p (c t) -> p c t", c=4), zs, [zsb], [ZT[s].b(blk)])
                    (pa, pab) = psf.next()
                    mm(pa[0:8, 0:TA], pab, [(wA1[:, kc, O_A:O_A + 8], ht[:, kc, 2:2 + TA]) for kc in range(8)], [wb, hb])
                    (pbb, pbbb) = psf.next()
                    mm(pbb[0:8, 0:TA], pbbb, [(wA1[:, kc, O_B:O_B + 8], ht[:, kc, 2:2 + TA]) for kc in range(8)], [wb, hb])
                    e_t, e_b = abt["e"]
                    act(e_t, pa[0:8, 0:TA], AF.Exp, [pab, dtbb], [e_b], bias=dtb[:])
                    sp_t, sp_b = abt["sp"]
                    act(sp_t, e_t, AF.Ln, [e_b, epsb], [sp_b], bias=eps_t[0:8, 1:2])
                    g_t, g_b = abt["g"]
                    ts("dve", g_t, sp_t, negA[:, 0:1], None, ALU.mult, None, [sp_b, negAb], [g_b])
                    pf_t, pf_b = abt["pfx"]
                    S.add("dve", lambda h, pf_t=pf_t, g_t=g_t: h.tensor_tensor_scan(out=pf_t, data0=m128[:], data1=g_t, initial=0.0, op0=ALU.mult, op1=ALU.add), [m128b, g_b], [pf_b])
                    tm_t, tm_b = abt["tmp"]
                    pf3 = pf_t.rearrange("p (c k) -> p c k", k=128)
                    tot_bc = pf3[:, :, 127:128].to_broadcast([8, NCH, 128])
                    tt("dve", tm_t.rearrange("p (c k) -> p c k", k=128), tot_bc, pf3, ALU.subtract, [pf_b], [tm_b])
                    sf_t, sf_b = abt["sfx"]
                    tt("dve", sf_t, tm_t, g_t, ALU.add, [tm_b, g_b], [sf_b])
                    (gs, gsb) = gs_r.next()
                    ts("dve", gs[:, 0, :], pf_t, sel[:, 0:1], None, ALU.mult, None, [pf_b, selb], [gsb])
                    stt(gs[:, 0, :], sf_t, sel[:, 1:2], gs[:, 0, :], ALU.mult, ALU.add, [sf_b, selb, gsb], [gsb])
                    act(gs[:, 1, :], pbb[0:8, 0:TA], AF.Sigmoid, [pbbb], [gsb])
                    tt("dve", gs[:, 2, :].rearrange("p (c k) -> p c k", k=128), tot_bc, gs[:, 0, :].rearrange("p (c k) -> p c k", k=128), ALU.subtract, [pf_b, gsb], [gsb])
                    dma(GS[s].ap[:, :, t0:t0 + TA].rearrange("q r t -> r q t"), gs, [gsb], [GS[s].b(blk)])
                    for j in range(NCH):
                        (pt, pb) = psf.next()
                        mmh(pt, pb, [(pt[:, q * 8:(q + 1) * 8], [(gs[:, q, j * 128:(j + 1) * 128], ident_f[0:8, 0:8])]) for q in range(3)], [gsb, ident_fb])
                        (gt, gtb) = gt_r.next()
                        cp("act", gt, pt[:, 0:24], [pb], [gtb])
                        r0 = t0 + j * 128
                        dma(GT[s].ap[r0:r0 + 128, :], gt, [gtb], [GT[s].b(r0 // 128)])

                pipeline([(s, L, blk) for s, L in SL for blk in range(L // TA)], a1_load, a1_compute)
                end_phase()

            if on("A2a"):
                (war, wb) = acquire("A2a", l)
                wA = war[:, 0:8 * 2048].rearrange("p (k c) -> p k c", k=8)
                prefetch("A2b", l)
                CQ, CF, CI = 0, 512, 1536
                hT_r = ring(2, [128, 8, TA], BF16)
                qs_r = ring(2, [128, 4, TA])
                ih_r = ring(2, [128, 4, TA], BF16)
                hq_r = [ring(2, [128, 4, TA], BF16) for _ in range(2)]
                hk_r = [ring(2, [128, 4, TA], BF16) for _ in range(2)]
                ab_r = ring(2, [128, 64])
                f_r = ring(50, [128, TA])
                tok_r = ring(2, [128, 1536], BF16)
                scl = float(DK ** -0.5)

                def a2_chain(d_, hh, ht, hb, qs, qsb, hq, hqb, hk, hkb, ab5, abb):
                    dh = d_ * 4 + hh
                    (pt, pb) = psf.next()
                    c0 = CF + d_ * 512 + hh * 128
                    mm(pt[:, 0:TA], pb, [(wA[:, kc, c0:c0 + 128], ht[:, kc, :]) for kc in range(8)], [wb, hb])
                    (sg, sgb) = f_r.next()
                    act(sg, pt[:, 0:TA], AF.Sigmoid, [pb], [sgb])
                    yield
                    (ff, ffb) = f_r.next()
                    ts("dve", ff, sg, omlt[:, l, dh:dh + 1], lbt[:, l, dh:dh + 1], ALU.mult, ALU.add, [sgb, omlb, lbb], [ffb])
                    yield
                    (lf, lfb) = f_r.next()
                    act(lf, ff, AF.Ln, [ffb], [lfb])
                    (kk, kkb) = f_r.next()
                    ts("dve", kk, ff, -1.0, 1.0, ALU.mult, ALU.add, [ffb], [kkb])
                    yield
                    (pf, pfb) = f_r.next()
                    S.add("dve", lambda h, pf=pf, lf=lf: h.tensor_tensor_scan(out=pf, data0=m64[:], data1=lf, initial=0.0, op0=ALU.mult, op1=ALU.add), [m64b, lfb], [pfb])
                    yield
                    if d_ == 0:
                        cum, cumb = pf, pfb
                        ri, ai = 31, 63
                    else:
                        pf3 = pf.rearrange("p (c k) -> p c k", k=64)
                        (tm, tmb) = f_r.next()
                        tt("dve", tm.rearrange("p (c k) -> p c k", k=64), pf3[:, :, 63:64].to_broadcast([128, NC64, 64]), pf3, ALU.subtract, [pfb], [tmb])
                        yield
                        (cum, cumb) = f_r.next()
                        tt("dve", cum, tm, lf, ALU.add, [tmb, lfb], [cumb])
                        yield
                        ri, ai = 32, 0
                    cum3 = cum.rearrange("p (c k) -> p c k", k=64)
                    (dd, ddb) = f_r.next()
                    tt("dve", dd.rearrange("p (c k) -> p c k", k=64), cum3, cum3[:, :, ri:ri + 1].to_broadcast([128, NC64, 64]), ALU.subtract, [cumb], [ddb])
                    yield
                    (eq, eqb) = f_r.next()
                    act(eq, dd, AF.Exp, [ddb], [eqb])
                    (ek, ekb) = f_r.next()
                    act(ek, dd, AF.Exp, [ddb], [ekb], scale=-1.0)
                    act(ab5[:, d_, 1, hh, :], cum3[:, :, ri], AF.Exp, [cumb], [abb])
                    yield
                    stt(hq[:, hh, :], qs[:, hh, :], scl, eq, ALU.mult, ALU.mult, [qsb, eqb], [hqb])
                    tt("dve", hk[:, hh, :], kk, ek, ALU.mult, [kkb, ekb], [hkb])
                    cp("dve", ab5[:, d_, 0, hh, :], eq.rearrange("p (c k) -> p c k", k=64)[:, :, ai], [eqb], [abb])

                def a2_load(it):
                    s, L, blk = it
                    (ht, hb) = hT_r.next()
                    dma(ht, HN[s].ap[blk].rearrange("p (c t) -> p c t", c=8), [HN[s].b(blk)], [hb])
                    return ht, hb

                def a2_compute(it, cur):
                    s, L, blk = it
                    t0 = blk * TA
                    (ht, hb) = cur
                    (qs, qsb) = qs_r.next()
                    (ih, ihb) = ih_r.next()
                    for hh in range(4):
                        (pt, pb) = psf.next()
                        mm(pt[:, 0:TA], pb, [(wA[:, kc, CQ + hh * 128:CQ + (hh + 1) * 128], ht[:, kc, :]) for kc in range(8)], [wb, hb])
                        act(qs[:, hh, :], pt[:, 0:TA], AF.Silu, [pb], [qsb])
                        (pt, pb) = psf.next()
                        mm(pt[:, 0:TA], pb, [(wA[:, kc, CI + hh * 128:CI + (hh + 1) * 128], ht[:, kc, :]) for kc in range(8)], [wb, hb])
                        cp("act", ih[:, hh, :], pt[:, 0:TA], [pb], [ihb])
                    (ab, abb) = ab_r.next()
                    ab5 = ab.rearrange("p (d q h c) -> p d q h c", d=2, q=2, h=4)
                    hqs = [hq_r[d_].next() for d_ in range(2)]
                    hks = [hk_r[d_].next() for d_ in range(2)]
                    run_chains([a2_chain(d_, hh, ht, hb, qs, qsb, hqs[d_][0], hqs[d_][1], hks[d_][0], hks[d_][1], ab5, abb) for d_ in range(2) for hh in range(4)], window=4)
                    for d_ in range(2):
                        dma(HQd[s].ap[d_, blk].rearrange("p (c t) -> p c t", c=4), hqs[d_][0], [hqs[d_][1]], [HQd[s].b((d_, blk))])
                        dma(HKd[s].ap[d_, blk].rearrange("p (c t) -> p c t", c=4), hks[d_][0], [hks[d_][1]], [HKd[s].b((d_, blk))])
                    dma(HAB[s].ap[blk], ab, [abb], [HAB[s].b(blk)])
                    for j in range(NCH):
                        (ph, phb) = psh.next()
                        (ph2, ph2b) = psh.next()

                        def fn(h, ph=ph, hks=hks, j=j):
                            ins = None
                            for d_ in range(2):
                                for hh in range(4):
                                    o = (d_ * 4 + hh) * 128
                                    ins = h.transpose(out=ph[:, o:o + 128], in_=hks[d_][0][:, hh, j * 128:(j + 1) * 128], identity=ident_h[:])
                            return ins
                        S.add("pe", fn, [hks[0][1], hks[1][1], ident_hb], [phb])

                        def fn2(h, ph2=ph2, ih=ih, j=j):
                            ins = None
                            for hh in range(4):
                                ins = h.transpose(out=ph2[:, hh * 128:(hh + 1) * 128], in_=ih[:, hh, j * 128:(j + 1) * 128], identity=ident_h[:])
                            return ins
                        S.add("pe", fn2, [ihb, ident_hb], [ph2b])
                        (tk, tkb) = tok_r.next()
                        cp("act", tk[:, 0:1024], ph[:], [phb], [tkb])
                        cp("dve", tk[:, 1024:1536], ph2[:, 0:512], [ph2b], [tkb])
                        r0 = t0 + j * 128
                        dma(HT[s].ap[r0:r0 + 128, :], tk, [tkb], [HT[s].b(r0 // 128)])

                pipeline([(s, L, blk) for s, L in SL for blk in range(L // TA)], a2_load, a2_compute)
                end_phase()

            if on("A2b"):
                (war, wb) = acquire("A2b", l)
                wA = war[:, 0:8 * 2560].rearrange("p (k c) -> p k c", k=8)
                hT_r = ring(2, [128, 8, TA], BF16)
                gh_r = ring(2, [128, 4, TA], BF16)
                mg_r = ring(2, [128, 8, TA], BF16)

                def a2b_load(it):
                    s, L, blk = it
                    (ht, hb) = hT_r.next()
                    dma(ht, HN[s].ap[blk].rearrange("p (c t) -> p c t", c=8), [HN[s].b(blk)], [hb])
                    return ht, hb

                def a2b_compute(it, cur):
                    s, L, blk = it
                    (ht, hb) = cur
                    (gh, ghb) = gh_r.next()
                    for hh in range(4):
                        (pt, pb) = psf.next()
                        mm(pt[:, 0:TA], pb, [(wA[:, kc, hh * 128:(hh + 1) * 128], ht[:, kc, :]) for kc in range(8)], [wb, hb])
                        act(gh[:, hh, :], pt[:, 0:TA], AF.Silu, [pb], [ghb])
                    dma(GHT[s].ap[blk].rearrange("p (c t) -> p c t", c=4), gh, [ghb], [GHT[s].b(blk)])
                    for half in range(2):
                        (mg, mgb) = mg_r.next()
                        for c in range(8):
                            c0 = 512 + (half * 8 + c) * 128
                            (pt, pb) = psf.next()
                            mm(pt[:, 0:TA], pb, [(wA[:, kc, c0:c0 + 128], ht[:, kc, :]) for kc in range(8)], [wb, hb])
                            if c % 2 == 0:
                                act(mg[:, c, :], pt[:, 0:TA], AF.Sigmoid, [pb], [mgb])
                            else:
                                act(mg[:, c, :], pt[:, 0:TA], AF.Sigmoid, [pb], [mgb])
                        dma(MG[s].ap[blk, :, half * 8 * TA:(half + 1) * 8 * TA].rearrange("p (c t) -> p c t", c=8), mg, [mgb], [MG[s].b((blk, half))])

                pipeline([(s, L, blk) for s, L in SL for blk in range(L // TA)], a2b_load, a2b_compute)
                end_phase()

            if on("B1"):
                def b1_chain(s, L, d_, wh):
                    nch = L // 128
                    mA, mBi, mBs = (1, 2, 3) if d_ == 0 else (3, 0, 1)
                    li = 127 if d_ == 0 else 0
                    S32, S32b = carve([128, 4, 128])
                    Sbf, Sbfb = carve([128, 4, 128], BF16)
                    mset("pool", S32, 0.0, [S32b])
                    mset("pool", Sbf, 0.0, [Sbfb])
                    qk_r = ring(2, [128, 8, 128], BF16)
                    os_r = ring(2, [128, 4, 128], BF16)
                    kv_r = ring(2, [128, 1024], BF16)
                    gt_r = ring(2, [128, 24])
                    gsb_r = ring(2, [128, 2, 4, 128])
                    sm_r = ring(6, [128, 4])
                    fm_r = ring(5, [128, 4, 128])
                    lf_r = ring(4, [128, 4, 128])
                    hm_r = ring(16, [128, 4, 128], BF16, where=wh)
                    iv_r = ring(12, [128, 4, 128], F32, where=wh)
                    order = list(range(nch)) if d_ == 0 else list(range(nch - 1, -1, -1))

                    def issue(c):
                        blk, cc = c // NCH, c % NCH
                        t0 = c * 128
                        (qk, qkb) = qk_r.next()
                        dma(qk, GQK[s].ap[blk].rearrange("p (c t) -> p c t", c=8)[:, :, cc * 128:(cc + 1) * 128], [GQK[s].b(blk)], [qkb])
                        (kv, kvb) = kv_r.next()
                        dma(kv, GKV[s].ap[t0:t0 + 128, :], [GKV[s].b(c)], [kvb])
                        (gt, gtb) = gt_r.next()
                        dma(gt, GT[s].ap[t0:t0 + 128, :], [GT[s].b(c)], [gtb])
                        (gsr, gsrb) = gsb_r.next()
                        for q_ in range(2):
                            gsap = AP(GS[s].ap.tensor, (q_ * 8 + d_ * 4) * L + t0, [[0, 128], [L, 4], [1, 128]])
                            dma(gsr[:, q_], gsap, [GS[s].b(t0 // TA)], [gsrb])
                        return qk, qkb, kv, kvb, gt, gtb, gsr, gsrb

                    nxt = issue(order[0])
                    for idx, c in enumerate(order):
                        (qk, qkb, kv, kvb, gt, gtb, gsr, gsrb) = nxt
                        if idx + 1 < len(order):
                            nxt = issue(order[idx + 1])
                        blk, cc = c // NCH, c % NCH
                        cumc = gt[:, d_ * 4:d_ * 4 + 4]
                        betac = gt[:, 8 + d_ * 4:12 + d_ * 4]
                        remc = gt[:, 16 + d_ * 4:20 + d_ * 4]
                        kT = qk[:, 4:8, :]
                        qT = qk[:, 0:4, :]
                        kv3k = kv[:, 0:512].rearrange("p (h d) -> p h d", h=4)
                        kv3v = kv[:, 512:1024].rearrange("p (h d) -> p h d", h=4)
                        (ecum, ecumb) = sm_r.next()
                        act(ecum, cumc, AF.Exp, [gtb], [ecumb])
                        (erem, eremb) = sm_r.next()
                        act(erem, remc, AF.Exp, [gtb], [eremb])
                        (erow, erowb) = lf_r.next()
                        act(erow, gsr[:, 0], AF.Exp, [gsrb], [erowb])
                        (pkk, pkkb) = psf.next()
                        mmh(pkk, pkkb, [(pkk[:, h_ * 128:(h_ + 1) * 128], [(kT[:, h_, :], kT[:, h_, :])]) for h_ in range(4)], [qkb])
                        (pqk, pqkb) = psf.next()
                        mmh(pqk, pqkb, [(pqk[:, h_ * 128:(h_ + 1) * 128], [(kT[:, h_, :], qT[:, h_, :])]) for h_ in range(4)], [qkb])
                        pkk3 = pkk[:].rearrange("p (h t) -> p h t", h=4)
                        pqk3 = pqk[:].rearrange("p (h t) -> p h t", h=4)
                        (da, dab) = fm_r.next()
                        stt(da, gsr[:, 0], -1.0, bc(cumc, 128), ALU.mult, ALU.add, [gsrb, gtb], [dab])
                        (db, dbb) = fm_r.next()
                        stt(db, bc(cumc, 128), -1.0, gsr[:, 0], ALU.mult, ALU.add, [gsrb, gtb], [dbb])
                        yield
                        (bec, becb) = sm_r.next()
                        tt("dve", bec, betac, ecum, ALU.mult, [gtb, ecumb], [becb])
                        tt("dve", da, da, bch(mask_f[:, 4 + mA, :], 4), ALU.add, [dab, mask_fb], [dab])
                        tt("dve", db, db, bch(mask_f[:, 4 + mBi, :], 4), ALU.add, [dbb, mask_fb], [dbb])
                        (bm, bmb) = fm_r.next()
                        tt("dve", bm, gsr[:, 1], bch(mask_f[:, mBs, :], 4), ALU.mult, [gsrb, mask_fb], [bmb])
                        yield
                        act(da, da, AF.Exp, [dab], [dab])
                        act(db, db, AF.Exp, [dbb], [dbb])
                        yield
                        (A0, A0b) = fm_r.next()
                        tt("dve", A0, pkk3, da, ALU.mult, [pkkb, dab], [A0b])
                        (B0, B0b) = fm_r.next()
                        tt("dve", B0, pkk3, db, ALU.mult, [pkkb, dbb], [B0b])
                        (qkT, qkTb) = hm_r.next()
                        tt("dve", qkT, pqk3, db, ALU.mult, [pqkb, dbb], [qkTb])
                        yield
                        (Am, Amb) = iv_r.next()
                        tt("dve", Am, A0, bc(betac, 128), ALU.mult, [A0b, gtb], [Amb])
                        (Bm, Bmb) = iv_r.next()
                        tt("dve", Bm, B0, bm, ALU.mult, [B0b, bmb], [Bmb])
                        yield
                        (Y, Yb) = iv_r.next()
                        tt("dve", Y, bch(ident_f[:], 4), Bm, ALU.subtract, [ident_fb, Bmb], [Yb])
                        (bv, bvb) = hm_r.next()
                        tt("dve", bv, kv3v, bc(betac, 128), ALU.mult, [kvb, gtb], [bvb])
                        (kbe, kbeb) = hm_r.next()
                        tt("dve", kbe, kv3k, bc(bec, 128), ALU.mult, [kvb, becb], [kbeb])
                        (kdt, kdtb) = hm_r.next()
                        tt("dve", kdt, kv3k, bc(erem, 128), ALU.mult, [kvb, eremb], [kdtb])
                        (qdT, qdTb) = hm_r.next()
                        tt("dve", qdT, qT, erow, ALU.mult, [qkb, erowb], [qdTb])
                        yield
                        P, Pb, Pt, Ptb = Am, Amb, Bm, Bmb
                        prevP = None
                        for k in range(1, 7):
                            (pp, ppb) = psf.next()
                            mmh(pp, ppb, [(pp[:, h_ * 128:(h_ + 1) * 128], [(Pt[:, h_, :], P[:, h_, :])]) for h_ in range(4)], [Pb, Ptb])
                            if k < 6:
                                (pp2, pp2b) = psf.next()
                                mmh(pp2, pp2b, [(pp2[:, h_ * 128:(h_ + 1) * 128], [(P[:, h_, :], Pt[:, h_, :])]) for h_ in range(4)], [Pb, Ptb])
                            if prevP is not None:
                                (pp3, pp3b) = psf.next()
                                mmh(pp3, pp3b, [(pp3[:, h_ * 128:(h_ + 1) * 128], [(prevP[0][:, h_, :], Y[:, h_, :])]) for h_ in range(4)], [prevP[1], Yb])
                            yield
                            (Pn, Pnb) = iv_r.next()
                            cp("act", Pn, pp[:].rearrange("p (h t) -> p h t", h=4), [ppb], [Pnb])
                            if k < 6:
                                (Ptn, Ptnb) = iv_r.next()
                                cp("act", Ptn, pp2[:].rearrange("p (h t) -> p h t", h=4), [pp2b], [Ptnb])
                            if prevP is not None:
                                (Yn, Ynb) = iv_r.next()
                                tt("dve", Yn, pp3[:].rearrange("p (h t) -> p h t", h=4), Y, ALU.add, [pp3b, Yb], [Ynb])
                                Y, Yb = Yn, Ynb
                            yield
                            prevP = (Pn, Pnb)
                            if k < 6:
                                P, Pb, Pt, Ptb = Pn, Pnb, Ptn, Ptnb
                        (pp3, pp3b) = psf.next()
                        mmh(pp3, pp3b, [(pp3[:, h_ * 128:(h_ + 1) * 128], [(prevP[0][:, h_, :], Y[:, h_, :])]) for h_ in range(4)], [prevP[1], Yb])
                        yield
                        (TTbf, TTb) = hm_r.next()
                        tt("dve", TTbf, pp3[:].rearrange("p (h t) -> p h t", h=4), Y, ALU.add, [pp3b, Yb], [TTb])
                        yield
                        (pu, pub) = psf.next()
                        mmh(pu, pub, [(pu[:, h_ * 128:(h_ + 1) * 128], [(TTbf[:, h_, :], bv[:, h_, :])]) for h_ in range(4)], [TTb, bvb])
                        (pw, pwb) = psf.next()
                        mmh(pw, pwb, [(pw[:, h_ * 128:(h_ + 1) * 128], [(kbe[:, h_, :], TTbf[:, h_, :])]) for h_ in range(4)], [TTb, kbeb])
                        yield
                        (u, ub) = lf_r.next()
                        cp("act", u, pu[:].rearrange("p (h t) -> p h t", h=4), [pub], [ub])
                        (wT, wTb) = hm_r.next()
                        cp("act", wT, pw[:].rearrange("p (h t) -> p h t", h=4), [pwb], [wTb])
                        yield
                        (pv, pvb) = psf.next()
                        mmh(pv, pvb, [(pv[:, h_ * 128:(h_ + 1) * 128], [(wT[:, h_, :], Sbf[:, h_, :])]) for h_ in range(4)], [wTb, Sbfb])
                        yield
                        (vn, vnb) = hm_r.next()
                        tt("dve", vn, u, pv[:].rearrange("p (h t) -> p h t", h=4), ALU.subtract, [ub, pvb], [vnb])
                        yield
                        (po, pob) = psf.next()
                        mmh(po, pob, [(po[:, h_ * 128:(h_ + 1) * 128], [(Sbf[:, h_, :], qdT[:, h_, :]), (vn[:, h_, :], qkT[:, h_, :])]) for h_ in range(4)], [Sbfb, qdTb, vnb, qkTb])
                        (ps_, psb) = psf.next()
                        mmh(ps_, psb, [(ps_[:, h_ * 128:(h_ + 1) * 128], [(kdt[:, h_, :], vn[:, h_, :])]) for h_ in range(4)], [kdtb, vnb])
                        yield
                        (ost, ostb) = os_r.next()
                        cp("act", ost, po[:].rearrange("p (h t) -> p h t", h=4), [pob], [ostb])
                        dma(OA[s].ap[d_, blk].rearrange("p (c t) -> p c t", c=4)[:, :, cc * 128:(cc + 1) * 128], ost, [ostb], [OA[s].b((d_, blk))])
                        for h_ in range(4):
                            stt(S32[:, h_, :], S32[:, h_, :], erow[:, h_, li:li + 1], ps_[:, h_ * 128:(h_ + 1) * 128], ALU.mult, ALU.add, [S32b, erowb, psb], [S32b])
                        yield
                        cp("act", Sbf, S32, [S32b], [Sbfb])
                        yield

                for s, L in SL:
                    run_chains([b1_chain(s, L, 0, "w0"), b1_chain(s, L, 1, "w1")])
                    end_phase()

            if on("B2"):
                prefetch("C1", l)

                def b2_chain(s, L, d_):
                    n64 = L // 64
                    nb = L // TA
                    mI = 2 if d_ == 0 else 0
                    abt_, abtb = carve([128, nb, 64])
                    dma(abt_, HAB[s].ap.rearrange("b p x -> p b x"), [HAB[s].b(k) for k in range(nb)], [abtb])
                    ab6 = abt_.rearrange("p b (d q h c) -> p b d q h c", d=2, q=2, h=4)
                    Aall, Aallb = carve([128, 4, n64])
                    Ball, Ballb = carve([128, 4, n64])
                    rr, rrb = carve([128, 4, n64])
                    for h_ in range(4):
                        cp("dve", Aall[:, h_, :].rearrange("p (b c) -> p b c", c=NC64), ab6[:, :, d_, 0, h_, :], [abtb], [Aallb])
                        cp("dve", Ball[:, h_, :].rearrange("p (b c) -> p b c", c=NC64), ab6[:, :, d_, 1, h_, :], [abtb], [Ballb])
                    if d_ == 0:
                        tt("dve", rr[:, :, 0:n64 - 1], Aall[:, :, 0:n64 - 1], Ball[:, :, 1:n64], ALU.mult, [Aallb, Ballb], [rrb])
                    else:
                        tt("dve", rr[:, :, 1:n64], Aall[:, :, 1:n64], Ball[:, :, 0:n64 - 1], ALU.mult, [Aallb, Ballb], [rrb])
                    S32, S32b = carve([128, 4, 128])
                    Sbf, Sbfb = carve([128, 4, 128], BF16)
                    mset("pool", S32, 0.0, [S32b])
                    mset("pool", Sbf, 0.0, [Sbfb])
                    qd_r = ring(2, [128, 4, 64], BF16)
                    kd_r = ring(2, [128, 4, 64], BF16)
                    os_r = ring(2, [128, 4, 64], BF16)
                    tok_r = ring(3, [64, 1536], BF16)
                    am_r = ring(2, [64, 4, 64], BF16)
                    tmp_r = ring(2, [128, 4, 128])
                    order = list(range(n64)) if d_ == 0 else list(range(n64 - 1, -1, -1))

                    def issue(c):
                        blk, cc = c // NC64, c % NC64
                        (qd, qdb) = qd_r.next()
                        dma(qd, HQd[s].ap[d_, blk].rearrange("p (c t) -> p c t", c=4)[:, :, cc * 64:(cc + 1) * 64], [HQd[s].b((d_, blk))], [qdb])
                        (kd, kdb) = kd_r.next()
                        dma(kd, HKd[s].ap[d_, blk].rearrange("p (c t) -> p c t", c=4)[:, :, cc * 64:(cc + 1) * 64], [HKd[s].b((d_, blk))], [kdb])
                        (tk, tkb) = tok_r.next()
                        dma(tk, HT[s].ap[c * 64:(c + 1) * 64, :], [HT[s].b(c // 2)], [tkb])
                        return qd, qdb, kd, kdb, tk, tkb

                    nxt = issue(order[0])
                    for ci_, c in enumerate(order):
                        (qd, qdb, kd, kdb, tk, tkb) = nxt
                        if ci_ + 1 < n64:
                            nxt = issue(order[ci_ + 1])
                        blk, cc = c // NC64, c % NC64
                        if d_ == 0:
                            wtick(1)
                        (pa, pab) = psf.next()
                        mmh(pa, pab, [(pa[0:64, h_ * 64:(h_ + 1) * 64], [(kd[:, h_, :], qd[:, h_, :])]) for h_ in range(4)], [kdb, qdb])
                        if ci_ < n64 - 1:
                            (pp, ppb) = psf.next()
                            mmh(pp, ppb, [(pp[:, h_ * 128:(h_ + 1) * 128], [(tk[:, d_ * 512 + h_ * 128:d_ * 512 + (h_ + 1) * 128], tk[:, 1024 + h_ * 128:1024 + (h_ + 1) * 128])]) for h_ in range(4)], [tkb])
                        yield
                        (am, amb) = am_r.next()
                        tt("dve", am, pa[0:64, 0:256].rearrange("p (h t) -> p h t", h=4), bch(mask_f[0:64, mI, 0:64], 4), ALU.mult, [pab, mask_fb], [amb])
                        yield
                        (po, pob) = psf.next()
                        mmh(po, pob, [(po[:, h_ * 64:(h_ + 1) * 64], [(Sbf[:, h_, :], qd[:, h_, :]), (tk[:, 1024 + h_ * 128:1024 + (h_ + 1) * 128], am[:, h_, :])]) for h_ in range(4)], [Sbfb, qdb, tkb, amb])
                        yield
                        (ost, ostb) = os_r.next()
                        cp("act", ost, po[:, 0:256].rearrange("p (h t) -> p h t", h=4), [pob], [ostb])
                        dma(OB[s].ap[d_, blk].rearrange("p (c t) -> p c t", c=4)[:, :, cc * 64:(cc + 1) * 64], ost, [ostb], [OB[s].b((d_, blk))])
                        if ci_ < n64 - 1:
                            (tm, tmb) = tmp_r.next()
                            tt("dve", tm, pp[:].rearrange("p (h t) -> p h t", h=4), S32, ALU.add, [ppb, S32b], [tmb])
                            yield
                            tt("dve", S32, tm, bc(rr[:, :, c], 128), ALU.mult, [tmb, rrb], [S32b])
                            tt("dve", Sbf, tm, bc(rr[:, :, c], 128), ALU.mult, [tmb, rrb], [Sbfb])
                        yield

                for s, L in SL:
                    run_chains([b2_chain(s, L, 0), b2_chain(s, L, 1)])
                    end_phase()

            if on("C1"):
                (war, wb) = acquire("C1", l)
                wbg_ = war[:, 0:4096].rearrange("p (k c) -> p k c", k=4)
                wbh_ = war[:, 4096:8192].rearrange("p (k c) -> p k c", k=4)
                wo_ = war[:, 8192:16384].rearrange("p (k c) -> p k c", k=8)
                in_r = ring(12, [128, 4, TA], BF16, where=('w1' if war is warena[0][0] else 'w0'))
                mg_r = ring(4, [128, 8, TA], BF16)
                x_r = ring(2, [128, 8, TA])
                o_r = ring(2, [128, 4, TA])
                rs4_r = ring(2, [128, 4, TA])
                sq4_r = ring(2, [128, 4, TA], BF16)
                onz_r = ring(2, [128, 4, TA], BF16)
                mrg_r = ring(1, [128, 8, TA], BF16)
                f_r = ring(6, [128, TA])
                mo_r = ring(1, [128, 8, TA])
                sq8_r = ring(1, [128, 8, TA], BF16)
                res_r = ring(1, [128, 8, TA])
                rstd_r = ring(2, [128, TA])
                tmp_r = ring(2, [128, TA])
                def c1_load(it):
                    s, L, blk = it
                    ins_ = []
                    for (OX, GX) in ((OA, ZT), (OB, GHT)):
                        for (dt_, key) in ((OX[s].ap[0, blk], OX[s].b((0, blk))), (OX[s].ap[1, blk], OX[s].b((1, blk))), (GX[s].ap[blk], GX[s].b(blk))):
                            (t_, tb_) = in_r.next()
                            dma(t_, dt_.rearrange("p (c t) -> p c t", c=4), [key], [tb_])
                            ins_.append((t_, tb_))
                    mgs = []
                    for half in range(2):
                        (mg, mgb) = mg_r.next()
                        dma(mg, MG[s].ap[blk, :, half * 8 * TA:(half + 1) * 8 * TA].rearrange("p (c t) -> p c t", c=8), [MG[s].b((blk, half))], [mgb])
                        mgs.append((mg, mgb))
                    (xt, xb) = x_r.next()
                    dma(xt, XT[s][par].ap[:, :, blk * TA:(blk + 1) * TA].rearrange("c p t -> p c t"), [XT[s][par].b(blk)], [xb])
                    return ins_, mgs, xt, xb

                def c1_compute(it, cur):
                    s, L, blk = it
                    (ins_, mgs, xt, xb) = cur
                    onz = [None, None]

                    def c1_branch(bi, nw_, nwb_):
                        (of, ofb), (ob_, obb), (gz, gzb) = ins_[bi * 3:bi * 3 + 3]
                        (o, ob2) = o_r.next()
                        tt("dve", o, of, ob_, ALU.add, [ofb, obb], [ob2])
                        yield
                        (sq4, sq4b) = sq4_r.next()
                        act(sq4, o, AF.Square, [ob2], [sq4b])
                        yield
                        (rs4, rs4b) = rs4_r.next()
                        pts = []
                        for hp in range(2):
                            (pt, pb) = psf.next()
                            mmh(pt, pb, [(pt[:, q * TA:(q + 1) * TA], [(ones_h[:], sq4[:, hp * 2 + q, :])]) for q in range(2)], [ones_hb, sq4b])
                            pts.append((pt, pb))
                        yield
                        for hp in range(2):
                            (pt, pb) = pts[hp]
                            act(rs4[:, hp * 2:hp * 2 + 2, :], pt[:].rearrange("p (h t) -> p h t", h=2), AF.Sqrt, [pb, epsb], [rs4b], bias=eps_t[:, 0:1], scale=1.0 / 128)
                        yield
                        S.add("dve", lambda h, rs4=rs4: h.reciprocal(out=rs4, in_=rs4), [rs4b], [rs4b])
                        yield
                        stt(o, o, nw_[:, 0:1], rs4, ALU.mult, ALU.mult, [ob2, nwb_, rs4b], [ob2])
                        yield
                        (oz, ozb) = onz_r.next()
                        tt("dve", oz, o, gz, ALU.mult, [ob2, gzb], [ozb])
                        onz[bi] = (oz, ozb)

                    run_chains([c1_branch(0, nwa, nwab), c1_branch(1, nwh, nwhb)])
                    (mrg, mrgb) = mrg_r.next()
                    for oc in range(8):
                        (pa, pab) = psf.next()
                        mm(pa[:, 0:TA], pab, [(wbg_[:, kc, oc * 128:(oc + 1) * 128], onz[0][0][:, kc, :]) for kc in range(4)], [wb, onz[0][1]])
                        (pb_, pbb) = psf.next()
                        mm(pb_[:, 0:TA], pbb, [(wbh_[:, kc, oc * 128:(oc + 1) * 128], onz[1][0][:, kc, :]) for kc in range(4)], [wb, onz[1][1]])
                        (t1, t1b) = f_r.next()
                        tt("dve", t1, pa[:, 0:TA], mgs[0][0][:, oc, :], ALU.mult, [pab, mgs[0][1]], [t1b])
                        (t2, t2b) = f_r.next()
                        tt("dve", t2, pb_[:, 0:TA], mgs[1][0][:, oc, :], ALU.mult, [pbb, mgs[1][1]], [t2b])
                        tt("dve", mrg[:, oc, :], t1, t2, ALU.add, [t1b, t2b], [mrgb])
                    (mo, mob) = mo_r.next()
                    (sq8, sq8b) = sq8_r.next()
                    for oc in range(8):
                        (pt, pb) = psf.next()
                        mm(pt[:, 0:TA], pb, [(wo_[:, kc, oc * 128:(oc + 1) * 128], mrg[:, kc, :]) for kc in range(8)], [wb, mrgb])
                        cp("act", mo[:, oc, :], pt[:, 0:TA], [pb], [mob])
                        act(sq8[:, oc, :], pt[:, 0:TA], AF.Square, [pb], [sq8b])
                    res, resb = post_norm_add(res_r, xt, xb, mo, mob, sq8, sq8b, g_po, g_pob, rstd_r, tmp_r, f_r)
                    dma(XM[s].ap[:, :, blk * TA:(blk + 1) * TA].rearrange("c p t -> p c t"), res, [resb], [XM[s].b(blk)])

                pipeline([(s, L, blk) for s, L in SL for blk in range(L // TA)], c1_load, c1_compute)
                end_phase()

            for half in range(2):
                if not on("C2a"):
                    continue
                (war, wb) = acquire(f"C2a{half}", l)
                NF = 11 * 128
                wg_ = war[:, 0:8 * NF].rearrange("p (k c) -> p k c", k=8)
                wu_ = war[:, 8 * NF:16 * NF].rearrange("p (k c) -> p k c", k=8)
                prefetch("C2a1" if half == 0 else "C2b", l)
                xT_r = ring(2, [128, 8, TA])
                sq_r = ring(1, [128, 8, TA], BF16)
                hT_r = ring(2, [128, 8, TA], BF16)
                rstd_r = ring(2, [128, TA])
                tmp_r = ring(2, [128, TA])
                sg_r = ring(4, [128, TA])
                a_r = ring(2, [128, 11, TA], BF16)

                def c2a_load(it, half=half):
                    s, L, blk = it
                    if half == 0:
                        return x_load(XM[s], L, blk, False, xT_r)
                    (ht, hb) = hT_r.next()
                    dma(ht, H2[s].ap[blk].rearrange("p (c t) -> p c t", c=8), [H2[s].b(blk)], [hb])
                    return ht, hb

                def c2a_compute(it, cur, half=half):
                    s, L, blk = it
                    if half == 0:
                        ht, hb = norm_compute(cur[0], cur[1], g_pf, g_pfb, False, sq_r, hT_r, rstd_r, tmp_r)
                        dma(H2[s].ap[blk].rearrange("p (c t) -> p c t", c=8), ht, [hb], [H2[s].b(blk)])
                    else:
                        (ht, hb) = cur
                    (a_, ab_) = a_r.next()
                    for fc in range(11):
                        (pg, pgb) = psf.next()
                        mm(pg[:, 0:TA], pgb, [(wg_[:, kc, fc * 128:(fc + 1) * 128], ht[:, kc, 0:TA]) for kc in range(8)], [wb, hb])
                        (pu, pub) = psf.next()
                        mm(pu[:, 0:TA], pub, [(wu_[:, kc, fc * 128:(fc + 1) * 128], ht[:, kc, 0:TA]) for kc in range(8)], [wb, hb])
                        (sg, sgb) = sg_r.next()
                        act(sg, pg[:, 0:TA], AF.Silu, [pgb], [sgb])
                        tt("dve", a_[:, fc, :], sg, pu[:, 0:TA], ALU.mult, [sgb, pub], [ab_])
                    dma(ACTS[s].ap[blk, :, half * 11 * TA:(half + 1) * 11 * TA].rearrange("p (c t) -> p c t", c=11), a_, [ab_], [ACTS[s].b((blk, half))])

                pipeline([(s, L, blk) for s, L in SL for blk in range(L // TA)], c2a_load, c2a_compute)
                end_phase()

            if on("C2b"):
                (war, wb) = acquire("C2b", l)
                wd_ = war[:, 0:22 * 1024].rearrange("p (k c) -> p k c", k=22)
                prefetch("A1", l + 1)
                a_r = ring(2, [128, 22, TA], BF16)
                x_r = ring(2, [128, 8, TA])
                ff_r = ring(1, [128, 8, TA])
                sq8_r = ring(1, [128, 8, TA], BF16)
                res_r = ring(2, [128, 8, TA])
                rstd_r = ring(2, [128, TA])
                tmp_r = ring(2, [128, TA])
                f_r = ring(4, [128, TA])
                yt_r = ring(2, [128, D])
                last = (l == depth - 1)

                def c2b_load(it):
                    s, L, blk = it
                    (a_, ab_) = a_r.next()
                    dma(a_, ACTS[s].ap[blk].rearrange("p (c t) -> p c t", c=22), [ACTS[s].b((blk, 0)), ACTS[s].b((blk, 1))], [ab_])
                    (xt, xb) = x_r.next()
                    dma(xt, XM[s].ap[:, :, blk * TA:(blk + 1) * TA].rearrange("c p t -> p c t"), [XM[s].b(blk)], [xb])
                    return a_, ab_, xt, xb

                def c2b_compute(it, cur):
                    s, L, blk = it
                    (a_, ab_, xt, xb) = cur
                    (ffo, ffob) = ff_r.next()
                    (sq8, sq8b) = sq8_r.next()
                    for oc in range(8):
                        (pt, pb) = psf.next()
                        mm(pt[:, 0:TA], pb, [(wd_[:, fc, oc * 128:(oc + 1) * 128], a_[:, fc, :]) for fc in range(22)], [wb, ab_])
                        cp("act", ffo[:, oc, :], pt[:, 0:TA], [pb], [ffob])
                        act(sq8[:, oc, :], pt[:, 0:TA], AF.Square, [pb], [sq8b])
                    res, resb = post_norm_add(res_r, xt, xb, ffo, ffob, sq8, sq8b, g_qf, g_qfb, rstd_r, tmp_r, f_r)
                    if not last:
                        dma(XT[s][1 - par].ap[:, :, blk * TA:(blk + 1) * TA].rearrange("c p t -> p c t"), res, [resb], [XT[s][1 - par].b(blk)])
                    else:
                        for j in range(NCH):
                            (yt, ytb) = yt_r.next()
                            for hf in range(2):
                                (pt, pb) = psf.next()
                                mmh(pt, pb, [(pt[:, c * 128:(c + 1) * 128], [(res[:, hf * 4 + c, j * 128:(j + 1) * 128], ident_f[:])]) for c in range(4)], [resb, ident_fb])
                                cp("act" if hf == 0 else "dve", yt[:, hf * 512:(hf + 1) * 512], pt[:], [pb], [ytb])
                            r0 = blk * TA + j * 128
                            dma(y_out[s].ap[r0:r0 + 128, :], yt, [ytb], [y_out[s].b(r0 // 128)])

                pipeline([(s, L, blk) for s, L in SL for blk in range(L // TA)], c2b_load, c2b_compute)
                end_phase()
        S.emit(es)
    return nc


def host_consts():
    ident = np.eye(128, dtype=np.float32)
    p = np.arange(128)[:, None]
    f = np.arange(128)[None, :]
    m = np.zeros((128, 8, 128), np.float32)
    m[:, 0] = (f <= p)
    m[:, 1] = (f < p)
    m[:, 2] = (f >= p)
    m[:, 3] = (f > p)
    m[:, 4:8] = (m[:, 0:4] - 1.0) * 30000.0
    sel = np.zeros((8, 2), np.float32)
    sel[0:4, 0] = 1.0
    sel[4:8, 1] = 1.0
    return {"c_ident": ident, "c_mask": m, "c_sel": sel}


W_KEYS = ["w_in", "conv_w", "gdn_norm_w", "hgrn_norm_w", "w_branch_gdn", "w_branch_hgrn", "w_out", "norm_pre_mix",
          "norm_post_mix", "norm_pre_ffn", "norm_post_ffn", "w_ffn_gate", "w_ffn_up", "w_ffn_down", "hgrn_lb_logits"]


def make_in_map(inputs, xs, depth):
    m = {f"x{s}": np.ascontiguousarray(x, dtype=np.float32) for s, x in enumerate(xs)}
    for k in W_KEYS:
        m[k] = np.ascontiguousarray(np.asarray(inputs[k], dtype=np.float32)[:depth])
    m["gdn_a_log"] = np.ascontiguousarray(np.asarray(inputs["gdn_a_log"], dtype=np.float32)[:depth].reshape(depth, 8))
    m["gdn_dt_bias"] = np.ascontiguousarray(np.asarray(inputs["gdn_dt_bias"], dtype=np.float32)[:depth].reshape(depth, 8))
    m.update(host_consts())
    return m


_NC_CACHE = {}


def kernel(**inputs):
    xp = np.asarray(inputs["x_prompt"], dtype=np.float32)
    xs = np.asarray(inputs["x_sample"], dtype=np.float32)
    depth = int(np.asarray(inputs["w_in"]).shape[0])
    seq_lens = (xp.shape[1], xs.shape[1])
    key = (seq_lens, depth)
    if key not in _NC_CACHE:
        _NC_CACHE[key] = build(list(seq_lens), depth)
    nc = _NC_CACHE[key]
    in_maps = [make_in_map(inputs, [xp[c], xs[c]], depth) for c in range(NCORES)]
    res = run_bass_kernel_spmd(nc, in_maps, core_ids=list(range(NCORES)))
    yp = np.stack([np.asarray(res.results[c]["y0"], dtype=np.float32) for c in range(NCORES)], axis=0)
    ys = np.stack([np.asarray(res.results[c]["y1"], dtype=np.float32) for c in range(NCORES)], axis=0)
    return (yp, ys)
```

```python
import numpy as np
from contextlib import ExitStack
import concourse.bass as bass
import concourse.mybir as mybir
from concourse.ap import AP
from concourse.bass_utils import run_bass_kernel_spmd

F32 = mybir.dt.float32
BF16 = mybir.dt.bfloat16
AF = mybir.ActivationFunctionType
ALU = mybir.AluOpType
AX = mybir.AxisListType

D = 1024
NH = 4
DK = 128
DFF = 2816
NFF = DFF // 128
INW = 6672
EPS = 1e-6
O_Q, O_K, O_V, O_Z, O_A, O_B, O_QH, O_FH, O_IH, O_GH, O_GATE = 0, 512, 1024, 1536, 2048, 2056, 2064, 2576, 3600, 4112, 4624
TA = 256
NCORES = 8


class Buf:
    __slots__ = ("name", "last_w", "rd_eng", "rd_dma")

    def __init__(self, name=""):
        self.name = name
        self.last_w = None
        self.rd_eng = {}
        self.rd_dma = []


class Op:
    __slots__ = ("eng", "fn", "idx", "marked", "dma", "deps", "sem", "val")

    def __init__(self, eng, fn, dma):
        self.eng = eng
        self.fn = fn
        self.dma = dma
        self.marked = False
        self.deps = ()
        self.sem = None
        self.val = 0


class Sched:
    ENGS = ("pe", "act", "dve", "pool", "sp")
    GEN = 30000
    NDMA = 40

    def __init__(self, nc):
        self.nc = nc
        self.streams = {e: [] for e in self.ENGS}
        self.seen = {e: {} for e in self.ENGS}
        self.seen_dma = {e: set() for e in self.ENGS}
        self.ndma = 0
        self.dma_ops = []
        self.pending = {e: None for e in self.ENGS}

    def barrier(self):
        deps = []
        for e in ("pe", "act", "dve", "pool"):
            if self.streams[e]:
                deps.append(self.streams[e][-1])
        deps.extend(self.dma_ops[-self.NDMA:])
        for e in self.ENGS:
            self.pending[e] = list(deps) + (self.pending[e] or [])

    def add(self, eng, fn, reads=(), writes=(), dma=False):
        op = Op(eng, fn, dma)
        st = self.streams[eng]
        op.idx = len(st)
        st.append(op)
        deps = []
        if self.pending[eng] is not None:
            deps.extend(self.pending[eng])
            self.pending[eng] = None
        for b in reads:
            if b.last_w is not None:
                deps.append(b.last_w)
        for b in writes:
            if b.last_w is not None:
                deps.append(b.last_w)
            deps.extend(b.rd_eng.values())
            deps.extend(b.rd_dma)
        need = {}
        dma_need = []
        seen = self.seen[eng]
        sdma = self.seen_dma[eng]
        for d in deps:
            if d is op:
                continue
            if d.dma:
                if d not in sdma:
                    sdma.add(d)
                    dma_need.append(d)
            else:
                if d.eng == "pe" and eng == "pe" and not dma:
                    continue
                if seen.get(d.eng, -1) >= d.idx:
                    continue
                if need.get(d.eng, -1) < d.idx:
                    need[d.eng] = d.idx
        wl = []
        for e2, k in need.items():
            seen[e2] = k
            dop = self.streams[e2][k]
            dop.marked = True
            wl.append(dop)
        op.deps = wl + dma_need
        if dma:
            i = self.ndma
            self.ndma += 1
            self.dma_ops.append(op)
            if i >= self.NDMA:
                prev = self.dma_ops[i - self.NDMA]
                if prev not in sdma:
                    sdma.add(prev)
                    op.deps.append(prev)
        for b in reads:
            if dma:
                b.rd_dma.append(op)
            else:
                b.rd_eng[eng] = op
        for b in writes:
            b.last_w = op
            b.rd_eng = {}
            b.rd_dma = []
        return op

    def emit(self, es):
        nc = self.nc
        for e in self.ENGS:
            nmark = sum(1 for o in self.streams[e] if o.marked and not o.dma)
            ngen = max(1, -(-nmark // self.GEN))
            sems = [es.enter_context(nc.semaphore(f"s_{e}_{g}")) for g in range(ngen)]
            c = 0
            for o in self.streams[e]:
                if o.marked and not o.dma:
                    o.sem = sems[c // self.GEN]
                    o.val = c % self.GEN + 1
                    c += 1
        nd = min(self.NDMA, max(1, self.ndma))
        dsems = [es.enter_context(nc.semaphore(f"s_dma_{i}")) for i in range(nd)]
        final_dma = {}
        for i, o in enumerate(self.dma_ops):
            o.sem = dsems[i % self.NDMA]
            o.val = 16 * (i // self.NDMA + 1)
            final_dma[i % self.NDMA] = o.val

        def run(e, h, last=False):
            for o in self.streams[e]:
                for d in o.deps:
                    h.wait_ge(d.sem, d.val)
                ins = o.fn(h)
                if o.dma:
                    ins.then_inc(o.sem, 16)
                elif o.marked:
                    ins.then_inc(o.sem, 1)
            if last:
                for i, v in final_dma.items():
                    h.wait_ge(dsems[i], v)

        block = es.enter_context(nc.Block())

        @block.tensor
        def _(h):
            run("pe", h)

        @block.scalar
        def _(h):
            run("act", h)

        @block.vector
        def _(h):
            run("dve", h)

        @block.gpsimd
        def _(h):
            run("pool", h)

        @block.sync
        def _(h):
            run("sp", h, last=True)


class Ring:
    def __init__(self, tiles):
        self.tiles = tiles
        self.i = 0

    def next(self):
        t = self.tiles[self.i % len(self.tiles)]
        self.i += 1
        return t


class DT:
    def __init__(self, ap):
        self.ap = ap
        self.bufs = {}

    def b(self, key):
        r = self.bufs.get(key)
        if r is None:
            r = self.bufs[key] = Buf()
        return r


FA_N = 15360
HA_N = 22528
WAR_N = 22528


def build(seq_lens, depth, dbg=(), phases=None):
    nc = bass.Bass("TRN2", target_bir_lowering=False)
    nseq = len(seq_lens)
    XH = TA + 4
    NCH = TA // 128
    NC64 = TA // 64

    def on(p):
        return phases is None or p in phases

    def din(name, shape, dt=F32):
        return nc.dram_tensor(name, list(shape), dt, kind="ExternalInput").ap()

    def dscr(name, shape, dt=F32):
        kind = "ExternalOutput" if name in dbg else "Internal"
        return DT(nc.dram_tensor(name, list(shape), dt, kind=kind).ap())

    x_in = [din(f"x{s}", [seq_lens[s], D]) for s in range(nseq)]
    y_out = [DT(nc.dram_tensor(f"y{s}", [seq_lens[s], D], F32, kind="ExternalOutput").ap()) for s in range(nseq)]
    w_in = din("w_in", [depth, D, INW])
    conv_w = din("conv_w", [depth, 5, 1536])
    a_log = din("gdn_a_log", [depth, 8])
    dt_bias = din("gdn_dt_bias", [depth, 8])
    gdn_nw = din("gdn_norm_w", [depth, 128])
    lb_logits = din("hgrn_lb_logits", [depth, 2, 512])
    hg_nw = din("hgrn_norm_w", [depth, 128])
    w_bg = din("w_branch_gdn", [depth, 512, D])
    w_bh = din("w_branch_hgrn", [depth, 512, D])
    w_o = din("w_out", [depth, D, D])
    n_pre_mix = din("norm_pre_mix", [depth, D])
    n_post_mix = din("norm_post_mix", [depth, D])
    n_pre_ffn = din("norm_pre_ffn", [depth, D])
    n_post_ffn = din("norm_post_ffn", [depth, D])
    w_g = din("w_ffn_gate", [depth, D, DFF])
    w_u = din("w_ffn_up", [depth, D, DFF])
    w_d = din("w_ffn_down", [depth, DFF, D])
    c_ident = din("c_ident", [128, 128])
    c_mask = din("c_mask", [128, 8, 128])
    c_sel = din("c_sel", [8, 2])

    SL = list(enumerate(seq_lens))
    XT = [[dscr(f"XT{p}_{s}", [8, 128, L]) for p in range(2)] for s, L in SL]
    XM = [dscr(f"XM_{s}", [8, 128, L]) for s, L in SL]
    HN = [dscr(f"HN_{s}", [L // TA, 128, 8 * TA], BF16) for s, L in SL]
    H2 = [dscr(f"H2_{s}", [L // TA, 128, 8 * TA], BF16) for s, L in SL]
    GQK = [dscr(f"GQK_{s}", [L // TA, 128, 8 * TA], BF16) for s, L in SL]
    GKV = [dscr(f"GKV_{s}", [L, 1024], BF16) for s, L in SL]
    GS = [dscr(f"GS_{s}", [3, 8, L]) for s, L in SL]
    GT = [dscr(f"GT_{s}", [L, 24]) for s, L in SL]
    ZT = [dscr(f"ZT_{s}", [L // TA, 128, 4 * TA], BF16) for s, L in SL]
    HQd = [dscr(f"HQ_{s}", [2, L // TA, 128, 4 * TA], BF16) for s, L in SL]
    HKd = [dscr(f"HK_{s}", [2, L // TA, 128, 4 * TA], BF16) for s, L in SL]
    HAB = [dscr(f"HAB_{s}", [L // TA, 128, 64]) for s, L in SL]
    HT = [dscr(f"HT_{s}", [L, 1536], BF16) for s, L in SL]
    GHT = [dscr(f"GHT_{s}", [L // TA, 128, 4 * TA], BF16) for s, L in SL]
    MG = [dscr(f"MG_{s}", [L // TA, 128, 16 * TA], BF16) for s, L in SL]
    OA = [dscr(f"OA_{s}", [2, L // TA, 128, 4 * TA], BF16) for s, L in SL]
    OB = [dscr(f"OB_{s}", [2, L // TA, 128, 4 * TA], BF16) for s, L in SL]
    ACTS = [dscr(f"ACT_{s}", [L // TA, 128, 22 * TA], BF16) for s, L in SL]

    with ExitStack() as es:
        S = Sched(nc)
        cnt = [0]

        def sb(shape, dt=F32, name=None):
            cnt[0] += 1
            nm = name or f"t{cnt[0]}"
            t = es.enter_context(nc.sbuf_tensor(nm, list(shape), dt))
            return t, Buf(nm)

        def pst(shape, dt=F32, name=None):
            t = es.enter_context(nc.psum_tensor(name, list(shape), dt))
            return t, Buf(name)

        farena = es.enter_context(nc.sbuf_tensor("farena", [128, FA_N], F32))
        harena = es.enter_context(nc.sbuf_tensor("harena", [128, HA_N], BF16))
        off = {"f": 0, "h": 0, "w0": 0, "w1": 0}

        def carve(shape, dt=F32, where=None):
            k = where or ("f" if dt == F32 else "h")
            ar = {"f": farena, "h": harena, "w0": warena[0][0] if warena else None, "w1": warena[1][0] if warena else None}[k]
            cap = {"f": FA_N, "h": HA_N, "w0": WAR_N, "w1": WAR_N}[k]
            n = 1
            for d_ in shape[1:]:
                n *= d_
            o = off[k]
            if k in ("w0", "w1") and dt == F32:
                off[k] = o + 2 * n
                assert off[k] <= cap, (k, off[k])
                v = ar[0:shape[0], o:o + 2 * n].bitcast(F32)
            else:
                off[k] = o + n
                assert off[k] <= cap, (k, off[k])
                v = ar[0:shape[0], o:o + n]
            if len(shape) == 3:
                v = v.rearrange("p (a b) -> p a b", a=shape[1])
            elif len(shape) == 4:
                v = v.rearrange("p (a b c) -> p a b c", a=shape[1], b=shape[2])
            return v, Buf()

        def ring(n, shape, dt=F32, where=None):
            return Ring([carve(shape, dt, where) for _ in range(n)])

        def end_phase():
            S.barrier()
            for k_ in off:
                off[k_] = 0

        def run_chains(gens, window=None):
            gens = list(gens)
            active = []
            while gens or active:
                while gens and (window is None or len(active) < window):
                    active.append(gens.pop(0))
                for g in list(active):
                    try:
                        next(g)
                    except StopIteration:
                        active.remove(g)

        def pipeline(items, load_fn, compute_fn):
            nxt = load_fn(items[0])
            for i, it in enumerate(items):
                cur = nxt
                nxt = load_fn(items[i + 1]) if i + 1 < len(items) else None
                wtick(2)
                compute_fn(it, cur)

        warena = []

        def dma(out, in_, reads, writes, nonc=False):
            if nonc:
                S.add("sp", lambda h: h.dma_start(out=out, in_=in_, allow_slow_non_contiguous=True), reads, writes, dma=True)
            else:
                S.add("sp", lambda h: h.dma_start(out=out, in_=in_), reads, writes, dma=True)

        def mm(out, wbuf, pairs, reads):
            n = len(pairs)

            def fn(h):
                ins = None
                for i, (l, r) in enumerate(pairs):
                    ins = h.matmul(out, lhsT=l, rhs=r, start=(i == 0), stop=(i == n - 1))
                return ins
            S.add("pe", fn, reads, [wbuf])

        def mmh(pt, pb, items, reads):
            def fn(h):
                ins = None
                for (o, prs) in items:
                    n = len(prs)
                    for i, (l, r) in enumerate(prs):
                        ins = h.matmul(o, lhsT=l, rhs=r, start=(i == 0), stop=(i == n - 1))
                return ins
            S.add("pe", fn, reads, [pb])

        def act(out, in_, func, reads, writes, **kw):
            S.add("act", lambda h: h.activation(out=out, in_=in_, func=func, **kw), reads, writes)

        def ts(eng, out, in0, s1, s2, op0, op1, reads, writes):
            if s2 is None:
                S.add(eng, lambda h: h.tensor_scalar(out=out, in0=in0, scalar1=s1, scalar2=None, op0=op0), reads, writes)
            else:
                S.add(eng, lambda h: h.tensor_scalar(out=out, in0=in0, scalar1=s1, scalar2=s2, op0=op0, op1=op1), reads, writes)

        def stt(out, in0, sc, in1, op0, op1, reads, writes):
            S.add("dve", lambda h: h.scalar_tensor_tensor(out=out, in0=in0, scalar=sc, in1=in1, op0=op0, op1=op1), reads, writes)

        def tt(eng, out, in0, in1, op, reads, writes):
            S.add(eng, lambda h: h.tensor_tensor(out=out, in0=in0, in1=in1, op=op), reads, writes)

        def cp(eng, out, in_, reads, writes):
            if eng == "act":
                S.add("act", lambda h: h.activation(out=out, in_=in_, func=AF.Copy), reads, writes)
            else:
                S.add(eng, lambda h: h.tensor_copy(out=out, in_=in_), reads, writes)

        def mset(eng, ap, val, writes):
            S.add(eng, lambda h: h.memset(ap, val), [], writes)

        def bc(ap2, n):
            return ap2.unsqueeze(2).to_broadcast([ap2.shape[0], ap2.shape[1], n])

        def bch(ap2, n):
            return ap2.unsqueeze(1).to_broadcast([ap2.shape[0], n, ap2.shape[1]])

        ident_f, ident_fb = sb([128, 128], F32, "ident_f")
        ident_h, ident_hb = sb([128, 128], BF16, "ident_h")
        ones_h, ones_hb = sb([128, 128], BF16, "ones_h")
        mask_f, mask_fb = sb([128, 8, 128], F32, "mask_f")
        sel, selb = sb([8, 2], F32, "sel")
        eps_t, epsb = sb([128, 2], F32, "eps_t")
        dma(ident_f[:], c_ident[:, :], [], [ident_fb])
        dma(mask_f[:], c_mask[:, :, :], [], [mask_fb])
        dma(sel[:], c_sel[:, :], [], [selb])
        cp("dve", ident_h[:], ident_f[:], [ident_fb], [ident_hb])
        mset("dve", ones_h[:], 1.0, [ones_hb])
        mset("dve", eps_t[:, 0:1], float(EPS), [epsb])
        mset("dve", eps_t[:, 1:2], 1.0, [epsb])
        m128, m128b = sb([8, TA], F32, "m128")
        m64, m64b = sb([128, TA], F32, "m64")
        mset("pool", m128[:], 1.0, [m128b])
        mset("pool", m128[:].rearrange("p (c k) -> p c k", k=128)[:, :, 0:1], 0.0, [m128b])
        mset("pool", m64[:], 1.0, [m64b])
        mset("pool", m64[:].rearrange("p (c k) -> p c k", k=64)[:, :, 0:1], 0.0, [m64b])

        psf = Ring([pst([128, 512], F32, f"psf{i}") for i in range(6)])
        psh = Ring([pst([128, 1024], BF16, f"psh{i}") for i in range(2)])

        warena.extend([sb([128, WAR_N], BF16, f"warena{i}") for i in range(2)])
        wcnt = [0]
        wstage = Ring([sb([128, 320], F32, f"wstage{i}") for i in range(4)])
        cast_rr = [0]

        def next_arena():
            a = warena[wcnt[0] % 2]
            wcnt[0] += 1
            return a

        def w_gen(specs_, wb):
            pend = []
            for (dst, src, ncols) in specs_:
                c0 = 0
                while c0 < ncols:
                    w = min(320, ncols - c0)
                    (st, stb) = wstage.next()
                    dma(st[:, 0:w], src[:, c0:c0 + w], [], [stb])
                    pend.append((dst[:, c0:c0 + w], st[:, 0:w], stb))
                    if len(pend) > 2:
                        (d__, s__, sb__) = pend.pop(0)
                        cp("pool", d__, s__, [sb__], [wb])
                    c0 += w
                    yield
            for (d__, s__, sb__) in pend:
                cp("pool", d__, s__, [sb__], [wb])

        def w_specs(phase, l, war):
            if phase == "A1":
                v = war[:, 0:8 * 2064].rearrange("p (k c) -> p k c", k=8)
                return [(v[:, kc, :], w_in[l, kc * 128:(kc + 1) * 128, 0:2064], 2064) for kc in range(8)]
            if phase == "A2a":
                v = war[:, 0:8 * 2048].rearrange("p (k c) -> p k c", k=8)
                return [(v[:, kc, :], w_in[l, kc * 128:(kc + 1) * 128, O_QH:O_QH + 2048], 2048) for kc in range(8)]
            if phase == "A2b":
                v = war[:, 0:8 * 2560].rearrange("p (k c) -> p k c", k=8)
                return [(v[:, kc, :], w_in[l, kc * 128:(kc + 1) * 128, O_GH:O_GH + 2560], 2560) for kc in range(8)]
            if phase == "C1":
                a_ = war[:, 0:4096].rearrange("p (k c) -> p k c", k=4)
                b_ = war[:, 4096:8192].rearrange("p (k c) -> p k c", k=4)
                c_ = war[:, 8192:16384].rearrange("p (k c) -> p k c", k=8)
                r = []
                for kc in range(4):
                    r.append((a_[:, kc, :], w_bg[l, kc * 128:(kc + 1) * 128, :], 1024))
                    r.append((b_[:, kc, :], w_bh[l, kc * 128:(kc + 1) * 128, :], 1024))
                for kc in range(8):
                    r.append((c_[:, kc, :], w_o[l, kc * 128:(kc + 1) * 128, :], 1024))
                return r
            if phase in ("C2a0", "C2a1"):
                half = int(phase[-1])
                NF = 11 * 128
                g_ = war[:, 0:8 * NF].rearrange("p (k c) -> p k c", k=8)
                u_ = war[:, 8 * NF:16 * NF].rearrange("p (k c) -> p k c", k=8)
                r = []
                for kc in range(8):
                    r.append((g_[:, kc, :], w_g[l, kc * 128:(kc + 1) * 128, half * NF:(half + 1) * NF], NF))
                    r.append((u_[:, kc, :], w_u[l, kc * 128:(kc + 1) * 128, half * NF:(half + 1) * NF], NF))
                return r
            if phase == "C2b":
                v = war[:, 0:22 * 1024].rearrange("p (k c) -> p k c", k=22)
                return [(v[:, fc, :], w_d[l, fc * 128:(fc + 1) * 128, :], 1024) for fc in range(22)]
            raise ValueError(phase)

        wpre = {}
        wcur = [None]

        def prefetch(phase, l):
            if l >= depth or not on(phase[:3] if phase.startswith("C2a") else phase):
                return
            (war, wb) = next_arena()
            g = w_gen(w_specs(phase, l, war), wb)
            wpre[(phase, l)] = (war, wb, g)
            wcur[0] = g

        def wtick(n=2):
            g = wcur[0]
            if g is None:
                return
            for _ in range(n):
                try:
                    next(g)
                except StopIteration:
                    wcur[0] = None
                    return

        def acquire(phase, l):
            if (phase, l) in wpre:
                (war, wb, g) = wpre.pop((phase, l))
            else:
                (war, wb) = next_arena()
                g = w_gen(w_specs(phase, l, war), wb)
            for _ in g:
                pass
            if wcur[0] is g:
                wcur[0] = None
            return war, wb

        def vec_cols(src_1d, n, name):
            t, tb = sb([128, n], F32, name)
            dma(t[:], src_1d.rearrange("(c p) -> p c", p=128), [], [tb], nonc=True)
            return t, tb

        def x_load(xsrc, L, blk, halo, xT_r):
            t0 = blk * TA
            nb = L // TA
            (xt, xb) = xT_r.next()
            lo, hi = (2, 2) if halo else (0, 0)
            W = TA + lo + hi
            a0 = t0 - lo
            a1 = t0 + TA + hi
            o0 = 0
            if a0 < 0:
                mset("pool", xt[:, :, 0:lo], 0.0, [xb])
                o0 = lo
                a0 = 0
            if a1 > L:
                mset("pool", xt[:, :, W - hi:W], 0.0, [xb])
                a1 = L
            rb = [xsrc.b(k) for k in range(max(0, blk - 1), min(nb, blk + 2))] if halo else [xsrc.b(blk)]
            dma(xt[:, :, o0:o0 + (a1 - a0)], xsrc.ap[:, :, a0:a1].rearrange("c p t -> p c t"), rb, [xb])
            return xt, xb

        def norm_compute(xt, xb, gain, gainb, halo, sq_r, hT_r, rstd_r, tmp_r):
            W = TA + (4 if halo else 0)
            (sq, sqb) = sq_r.next()
            act(sq[:, :, 0:W], xt[:, :, 0:W], AF.Square, [xb], [sqb])
            (pt, pb) = psf.next()
            mm(pt[:, 0:W], pb, [(ones_h[:], sq[:, c, 0:W]) for c in range(8)], [ones_hb, sqb])
            (rs, rsb) = rstd_r.next()
            (tmp, tmpb) = tmp_r.next()
            act(rs[:, 0:W], pt[:, 0:W], AF.Ln, [pb, epsb], [rsb], bias=eps_t[:, 0:1], scale=1.0 / D)
            act(rs[:, 0:W], rs[:, 0:W], AF.Exp, [rsb], [rsb], scale=-0.5)
            (ht, hb) = hT_r.next()
            for c in range(8):
                stt(ht[:, c, 0:W], xt[:, c, 0:W], gain[:, c:c + 1], rs[:, 0:W], ALU.mult, ALU.mult, [xb, gainb, rsb], [hb])
            return ht, hb

        def post_norm_add(res_r, xt, xb, val, valb, sq8, sq8b, gain, gainb, rstd_r, tmp_r, f_r):
            (pt, pb) = psf.next()
            mm(pt[:, 0:TA], pb, [(ones_h[:], sq8[:, c, :]) for c in range(8)], [ones_hb, sq8b])
            (rs, rsb) = rstd_r.next()
            (tmp, tmpb) = tmp_r.next()
            act(rs[:, 0:TA], pt[:, 0:TA], AF.Ln, [pb, epsb], [rsb], bias=eps_t[:, 0:1], scale=1.0 / D)
            act(rs[:, 0:TA], rs[:, 0:TA], AF.Exp, [rsb], [rsb], scale=-0.5)
            (res, resb) = res_r.next()
            for c in range(8):
                (t1, t1b) = f_r.next()
                stt(t1[:, 0:TA], val[:, c, :], gain[:, c:c + 1], rs[:, 0:TA], ALU.mult, ALU.mult, [valb, gainb, rsb], [t1b])
                tt("dve", res[:, c, :], xt[:, c, 0:TA], t1[:, 0:TA], ALU.add, [xb, t1b], [resb])
            return res, resb

        lg, lgb = sb([128, depth, 8], F32, "lb_lg")
        lbt, lbb = sb([128, depth, 8], F32, "lb")
        omlt, omlb = sb([128, depth, 8], F32, "oml")
        for l in range(depth):
            for d_ in range(2):
                dma(lg[:, l, d_ * 4:(d_ + 1) * 4], lb_logits[l, d_].rearrange("(h p) -> p h", p=128), [], [lgb], nonc=True)
        mx, mxb = sb([128, 8], F32, "lb_mx")
        cp("dve", mx[:], lg[:, 0, :], [lgb], [mxb])
        for l in range(1, depth):
            tt("dve", mx[:], mx[:], lg[:, l, :], ALU.max, [mxb, lgb], [mxb])
        for l in range(depth):
            tt("dve", lg[:, l, :], lg[:, l, :], mx[:], ALU.subtract, [lgb, mxb], [lgb])
        act(lg[:], lg[:], AF.Exp, [lgb], [lgb])
        cp("dve", mx[:], lg[:, 0, :], [lgb], [mxb])
        for l in range(1, depth):
            tt("dve", mx[:], mx[:], lg[:, l, :], ALU.add, [mxb, lgb], [mxb])
        S.add("dve", lambda h: h.reciprocal(out=mx[:], in_=mx[:]), [mxb], [mxb])
        mset("dve", lbt[:, 0, :], 0.0, [lbb])
        for l in range(1, depth):
            tt("dve", lg[:, l, :], lg[:, l, :], mx[:], ALU.mult, [lgb, mxb], [lgb])
            tt("dve", lbt[:, l, :], lbt[:, l - 1, :], lg[:, l, :], ALU.add, [lbb, lgb], [lbb])
        ts("dve", omlt[:], lbt[:], -1.0, 1.0, ALU.mult, ALU.add, [lbb], [omlb])

        if on("P0"):
            xin_r = ring(2, [128, D], F32)
            xo_r = ring(2, [128, 8, 128], F32)
            for s, L in SL:
                for j in range(L // 128):
                    (xt, xb) = xin_r.next()
                    dma(xt, x_in[s][j * 128:(j + 1) * 128, :], [], [xb])
                    (xo, xob) = xo_r.next()
                    for half in range(2):
                        (pt, pb) = psf.next()
                        mmh(pt, pb, [(pt[:, c * 128:(c + 1) * 128], [(xt[:, (half * 4 + c) * 128:(half * 4 + c + 1) * 128], ident_f[:])]) for c in range(4)], [xb, ident_fb])
                        cp("act" if half == 0 else "dve", xo[:, half * 4:(half + 1) * 4, :], pt[:].rearrange("p (c t) -> p c t", c=4), [pb], [xob])
                    dma(XT[s][0].ap[:, :, j * 128:(j + 1) * 128].rearrange("c p t -> p c t"), xo, [xob], [XT[s][0].b(j * 128 // TA)])
            end_phase()

        for l in range(depth):
            par = l % 2
            g_pm, g_pmb = vec_cols(n_pre_mix[l], 8, f"g_pm{l}")
            g_po, g_pob = vec_cols(n_post_mix[l], 8, f"g_po{l}")
            g_pf, g_pfb = vec_cols(n_pre_ffn[l], 8, f"g_pf{l}")
            g_qf, g_qfb = vec_cols(n_post_ffn[l], 8, f"g_qf{l}")
            nwa, nwab = sb([128, 1], F32, f"nwa{l}")
            nwh, nwhb = sb([128, 1], F32, f"nwh{l}")
            dma(nwa[:], gdn_nw[l].rearrange("(p o) -> p o", o=1), [], [nwab], nonc=True)
            dma(nwh[:], hg_nw[l].rearrange("(p o) -> p o", o=1), [], [nwhb], nonc=True)
            cw, cwb = sb([128, 12, 5], F32, f"cw{l}")
            for j in range(5):
                dma(cw[:, :, j], conv_w[l, j].rearrange("(c p) -> p c", p=128), [], [cwb], nonc=True)
            alog, alogb = sb([8, 1], F32, f"alog{l}")
            dtb, dtbb = sb([8, 1], F32, f"dtb{l}")
            dma(alog[:], a_log[l].rearrange("(p o) -> p o", o=1), [], [alogb], nonc=True)
            dma(dtb[:], dt_bias[l].rearrange("(p o) -> p o", o=1), [], [dtbb], nonc=True)
            negA, negAb = sb([8, 1], F32, f"negA{l}")
            act(negA[:], alog[:], AF.Exp, [alogb], [negAb])
            ts("dve", negA[:], negA[:], -1.0, None, ALU.mult, None, [negAb], [negAb])

            if on("A1"):
                (war, wb) = acquire("A1", l)
                wA1 = war[:, 0:8 * 2064].rearrange("p (k c) -> p k c", k=8)
                prefetch("A2a", l)
                xT_r = ring(2, [128, 8, XH])
                sq_r = ring(1, [128, 8, XH], BF16)
                hT_r = ring(2, [128, 8, XH], BF16)
                rstd_r = ring(2, [128, XH])
                tmp_r = ring(2, [128, XH])
                cvin_r = ring(6, [128, XH])
                acc_r = ring(6, [128, TA])
                sil_r = ring(6, [128, TA])
                sqb_r = ring(6, [128, TA], BF16)
                rn_r = ring(6, [128, TA])
                ptmp_r = ring(2, [128, TA])
                qk_r = ring(2, [128, 8, TA], BF16)
                v_r = ring(2, [128, 4, TA], BF16)
                kvt_r = ring(2, [128, 1024], BF16)
                z_r = ring(2, [128, 4, TA], BF16)
                abt = {k: carve([8, TA]) for k in ("e", "sp", "g", "pfx", "tmp", "sfx")}
                gs_r = ring(2, [8, 3, TA])
                gt_r = ring(2, [128, 24])

                def a1_chunk(ci, ht, hb, qk, qkb, vs, vsb):
                    (pt, pb) = psf.next()
                    mm(pt[:, 0:XH], pb, [(wA1[:, kc, ci * 128:(ci + 1) * 128], ht[:, kc, :]) for kc in range(8)], [wb, hb])
                    (cv, cvb) = cvin_r.next()
                    cp("act", cv, pt[:, 0:XH], [pb], [cvb])
                    yield
                    (ac, acb) = acc_r.next()
                    if False:
                        ts("pool", ac, cv[:, 0:TA], cw[:, ci, 0:1], None, ALU.mult, None, [cvb, cwb], [acb])
                        for j in range(1, 5):
                            (ptm, ptmb) = ptmp_r.next()
                            ts("pool", ptm, cv[:, j:j + TA], cw[:, ci, j:j + 1], None, ALU.mult, None, [cvb, cwb], [ptmb])
                            tt("dve", ac, ac, ptm, ALU.add, [acb, ptmb], [acb])
                    else:
                        ts("dve", ac, cv[:, 0:TA], cw[:, ci, 0:1], None, ALU.mult, None, [cvb, cwb], [acb])
                        for j in range(1, 5):
                            stt(ac, cv[:, j:j + TA], cw[:, ci, j:j + 1], ac, ALU.mult, ALU.add, [cvb, cwb, acb], [acb])
                    yield
                    if ci < 8:
                        (sl, slb) = sil_r.next()
                        act(sl, ac, AF.Silu, [acb], [slb])
                        (sq2, sq2b) = sqb_r.next()
                        act(sq2, sl, AF.Square, [slb], [sq2b])
                        yield
                        (p2, p2b) = psf.next()
                        mm(p2[:, 0:TA], p2b, [(ones_h[:], sq2)], [ones_hb, sq2b])
                        yield
                        (rn, rnb) = rn_r.next()
                        act(rn, p2[:, 0:TA], AF.Ln, [p2b, epsb], [rnb], bias=eps_t[:, 0:1], scale=1.0)
                        act(rn, rn, AF.Exp, [rnb], [rnb], scale=-0.5)
                        yield
                        scl = float(DK ** -0.5) if ci < 4 else 1.0
                        stt(qk[:, ci, :], sl, scl, rn, ALU.mult, ALU.mult, [slb, rnb], [qkb])
                    else:
                        act(vs[:, ci - 8, :], ac, AF.Silu, [acb], [vsb])

                def a1_load(it):
                    s, L, blk = it
                    return x_load(XT[s][par], L, blk, True, xT_r)

                def a1_compute(it, cur):
                    s, L, blk = it
                    t0 = blk * TA
                    (xt, xb) = cur
                    ht, hb = norm_compute(xt, xb, g_pm, g_pmb, True, sq_r, hT_r, rstd_r, tmp_r)
                    dma(HN[s].ap[blk].rearrange("p (c t) -> p c t", c=8), ht[:, :, 2:2 + TA], [hb], [HN[s].b(blk)])
                    (qk, qkb) = qk_r.next()
                    (vs, vsb) = v_r.next()
                    run_chains([a1_chunk(ci, ht, hb, qk, qkb, vs, vsb) for ci in range(12)], window=5)
                    dma(GQK[s].ap[blk].rearrange("p (c t) -> p c t", c=8), qk, [qkb], [GQK[s].b(blk)])
                    for j in range(NCH):
                        (ph, phb) = psh.next()

                        def fn(h, ph=ph, qk=qk, vs=vs, j=j):
                            ins = None
                            for hh in range(4):
                                ins = h.transpose(out=ph[:, hh * 128:(hh + 1) * 128], in_=qk[:, 4 + hh, j * 128:(j + 1) * 128], identity=ident_h[:])
                            for hh in range(4):
                                ins = h.transpose(out=ph[:, 512 + hh * 128:512 + (hh + 1) * 128], in_=vs[:, hh, j * 128:(j + 1) * 128], identity=ident_h[:])
                            return ins
                        S.add("pe", fn, [qkb, vsb, ident_hb], [phb])
                        (kv, kvb) = kvt_r.next()
                        cp("act", kv, ph[:], [phb], [kvb])
                        r0 = t0 + j * 128
                        dma(GKV[s].ap[r0:r0 + 128, :], kv, [kvb], [GKV[s].b(r0 // 128)])
                    (zs, zsb) = z_r.next()
                    for hh in range(4):
                        (pt, pb) = psf.next()
                        mm(pt[:, 0:TA], pb, [(wA1[:, kc, O_Z + hh * 128:O_Z + (hh + 1) * 128], ht[:, kc, 2:2 + TA]) for kc in range(8)], [wb, hb])
                        act(zs[:, hh, :], pt[:, 0:TA], AF.Silu, [pb], [zsb])
                    dma(ZT[s].ap[blk].rearrange("p (c t) -> p c t", c=4), zs, [zsb], [ZT[s].b(blk)])
                    (pa, pab) = psf.next()
                    mm(pa[0:8, 0:TA], pab, [(wA1[:, kc, O_A:O_A + 8], ht[:, kc, 2:2 + TA]) for kc in range(8)], [wb, hb])
                    (pbb, pbbb) = psf.next()
                    mm(pbb[0:8, 0:TA], pbbb, [(wA1[:, kc, O_B:O_B + 8], ht[:, kc, 2:2 + TA]) for kc in range(8)], [wb, hb])
                    e_t, e_b = abt["e"]
                    act(e_t, pa[0:8, 0:TA], AF.Exp, [pab, dtbb], [e_b], bias=dtb[:])
                    sp_t, sp_b = abt["sp"]
                    act(sp_t, e_t, AF.Ln, [e_b, epsb], [sp_b], bias=eps_t[0:8, 1:2])
                    g_t, g_b = abt["g"]
                    ts("dve", g_t, sp_t, negA[:, 0:1], None, ALU.mult, None, [sp_b, negAb], [g_b])
                    pf_t, pf_b = abt["pfx"]
                    S.add("dve", lambda h, pf_t=pf_t, g_t=g_t: h.tensor_tensor_scan(out=pf_t, data0=m128[:], data1=g_t, initial=0.0, op0=ALU.mult, op1=ALU.add), [m128b, g_b], [pf_b])
                    tm_t, tm_b = abt["tmp"]
                    pf3 = pf_t.rearrange("p (c k) -> p c k", k=128)
                    tot_bc = pf3[:, :, 127:128].to_broadcast([8, NCH, 128])
                    tt("dve", tm_t.rearrange("p (c k) -> p c k", k=128), tot_bc, pf3, ALU.subtract, [pf_b], [tm_b])
                    sf_t, sf_b = abt["sfx"]
                    tt("dve", sf_t, tm_t, g_t, ALU.add, [tm_b, g_b], [sf_b])
                    (gs, gsb) = gs_r.next()
                    ts("dve", gs[:, 0, :], pf_t, sel[:, 0:1], None, ALU.mult, None, [pf_b, selb], [gsb])
                    stt(gs[:, 0, :], sf_t, sel[:, 1:2], gs[:, 0, :], ALU.mult, ALU.add, [sf_b, selb, gsb], [gsb])
                    act(gs[:, 1, :], pbb[0:8, 0:TA], AF.Sigmoid, [pbbb], [gsb])
                    tt("dve", gs[:, 2, :].rearrange("p (c k) -> p c k", k=128), tot_bc, gs[:, 0, :].rearrange("p (c k) -> p c k", k=128), ALU.subtract, [pf_b, gsb], [gsb])
                    dma(GS[s].ap[:, :, t0:t0 + TA].rearrange("q r t -> r q t"), gs, [gsb], [GS[s].b(blk)])
                    for j in range(NCH):
                        (pt, pb) = psf.next()
                        mmh(pt, pb, [(pt[:, q * 8:(q + 1) * 8], [(gs[:, q, j * 128:(j + 1) * 128], ident_f[0:8, 0:8])]) for q in range(3)], [gsb, ident_fb])
                        (gt, gtb) = gt_r.next()
                        cp("act", gt, pt[:, 0:24], [pb], [gtb])
                        r0 = t0 + j * 128
                        dma(GT[s].ap[r0:r0 + 128, :], gt, [gtb], [GT[s].b(r0 // 128)])

                pipeline([(s, L, blk) for s, L in SL for blk in range(L // TA)], a1_load, a1_compute)
                end_phase()

            if on("A2a"):
                (war, wb) = acquire("A2a", l)
                wA = war[:, 0:8 * 2048].rearrange("p (k c) -> p k c", k=8)
                prefetch("A2b", l)
                CQ, CF, CI = 0, 512, 1536
                hT_r = ring(2, [128, 8, TA], BF16)
                qs_r = ring(2, [128, 4, TA])
                ih_r = ring(2, [128, 4, TA], BF16)
                hq_r = [ring(2, [128, 4, TA], BF16) for _ in range(2)]
                hk_r = [ring(2, [128, 4, TA], BF16) for _ in range(2)]
                ab_r = ring(2, [128, 64])
                f_r = ring(50, [128, TA])
                tok_r = ring(2, [128, 1536], BF16)
                scl = float(DK ** -0.5)

                def a2_chain(d_, hh, ht, hb, qs, qsb, hq, hqb, hk, hkb, ab5, abb):
                    dh = d_ * 4 + hh
                    (pt, pb) = psf.next()
                    c0 = CF + d_ * 512 + hh * 128
                    mm(pt[:, 0:TA], pb, [(wA[:, kc, c0:c0 + 128], ht[:, kc, :]) for kc in range(8)], [wb, hb])
                    (sg, sgb) = f_r.next()
                    act(sg, pt[:, 0:TA], AF.Sigmoid, [pb], [sgb])
                    yield
                    (ff, ffb) = f_r.next()
                    ts("dve", ff, sg, omlt[:, l, dh:dh + 1], lbt[:, l, dh:dh + 1], ALU.mult, ALU.add, [sgb, omlb, lbb], [ffb])
                    yield
                    (lf, lfb) = f_r.next()
                    act(lf, ff, AF.Ln, [ffb], [lfb])
                    (kk, kkb) = f_r.next()
                    ts("dve", kk, ff, -1.0, 1.0, ALU.mult, ALU.add, [ffb], [kkb])
                    yield
                    (pf, pfb) = f_r.next()
                    S.add("dve", lambda h, pf=pf, lf=lf: h.tensor_tensor_scan(out=pf, data0=m64[:], data1=lf, initial=0.0, op0=ALU.mult, op1=ALU.add), [m64b, lfb], [pfb])
                    yield
                    if d_ == 0:
                        cum, cumb = pf, pfb
                        ri, ai = 31, 63
                    else:
                        pf3 = pf.rearrange("p (c k) -> p c k", k=64)
                        (tm, tmb) = f_r.next()
                        tt("dve", tm.rearrange("p (c k) -> p c k", k=64), pf3[:, :, 63:64].to_broadcast([128, NC64, 64]), pf3, ALU.subtract, [pfb], [tmb])
                        yield
                        (cum, cumb) = f_r.next()
                        tt("dve", cum, tm, lf, ALU.add, [tmb, lfb], [cumb])
                        yield
                        ri, ai = 32, 0
                    cum3 = cum.rearrange("p (c k) -> p c k", k=64)
                    (dd, ddb) = f_r.next()
                    tt("dve", dd.rearrange("p (c k) -> p c k", k=64), cum3, cum3[:, :, ri:ri + 1].to_broadcast([128, NC64, 64]), ALU.subtract, [cumb], [ddb])
                    yield
                    (eq, eqb) = f_r.next()
                    act(eq, dd, AF.Exp, [ddb], [eqb])
                    (ek, ekb) = f_r.next()
                    act(ek, dd, AF.Exp, [ddb], [ekb], scale=-1.0)
                    act(ab5[:, d_, 1, hh, :], cum3[:, :, ri], AF.Exp, [cumb], [abb])
                    yield
                    stt(hq[:, hh, :], qs[:, hh, :], scl, eq, ALU.mult, ALU.mult, [qsb, eqb], [hqb])
                    tt("dve", hk[:, hh, :], kk, ek, ALU.mult, [kkb, ekb], [hkb])
                    cp("dve", ab5[:, d_, 0, hh, :], eq.rearrange("p (c k) -> p c k", k=64)[:, :, ai], [eqb], [abb])

                def a2_load(it):
                    s, L, blk = it
                    (ht, hb) = hT_r.next()
                    dma(ht, HN[s].ap[blk].rearrange("p (c t) -> p c t", c=8), [HN[s].b(blk)], [hb])
                    return ht, hb

                def a2_compute(it, cur):
                    s, L, blk = it
                    t0 = blk * TA
                    (ht, hb) = cur
                    (qs, qsb) = qs_r.next()
                    (ih, ihb) = ih_r.next()
                    for hh in range(4):
                        (pt, pb) = psf.next()
                        mm(pt[:, 0:TA], pb, [(wA[:, kc, CQ + hh * 128:CQ + (hh + 1) * 128], ht[:, kc, :]) for kc in range(8)], [wb, hb])
                        act(qs[:, hh, :], pt[:, 0:TA], AF.Silu, [pb], [qsb])
                        (pt, pb) = psf.next()
                        mm(pt[:, 0:TA], pb, [(wA[:, kc, CI + hh * 128:CI + (hh + 1) * 128], ht[:, kc, :]) for kc in range(8)], [wb, hb])
                        cp("act", ih[:, hh, :], pt[:, 0:TA], [pb], [ihb])
                    (ab, abb) = ab_r.next()
                    ab5 = ab.rearrange("p (d q h c) -> p d q h c", d=2, q=2, h=4)
                    hqs = [hq_r[d_].next() for d_ in range(2)]
                    hks = [hk_r[d_].next() for d_ in range(2)]
                    run_chains([a2_chain(d_, hh, ht, hb, qs, qsb, hqs[d_][0], hqs[d_][1], hks[d_][0], hks[d_][1], ab5, abb) for d_ in range(2) for hh in range(4)], window=4)
                    for d_ in range(2):
                        dma(HQd[s].ap[d_, blk].rearrange("p (c t) -> p c t", c=4), hqs[d_][0], [hqs[d_][1]], [HQd[s].b((d_, blk))])
                        dma(HKd[s].ap[d_, blk].rearrange("p (c t) -> p c t", c=4), hks[d_][0], [hks[d_][1]], [HKd[s].b((d_, blk))])
                    dma(HAB[s].ap[blk], ab, [abb], [HAB[s].b(blk)])
                    for j in range(NCH):
                        (ph, phb) = psh.next()
                        (ph2, ph2b) = psh.next()

                        def fn(h, ph=ph, hks=hks, j=j):
                            ins = None
                            for d_ in range(2):
                                for hh in range(4):
                                    o = (d_ * 4 + hh) * 128
                                    ins = h.transpose(out=ph[:, o:o + 128], in_=hks[d_][0][:, hh, j * 128:(j + 1) * 128], identity=ident_h[:])
                            return ins
                        S.add("pe", fn, [hks[0][1], hks[1][1], ident_hb], [phb])

                        def fn2(h, ph2=ph2, ih=ih, j=j):
                            ins = None
                            for hh in range(4):
                                ins = h.transpose(out=ph2[:, hh * 128:(hh + 1) * 128], in_=ih[:, hh, j * 128:(j + 1) * 128], identity=ident_h[:])
                            return ins
                        S.add("pe", fn2, [ihb, ident_hb], [ph2b])
                        (tk, tkb) = tok_r.next()
                        cp("act", tk[:, 0:1024], ph[:], [phb], [tkb])
                        cp("dve", tk[:, 1024:1536], ph2[:, 0:512], [ph2b], [tkb])
                        r0 = t0 + j * 128
                        dma(HT[s].ap[r0:r0 + 128, :], tk, [tkb], [HT[s].b(r0 // 128)])

                pipeline([(s, L, blk) for s, L in SL for blk in range(L // TA)], a2_load, a2_compute)
                end_phase()

            if on("A2b"):
                (war, wb) = acquire("A2b", l)
                wA = war[:, 0:8 * 2560].rearrange("p (k c) -> p k c", k=8)
                hT_r = ring(2, [128, 8, TA], BF16)
                gh_r = ring(2, [128, 4, TA], BF16)
                mg_r = ring(2, [128, 8, TA], BF16)

                def a2b_load(it):
                    s, L, blk = it
                    (ht, hb) = hT_r.next()
                    dma(ht, HN[s].ap[blk].rearrange("p (c t) -> p c t", c=8), [HN[s].b(blk)], [hb])
                    return ht, hb

                def a2b_compute(it, cur):
                    s, L, blk = it
                    (ht, hb) = cur
                    (gh, ghb) = gh_r.next()
                    for hh in range(4):
                        (pt, pb) = psf.next()
                        mm(pt[:, 0:TA], pb, [(wA[:, kc, hh * 128:(hh + 1) * 128], ht[:, kc, :]) for kc in range(8)], [wb, hb])
                        act(gh[:, hh, :], pt[:, 0:TA], AF.Silu, [pb], [ghb])
                    dma(GHT[s].ap[blk].rearrange("p (c t) -> p c t", c=4), gh, [ghb], [GHT[s].b(blk)])
                    for half in range(2):
                        (mg, mgb) = mg_r.next()
                        for c in range(8):
                            c0 = 512 + (half * 8 + c) * 128
                            (pt, pb) = psf.next()
                            mm(pt[:, 0:TA], pb, [(wA[:, kc, c0:c0 + 128], ht[:, kc, :]) for kc in range(8)], [wb, hb])
                            if c % 2 == 0:
                                act(mg[:, c, :], pt[:, 0:TA], AF.Sigmoid, [pb], [mgb])
                            else:
                                act(mg[:, c, :], pt[:, 0:TA], AF.Sigmoid, [pb], [mgb])
                        dma(MG[s].ap[blk, :, half * 8 * TA:(half + 1) * 8 * TA].rearrange("p (c t) -> p c t", c=8), mg, [mgb], [MG[s].b((blk, half))])

                pipeline([(s, L, blk) for s, L in SL for blk in range(L // TA)], a2b_load, a2b_compute)
                end_phase()

            if on("B1"):
                def b1_chain(s, L, d_, wh):
                    nch = L // 128
                    mA, mBi, mBs = (1, 2, 3) if d_ == 0 else (3, 0, 1)
                    li = 127 if d_ == 0 else 0
                    S32, S32b = carve([128, 4, 128])
                    Sbf, Sbfb = carve([128, 4, 128], BF16)
                    mset("pool", S32, 0.0, [S32b])
                    mset("pool", Sbf, 0.0, [Sbfb])
                    qk_r = ring(2, [128, 8, 128], BF16)
                    os_r = ring(2, [128, 4, 128], BF16)
                    kv_r = ring(2, [128, 1024], BF16)
                    gt_r = ring(2, [128, 24])
                    gsb_r = ring(2, [128, 2, 4, 128])
                    sm_r = ring(6, [128, 4])
                    fm_r = ring(5, [128, 4, 128])
                    lf_r = ring(4, [128, 4, 128])
                    hm_r = ring(16, [128, 4, 128], BF16, where=wh)
                    iv_r = ring(12, [128, 4, 128], F32, where=wh)
                    order = list(range(nch)) if d_ == 0 else list(range(nch - 1, -1, -1))

                    def issue(c):
                        blk, cc = c // NCH, c % NCH
                        t0 = c * 128
                        (qk, qkb) = qk_r.next()
                        dma(qk, GQK[s].ap[blk].rearrange("p (c t) -> p c t", c=8)[:, :, cc * 128:(cc + 1) * 128], [GQK[s].b(blk)], [qkb])
                        (kv, kvb) = kv_r.next()
                        dma(kv, GKV[s].ap[t0:t0 + 128, :], [GKV[s].b(c)], [kvb])
                        (gt, gtb) = gt_r.next()
                        dma(gt, GT[s].ap[t0:t0 + 128, :], [GT[s].b(c)], [gtb])
                        (gsr, gsrb) = gsb_r.next()
                        for q_ in range(2):
                            gsap = AP(GS[s].ap.tensor, (q_ * 8 + d_ * 4) * L + t0, [[0, 128], [L, 4], [1, 128]])
                            dma(gsr[:, q_], gsap, [GS[s].b(t0 // TA)], [gsrb])
                        return qk, qkb, kv, kvb, gt, gtb, gsr, gsrb

                    nxt = issue(order[0])
                    for idx, c in enumerate(order):
                        (qk, qkb, kv, kvb, gt, gtb, gsr, gsrb) = nxt
                        if idx + 1 < len(order):
                            nxt = issue(order[idx + 1])
                        blk, cc = c // NCH, c % NCH
                        cumc = gt[:, d_ * 4:d_ * 4 + 4]
                        betac = gt[:, 8 + d_ * 4:12 + d_ * 4]
                        remc = gt[:, 16 + d_ * 4:20 + d_ * 4]
                        kT = qk[:, 4:8, :]
                        qT = qk[:, 0:4, :]
                        kv3k = kv[:, 0:512].rearrange("p (h d) -> p h d", h=4)
                        kv3v = kv[:, 512:1024].rearrange("p (h d) -> p h d", h=4)
                        (ecum, ecumb) = sm_r.next()
                        act(ecum, cumc, AF.Exp, [gtb], [ecumb])
                        (erem, eremb) = sm_r.next()
                        act(erem, remc, AF.Exp, [gtb], [eremb])
                        (erow, erowb) = lf_r.next()
                        act(erow, gsr[:, 0], AF.Exp, [gsrb], [erowb])
                        (pkk, pkkb) = psf.next()
                        mmh(pkk, pkkb, [(pkk[:, h_ * 128:(h_ + 1) * 128], [(kT[:, h_, :], kT[:, h_, :])]) for h_ in range(4)], [qkb])
                        (pqk, pqkb) = psf.next()
                        mmh(pqk, pqkb, [(pqk[:, h_ * 128:(h_ + 1) * 128], [(kT[:, h_, :], qT[:, h_, :])]) for h_ in range(4)], [qkb])
                        pkk3 = pkk[:].rearrange("p (h t) -> p h t", h=4)
                        pqk3 = pqk[:].rearrange("p (h t) -> p h t", h=4)
                        (da, dab) = fm_r.next()
                        stt(da, gsr[:, 0], -1.0, bc(cumc, 128), ALU.mult, ALU.add, [gsrb, gtb], [dab])
                        (db, dbb) = fm_r.next()
                        stt(db, bc(cumc, 128), -1.0, gsr[:, 0], ALU.mult, ALU.add, [gsrb, gtb], [dbb])
                        yield
                        (bec, becb) = sm_r.next()
                        tt("dve", bec, betac, ecum, ALU.mult, [gtb, ecumb], [becb])
                        tt("dve", da, da, bch(mask_f[:, 4 + mA, :], 4), ALU.add, [dab, mask_fb], [dab])
                        tt("dve", db, db, bch(mask_f[:, 4 + mBi, :], 4), ALU.add, [dbb, mask_fb], [dbb])
                        (bm, bmb) = fm_r.next()
                        tt("dve", bm, gsr[:, 1], bch(mask_f[:, mBs, :], 4), ALU.mult, [gsrb, mask_fb], [bmb])
                        yield
                        act(da, da, AF.Exp, [dab], [dab])
                        act(db, db, AF.Exp, [dbb], [dbb])
                        yield
                        (A0, A0b) = fm_r.next()
                        tt("dve", A0, pkk3, da, ALU.mult, [pkkb, dab], [A0b])
                        (B0, B0b) = fm_r.next()
                        tt("dve", B0, pkk3, db, ALU.mult, [pkkb, dbb], [B0b])
                        (qkT, qkTb) = hm_r.next()
                        tt("dve", qkT, pqk3, db, ALU.mult, [pqkb, dbb], [qkTb])
                        yield
                        (Am, Amb) = iv_r.next()
                        tt("dve", Am, A0, bc(betac, 128), ALU.mult, [A0b, gtb], [Amb])
                        (Bm, Bmb) = iv_r.next()
                        tt("dve", Bm, B0, bm, ALU.mult, [B0b, bmb], [Bmb])
                        yield
                        (Y, Yb) = iv_r.next()
                        tt("dve", Y, bch(ident_f[:], 4), Bm, ALU.subtract, [ident_fb, Bmb], [Yb])
                        (bv, bvb) = hm_r.next()
                        tt("dve", bv, kv3v, bc(betac, 128), ALU.mult, [kvb, gtb], [bvb])
                        (kbe, kbeb) = hm_r.next()
                        tt("dve", kbe, kv3k, bc(bec, 128), ALU.mult, [kvb, becb], [kbeb])
                        (kdt, kdtb) = hm_r.next()
                        tt("dve", kdt, kv3k, bc(erem, 128), ALU.mult, [kvb, eremb], [kdtb])
                        (qdT, qdTb) = hm_r.next()
                        tt("dve", qdT, qT, erow, ALU.mult, [qkb, erowb], [qdTb])
                        yield
                        P, Pb, Pt, Ptb = Am, Amb, Bm, Bmb
                        prevP = None
                        for k in range(1, 7):
                            (pp, ppb) = psf.next()
                            mmh(pp, ppb, [(pp[:, h_ * 128:(h_ + 1) * 128], [(Pt[:, h_, :], P[:, h_, :])]) for h_ in range(4)], [Pb, Ptb])
                            if k < 6:
                                (pp2, pp2b) = psf.next()
                                mmh(pp2, pp2b, [(pp2[:, h_ * 128:(h_ + 1) * 128], [(P[:, h_, :], Pt[:, h_, :])]) for h_ in range(4)], [Pb, Ptb])
                            if prevP is not None:
                                (pp3, pp3b) = psf.next()
                                mmh(pp3, pp3b, [(pp3[:, h_ * 128:(h_ + 1) * 128], [(prevP[0][:, h_, :], Y[:, h_, :])]) for h_ in range(4)], [prevP[1], Yb])
                            yield
                            (Pn, Pnb) = iv_r.next()
                            cp("act", Pn, pp[:].rearrange("p (h t) -> p h t", h=4), [ppb], [Pnb])
                            if k < 6:
                                (Ptn, Ptnb) = iv_r.next()
                                cp("act", Ptn, pp2[:].rearrange("p (h t) -> p h t", h=4), [pp2b], [Ptnb])
                            if prevP is not None:
                                (Yn, Ynb) = iv_r.next()
                                tt("dve", Yn, pp3[:].rearrange("p (h t) -> p h t", h=4), Y, ALU.add, [pp3b, Yb], [Ynb])
                                Y, Yb = Yn, Ynb
                            yield
                            prevP = (Pn, Pnb)
                            if k < 6:
                                P, Pb, Pt, Ptb = Pn, Pnb, Ptn, Ptnb
                        (pp3, pp3b) = psf.next()
                        mmh(pp3, pp3b, [(pp3[:, h_ * 128:(h_ + 1) * 128], [(prevP[0][:, h_, :], Y[:, h_, :])]) for h_ in range(4)], [prevP[1], Yb])
                        yield
                        (TTbf, TTb) = hm_r.next()
                        tt("dve", TTbf, pp3[:].rearrange("p (h t) -> p h t", h=4), Y, ALU.add, [pp3b, Yb], [TTb])
                        yield
                        (pu, pub) = psf.next()
                        mmh(pu, pub, [(pu[:, h_ * 128:(h_ + 1) * 128], [(TTbf[:, h_, :], bv[:, h_, :])]) for h_ in range(4)], [TTb, bvb])
                        (pw, pwb) = psf.next()
                        mmh(pw, pwb, [(pw[:, h_ * 128:(h_ + 1) * 128], [(kbe[:, h_, :], TTbf[:, h_, :])]) for h_ in range(4)], [TTb, kbeb])
                        yield
                        (u, ub) = lf_r.next()
                        cp("act", u, pu[:].rearrange("p (h t) -> p h t", h=4), [pub], [ub])
                        (wT, wTb) = hm_r.next()
                        cp("act", wT, pw[:].rearrange("p (h t) -> p h t", h=4), [pwb], [wTb])
                        yield
                        (pv, pvb) = psf.next()
                        mmh(pv, pvb, [(pv[:, h_ * 128:(h_ + 1) * 128], [(wT[:, h_, :], Sbf[:, h_, :])]) for h_ in range(4)], [wTb, Sbfb])
                        yield
                        (vn, vnb) = hm_r.next()
                        tt("dve", vn, u, pv[:].rearrange("p (h t) -> p h t", h=4), ALU.subtract, [ub, pvb], [vnb])
                        yield
                        (po, pob) = psf.next()
                        mmh(po, pob, [(po[:, h_ * 128:(h_ + 1) * 128], [(Sbf[:, h_, :], qdT[:, h_, :]), (vn[:, h_, :], qkT[:, h_, :])]) for h_ in range(4)], [Sbfb, qdTb, vnb, qkTb])
                        (ps_, psb) = psf.next()
                        mmh(ps_, psb, [(ps_[:, h_ * 128:(h_ + 1) * 128], [(kdt[:, h_, :], vn[:, h_, :])]) for h_ in range(4)], [kdtb, vnb])
                        yield
                        (ost, ostb) = os_r.next()
                        cp("act", ost, po[:].rearrange("p (h t) -> p h t", h=4), [pob], [ostb])
                        dma(OA[s].ap[d_, blk].rearrange("p (c t) -> p c t", c=4)[:, :, cc * 128:(cc + 1) * 128], ost, [ostb], [OA[s].b((d_, blk))])
                        for h_ in range(4):
                            stt(S32[:, h_, :], S32[:, h_, :], erow[:, h_, li:li + 1], ps_[:, h_ * 128:(h_ + 1) * 128], ALU.mult, ALU.add, [S32b, erowb, psb], [S32b])
                        yield
                        cp("act", Sbf, S32, [S32b], [Sbfb])
                        yield

                for s, L in SL:
                    run_chains([b1_chain(s, L, 0, "w0"), b1_chain(s, L, 1, "w1")])
                    end_phase()

            if on("B2"):
                prefetch("C1", l)

                def b2_chain(s, L, d_):
                    n64 = L // 64
                    nb = L // TA
                    mI = 2 if d_ == 0 else 0
                    abt_, abtb = carve([128, nb, 64])
                    dma(abt_, HAB[s].ap.rearrange("b p x -> p b x"), [HAB[s].b(k) for k in range(nb)], [abtb])
                    ab6 = abt_.rearrange("p b (d q h c) -> p b d q h c", d=2, q=2, h=4)
                    Aall, Aallb = carve([128, 4, n64])
                    Ball, Ballb = carve([128, 4, n64])
                    rr, rrb = carve([128, 4, n64])
                    for h_ in range(4):
                        cp("dve", Aall[:, h_, :].rearrange("p (b c) -> p b c", c=NC64), ab6[:, :, d_, 0, h_, :], [abtb], [Aallb])
                        cp("dve", Ball[:, h_, :].rearrange("p (b c) -> p b c", c=NC64), ab6[:, :, d_, 1, h_, :], [abtb], [Ballb])
                    if d_ == 0:
                        tt("dve", rr[:, :, 0:n64 - 1], Aall[:, :, 0:n64 - 1], Ball[:, :, 1:n64], ALU.mult, [Aallb, Ballb], [rrb])
                    else:
                        tt("dve", rr[:, :, 1:n64], Aall[:, :, 1:n64], Ball[:, :, 0:n64 - 1], ALU.mult, [Aallb, Ballb], [rrb])
                    S32, S32b = carve([128, 4, 128])
                    Sbf, Sbfb = carve([128, 4, 128], BF16)
                    mset("pool", S32, 0.0, [S32b])
                    mset("pool", Sbf, 0.0, [Sbfb])
                    qd_r = ring(2, [128, 4, 64], BF16)
                    kd_r = ring(2, [128, 4, 64], BF16)
                    os_r = ring(2, [128, 4, 64], BF16)
                    tok_r = ring(3, [64, 1536], BF16)
                    am_r = ring(2, [64, 4, 64], BF16)
                    tmp_r = ring(2, [128, 4, 128])
                    order = list(range(n64)) if d_ == 0 else list(range(n64 - 1, -1, -1))

                    def issue(c):
                        blk, cc = c // NC64, c % NC64
                        (qd, qdb) = qd_r.next()
                        dma(qd, HQd[s].ap[d_, blk].rearrange("p (c t) -> p c t", c=4)[:, :, cc * 64:(cc + 1) * 64], [HQd[s].b((d_, blk))], [qdb])
                        (kd, kdb) = kd_r.next()
                        dma(kd, HKd[s].ap[d_, blk].rearrange("p (c t) -> p c t", c=4)[:, :, cc * 64:(cc + 1) * 64], [HKd[s].b((d_, blk))], [kdb])
                        (tk, tkb) = tok_r.next()
                        dma(tk, HT[s].ap[c * 64:(c + 1) * 64, :], [HT[s].b(c // 2)], [tkb])
                        return qd, qdb, kd, kdb, tk, tkb

                    nxt = issue(order[0])
                    for ci_, c in enumerate(order):
                        (qd, qdb, kd, kdb, tk, tkb) = nxt
                        if ci_ + 1 < n64:
                            nxt = issue(order[ci_ + 1])
                        blk, cc = c // NC64, c % NC64
                        if d_ == 0:
                            wtick(1)
                        (pa, pab) = psf.next()
                        mmh(pa, pab, [(pa[0:64, h_ * 64:(h_ + 1) * 64], [(kd[:, h_, :], qd[:, h_, :])]) for h_ in range(4)], [kdb, qdb])
                        if ci_ < n64 - 1:
                            (pp, ppb) = psf.next()
                            mmh(pp, ppb, [(pp[:, h_ * 128:(h_ + 1) * 128], [(tk[:, d_ * 512 + h_ * 128:d_ * 512 + (h_ + 1) * 128], tk[:, 1024 + h_ * 128:1024 + (h_ + 1) * 128])]) for h_ in range(4)], [tkb])
                        yield
                        (am, amb) = am_r.next()
                        tt("dve", am, pa[0:64, 0:256].rearrange("p (h t) -> p h t", h=4), bch(mask_f[0:64, mI, 0:64], 4), ALU.mult, [pab, mask_fb], [amb])
                        yield
                        (po, pob) = psf.next()
                        mmh(po, pob, [(po[:, h_ * 64:(h_ + 1) * 64], [(Sbf[:, h_, :], qd[:, h_, :]), (tk[:, 1024 + h_ * 128:1024 + (h_ + 1) * 128], am[:, h_, :])]) for h_ in range(4)], [Sbfb, qdb, tkb, amb])
                        yield
                        (ost, ostb) = os_r.next()
                        cp("act", ost, po[:, 0:256].rearrange("p (h t) -> p h t", h=4), [pob], [ostb])
                        dma(OB[s].ap[d_, blk].rearrange("p (c t) -> p c t", c=4)[:, :, cc * 64:(cc + 1) * 64], ost, [ostb], [OB[s].b((d_, blk))])
                        if ci_ < n64 - 1:
                            (tm, tmb) = tmp_r.next()
                            tt("dve", tm, pp[:].rearrange("p (h t) -> p h t", h=4), S32, ALU.add, [ppb, S32b], [tmb])
                            yield
                            tt("dve", S32, tm, bc(rr[:, :, c], 128), ALU.mult, [tmb, rrb], [S32b])
                            tt("dve", Sbf, tm, bc(rr[:, :, c], 128), ALU.mult, [tmb, rrb], [Sbfb])
                        yield

                for s, L in SL:
                    run_chains([b2_chain(s, L, 0), b2_chain(s, L, 1)])
                    end_phase()

            if on("C1"):
                (war, wb) = acquire("C1", l)
                wbg_ = war[:, 0:4096].rearrange("p (k c) -> p k c", k=4)
                wbh_ = war[:, 4096:8192].rearrange("p (k c) -> p k c", k=4)
                wo_ = war[:, 8192:16384].rearrange("p (k c) -> p k c", k=8)
                in_r = ring(12, [128, 4, TA], BF16, where=('w1' if war is warena[0][0] else 'w0'))
                mg_r = ring(4, [128, 8, TA], BF16)
                x_r = ring(2, [128, 8, TA])
                o_r = ring(2, [128, 4, TA])
                rs4_r = ring(2, [128, 4, TA])
                sq4_r = ring(2, [128, 4, TA], BF16)
                onz_r = ring(2, [128, 4, TA], BF16)
                mrg_r = ring(1, [128, 8, TA], BF16)
                f_r = ring(6, [128, TA])
                mo_r = ring(1, [128, 8, TA])
                sq8_r = ring(1, [128, 8, TA], BF16)
                res_r = ring(1, [128, 8, TA])
                rstd_r = ring(2, [128, TA])
                tmp_r = ring(2, [128, TA])
                def c1_load(it):
                    s, L, blk = it
                    ins_ = []
                    for (OX, GX) in ((OA, ZT), (OB, GHT)):
                        for (dt_, key) in ((OX[s].ap[0, blk], OX[s].b((0, blk))), (OX[s].ap[1, blk], OX[s].b((1, blk))), (GX[s].ap[blk], GX[s].b(blk))):
                            (t_, tb_) = in_r.next()
                            dma(t_, dt_.rearrange("p (c t) -> p c t", c=4), [key], [tb_])
                            ins_.append((t_, tb_))
                    mgs = []
                    for half in range(2):
                        (mg, mgb) = mg_r.next()
                        dma(mg, MG[s].ap[blk, :, half * 8 * TA:(half + 1) * 8 * TA].rearrange("p (c t) -> p c t", c=8), [MG[s].b((blk, half))], [mgb])
                        mgs.append((mg, mgb))
                    (xt, xb) = x_r.next()
                    dma(xt, XT[s][par].ap[:, :, blk * TA:(blk + 1) * TA].rearrange("c p t -> p c t"), [XT[s][par].b(blk)], [xb])
                    return ins_, mgs, xt, xb

                def c1_compute(it, cur):
                    s, L, blk = it
                    (ins_, mgs, xt, xb) = cur
                    onz = [None, None]

                    def c1_branch(bi, nw_, nwb_):
                        (of, ofb), (ob_, obb), (gz, gzb) = ins_[bi * 3:bi * 3 + 3]
                        (o, ob2) = o_r.next()
                        tt("dve", o, of, ob_, ALU.add, [ofb, obb], [ob2])
                        yield
                        (sq4, sq4b) = sq4_r.next()
                        act(sq4, o, AF.Square, [ob2], [sq4b])
                        yield
                        (rs4, rs4b) = rs4_r.next()
                        pts = []
                        for hp in range(2):
                            (pt, pb) = psf.next()
                            mmh(pt, pb, [(pt[:, q * TA:(q + 1) * TA], [(ones_h[:], sq4[:, hp * 2 + q, :])]) for q in range(2)], [ones_hb, sq4b])
                            pts.append((pt, pb))
                        yield
                        for hp in range(2):
                            (pt, pb) = pts[hp]
                            act(rs4[:, hp * 2:hp * 2 + 2, :], pt[:].rearrange("p (h t) -> p h t", h=2), AF.Ln, [pb, epsb], [rs4b], bias=eps_t[:, 0:1], scale=1.0 / 128)
                        yield
                        act(rs4, rs4, AF.Exp, [rs4b], [rs4b], scale=-0.5)
                        yield
                        stt(o, o, nw_[:, 0:1], rs4, ALU.mult, ALU.mult, [ob2, nwb_, rs4b], [ob2])
                        yield
                        (oz, ozb) = onz_r.next()
                        tt("dve", oz, o, gz, ALU.mult, [ob2, gzb], [ozb])
                        onz[bi] = (oz, ozb)

                    run_chains([c1_branch(0, nwa, nwab), c1_branch(1, nwh, nwhb)])
                    (mrg, mrgb) = mrg_r.next()
                    for oc in range(8):
                        (pa, pab) = psf.next()
                        mm(pa[:, 0:TA], pab, [(wbg_[:, kc, oc * 128:(oc + 1) * 128], onz[0][0][:, kc, :]) for kc in range(4)], [wb, onz[0][1]])
                        (pb_, pbb) = psf.next()
                        mm(pb_[:, 0:TA], pbb, [(wbh_[:, kc, oc * 128:(oc + 1) * 128], onz[1][0][:, kc, :]) for kc in range(4)], [wb, onz[1][1]])
                        (t1, t1b) = f_r.next()
                        tt("dve", t1, pa[:, 0:TA], mgs[0][0][:, oc, :], ALU.mult, [pab, mgs[0][1]], [t1b])
                        (t2, t2b) = f_r.next()
                        tt("dve", t2, pb_[:, 0:TA], mgs[1][0][:, oc, :], ALU.mult, [pbb, mgs[1][1]], [t2b])
                        tt("dve", mrg[:, oc, :], t1, t2, ALU.add, [t1b, t2b], [mrgb])
                    (mo, mob) = mo_r.next()
                    (sq8, sq8b) = sq8_r.next()
                    for oc in range(8):
                        (pt, pb) = psf.next()
                        mm(pt[:, 0:TA], pb, [(wo_[:, kc, oc * 128:(oc + 1) * 128], mrg[:, kc, :]) for kc in range(8)], [wb, mrgb])
                        cp("act", mo[:, oc, :], pt[:, 0:TA], [pb], [mob])
                        act(sq8[:, oc, :], pt[:, 0:TA], AF.Square, [pb], [sq8b])
                    res, resb = post_norm_add(res_r, xt, xb, mo, mob, sq8, sq8b, g_po, g_pob, rstd_r, tmp_r, f_r)
                    dma(XM[s].ap[:, :, blk * TA:(blk + 1) * TA].rearrange("c p t -> p c t"), res, [resb], [XM[s].b(blk)])

                pipeline([(s, L, blk) for s, L in SL for blk in range(L // TA)], c1_load, c1_compute)
                end_phase()

            for half in range(2):
                if not on("C2a"):
                    continue
                (war, wb) = acquire(f"C2a{half}", l)
                NF = 11 * 128
                wg_ = war[:, 0:8 * NF].rearrange("p (k c) -> p k c", k=8)
                wu_ = war[:, 8 * NF:16 * NF].rearrange("p (k c) -> p k c", k=8)
                prefetch("C2a1" if half == 0 else "C2b", l)
                xT_r = ring(2, [128, 8, TA])
                sq_r = ring(1, [128, 8, TA], BF16)
                hT_r = ring(2, [128, 8, TA], BF16)
                rstd_r = ring(2, [128, TA])
                tmp_r = ring(2, [128, TA])
                sg_r = ring(4, [128, TA])
                a_r = ring(2, [128, 11, TA], BF16)

                def c2a_load(it, half=half):
                    s, L, blk = it
                    if half == 0:
                        return x_load(XM[s], L, blk, False, xT_r)
                    (ht, hb) = hT_r.next()
                    dma(ht, H2[s].ap[blk].rearrange("p (c t) -> p c t", c=8), [H2[s].b(blk)], [hb])
                    return ht, hb

                def c2a_compute(it, cur, half=half):
                    s, L, blk = it
                    if half == 0:
                        ht, hb = norm_compute(cur[0], cur[1], g_pf, g_pfb, False, sq_r, hT_r, rstd_r, tmp_r)
                        dma(H2[s].ap[blk].rearrange("p (c t) -> p c t", c=8), ht, [hb], [H2[s].b(blk)])
                    else:
                        (ht, hb) = cur
                    (a_, ab_) = a_r.next()
                    for fc in range(11):
                        (pg, pgb) = psf.next()
                        mm(pg[:, 0:TA], pgb, [(wg_[:, kc, fc * 128:(fc + 1) * 128], ht[:, kc, 0:TA]) for kc in range(8)], [wb, hb])
                        (pu, pub) = psf.next()
                        mm(pu[:, 0:TA], pub, [(wu_[:, kc, fc * 128:(fc + 1) * 128], ht[:, kc, 0:TA]) for kc in range(8)], [wb, hb])
                        (sg, sgb) = sg_r.next()
                        act(sg, pg[:, 0:TA], AF.Silu, [pgb], [sgb])
                        tt("dve", a_[:, fc, :], sg, pu[:, 0:TA], ALU.mult, [sgb, pub], [ab_])
                    dma(ACTS[s].ap[blk, :, half * 11 * TA:(half + 1) * 11 * TA].rearrange("p (c t) -> p c t", c=11), a_, [ab_], [ACTS[s].b((blk, half))])

                pipeline([(s, L, blk) for s, L in SL for blk in range(L // TA)], c2a_load, c2a_compute)
                end_phase()

            if on("C2b"):
                (war, wb) = acquire("C2b", l)
                wd_ = war[:, 0:22 * 1024].rearrange("p (k c) -> p k c", k=22)
                prefetch("A1", l + 1)
                a_r = ring(2, [128, 22, TA], BF16)
                x_r = ring(2, [128, 8, TA])
                ff_r = ring(1, [128, 8, TA])
                sq8_r = ring(1, [128, 8, TA], BF16)
                res_r = ring(2, [128, 8, TA])
                rstd_r = ring(2, [128, TA])
                tmp_r = ring(2, [128, TA])
                f_r = ring(4, [128, TA])
                yt_r = ring(2, [128, D])
                last = (l == depth - 1)

                def c2b_load(it):
                    s, L, blk = it
                    (a_, ab_) = a_r.next()
                    dma(a_, ACTS[s].ap[blk].rearrange("p (c t) -> p c t", c=22), [ACTS[s].b((blk, 0)), ACTS[s].b((blk, 1))], [ab_])
                    (xt, xb) = x_r.next()
                    dma(xt, XM[s].ap[:, :, blk * TA:(blk + 1) * TA].rearrange("c p t -> p c t"), [XM[s].b(blk)], [xb])
                    return a_, ab_, xt, xb

                def c2b_compute(it, cur):
                    s, L, blk = it
                    (a_, ab_, xt, xb) = cur
                    (ffo, ffob) = ff_r.next()
                    (sq8, sq8b) = sq8_r.next()
                    for oc in range(8):
                        (pt, pb) = psf.next()
                        mm(pt[:, 0:TA], pb, [(wd_[:, fc, oc * 128:(oc + 1) * 128], a_[:, fc, :]) for fc in range(22)], [wb, ab_])
                        cp("act", ffo[:, oc, :], pt[:, 0:TA], [pb], [ffob])
                        act(sq8[:, oc, :], pt[:, 0:TA], AF.Square, [pb], [sq8b])
                    res, resb = post_norm_add(res_r, xt, xb, ffo, ffob, sq8, sq8b, g_qf, g_qfb, rstd_r, tmp_r, f_r)
                    if not last:
                        dma(XT[s][1 - par].ap[:, :, blk * TA:(blk + 1) * TA].rearrange("c p t -> p c t"), res, [resb], [XT[s][1 - par].b(blk)])
                    else:
                        for j in range(NCH):
                            (yt, ytb) = yt_r.next()
                            for hf in range(2):
                                (pt, pb) = psf.next()
                                mmh(pt, pb, [(pt[:, c * 128:(c + 1) * 128], [(res[:, hf * 4 + c, j * 128:(j + 1) * 128], ident_f[:])]) for c in range(4)], [resb, ident_fb])
                                cp("act" if hf == 0 else "dve", yt[:, hf * 512:(hf + 1) * 512], pt[:], [pb], [ytb])
                            r0 = blk * TA + j * 128
                            dma(y_out[s].ap[r0:r0 + 128, :], yt, [ytb], [y_out[s].b(r0 // 128)])

                pipeline([(s, L, blk) for s, L in SL for blk in range(L // TA)], c2b_load, c2b_compute)
                end_phase()
        S.emit(es)
    return nc


def host_consts():
    ident = np.eye(128, dtype=np.float32)
    p = np.arange(128)[:, None]
    f = np.arange(128)[None, :]
    m = np.zeros((128, 8, 128), np.float32)
    m[:, 0] = (f <= p)
    m[:, 1] = (f < p)
    m[:, 2] = (f >= p)
    m[:, 3] = (f > p)
    m[:, 4:8] = (m[:, 0:4] - 1.0) * 30000.0
    sel = np.zeros((8, 2), np.float32)
    sel[0:4, 0] = 1.0
    sel[4:8, 1] = 1.0
    return {"c_ident": ident, "c_mask": m, "c_sel": sel}


W_KEYS = ["w_in", "conv_w", "gdn_norm_w", "hgrn_norm_w", "w_branch_gdn", "w_branch_hgrn", "w_out", "norm_pre_mix",
          "norm_post_mix", "norm_pre_ffn", "norm_post_ffn", "w_ffn_gate", "w_ffn_up", "w_ffn_down", "hgrn_lb_logits"]


def make_in_map(inputs, xs, depth):
    m = {f"x{s}": np.ascontiguousarray(x, dtype=np.float32) for s, x in enumerate(xs)}
    for k in W_KEYS:
        m[k] = np.ascontiguousarray(np.asarray(inputs[k], dtype=np.float32)[:depth])
    m["gdn_a_log"] = np.ascontiguousarray(np.asarray(inputs["gdn_a_log"], dtype=np.float32)[:depth].reshape(depth, 8))
    m["gdn_dt_bias"] = np.ascontiguousarray(np.asarray(inputs["gdn_dt_bias"], dtype=np.float32)[:depth].reshape(depth, 8))
    m.update(host_consts())
    return m


_NC_CACHE = {}


def kernel(**inputs):
    xp = np.asarray(inputs["x_prompt"], dtype=np.float32)
    xs = np.asarray(inputs["x_sample"], dtype=np.float32)
    depth = int(np.asarray(inputs["w_in"]).shape[0])
    seq_lens = (xp.shape[1], xs.shape[1])
    key = (seq_lens, depth)
    if key not in _NC_CACHE:
        _NC_CACHE[key] = build(list(seq_lens), depth)
    nc = _NC_CACHE[key]
    in_maps = [make_in_map(inputs, [xp[c], xs[c]], depth) for c in range(NCORES)]
    res = run_bass_kernel_spmd(nc, in_maps, core_ids=list(range(NCORES)))
    yp = np.stack([np.asarray(res.results[c]["y0"], dtype=np.float32) for c in range(NCORES)], axis=0)
    ys = np.stack([np.asarray(res.results[c]["y1"], dtype=np.float32) for c in range(NCORES)], axis=0)
    return (yp, ys)
```
